# Optimizing a Trainium2 kernel written in Bass

```python
import math
import jax, jax.numpy as jnp
from jax import lax
import numpy as np

D_MODEL = 1024
BATCH = 32
SEQ = 2048
DEPTH = 2

HEAD_DIM = 64
A_Q_HEADS = 8
A_KV_HEADS = 2
A_GROUP = A_Q_HEADS // A_KV_HEADS
WINDOW = 128
BLOCK = 128
B_HEADS = 8
A_WIDTH = A_Q_HEADS * HEAD_DIM
A_KV_WIDTH = A_KV_HEADS * HEAD_DIM
B_WIDTH = B_HEADS * HEAD_DIM
ATTN_WIDTH = A_WIDTH + B_WIDTH
ATTN_SPLITS = (A_WIDTH, A_KV_WIDTH, A_KV_WIDTH, B_WIDTH, B_WIDTH, B_WIDTH, B_HEADS, ATTN_WIDTH)
ATTN_IN = sum(ATTN_SPLITS)
REL_BUCKETS = 32
REL_MAX_EXACT = 16
REL_MAX_DIST = 128
LRU_WIDTH = D_MODEL
LRU_BLOCKS = 8
LRU_BLOCK_W = LRU_WIDTH // LRU_BLOCKS
CONV_WIDTH = 4
LRU_C = 8.0
N_ATTN_LAYERS = (DEPTH + 1) // 2
N_LRU_LAYERS = DEPTH // 2
EPS = 1e-6

kernel_name = "hybrid_swa_fox_rglru_adaln"


def rmsnorm(x, g):
    x32 = x.astype(jnp.float32)
    y = x32 * lax.rsqrt(jnp.mean(x32 * x32, axis=-1, keepdims=True) + EPS)
    return (y * g.astype(jnp.float32)).astype(x.dtype)


def t5_causal_bucket(rel):
    n = jnp.maximum(rel, 0)
    nf = jnp.maximum(n, 1).astype(jnp.float32)
    large = REL_MAX_EXACT + (jnp.log(nf / REL_MAX_EXACT) / math.log(REL_MAX_DIST / REL_MAX_EXACT)
                             * (REL_BUCKETS - REL_MAX_EXACT)).astype(jnp.int32)
    large = jnp.minimum(large, REL_BUCKETS - 1)
    return jnp.where(n < REL_MAX_EXACT, n, large)


def swa_sink_attention(q, k, v, sinks, rel_bias):
    B, S = q.shape[0], q.shape[1]
    nb = S // BLOCK
    qb = q.reshape(B, nb, BLOCK, A_KV_HEADS, A_GROUP, HEAD_DIM)
    pad = ((0, 0), (BLOCK, 0), (0, 0), (0, 0))
    kp = jnp.pad(k, pad)[:, :S].reshape(B, nb, BLOCK, A_KV_HEADS, HEAD_DIM)
    vp = jnp.pad(v, pad)[:, :S].reshape(B, nb, BLOCK, A_KV_HEADS, HEAD_DIM)
    kb = jnp.concatenate([kp, k.reshape(B, nb, BLOCK, A_KV_HEADS, HEAD_DIM)], axis=2)
    vb = jnp.concatenate([vp, v.reshape(B, nb, BLOCK, A_KV_HEADS, HEAD_DIM)], axis=2)
    scores = jnp.einsum('bnqhgd,bnkhd->bhgnqk', qb, kb).astype(jnp.float32) * (HEAD_DIM ** -0.5)
    qi = jnp.arange(BLOCK)[:, None]
    kj = jnp.arange(2 * BLOCK)[None, :]
    rel = qi - kj + BLOCK
    bias = rel_bias.astype(jnp.float32)[t5_causal_bucket(rel)]
    bias = jnp.transpose(bias, (2, 0, 1)).reshape(A_KV_HEADS, A_GROUP, 1, BLOCK, 2 * BLOCK)
    valid = (rel >= 0) & (rel < WINDOW)
    first = (jnp.arange(nb)[:, None, None] == 0) & (kj[None] < BLOCK)
    mask = valid[None] & ~first
    logits = jnp.where(mask, scores + bias, -jnp.inf)
    sink = jnp.broadcast_to(sinks.astype(jnp.float32).reshape(1, A_KV_HEADS, A_GROUP, 1, 1, 1),
                            logits.shape[:-1] + (1,))
    probs = jax.nn.softmax(jnp.concatenate([logits, sink], axis=-1), axis=-1)[..., :-1]
    out = jnp.einsum('bhgnqk,bnkhd->bnqhgd', probs.astype(v.dtype), vb)
    return out.reshape(B, S, A_WIDTH)


def forgetting_attention(q, k, v, log_f):
    B, S = q.shape[0], q.shape[1]
    nb = S // BLOCK
    F = jnp.cumsum(log_f, axis=1)
    outs = []
    for n in range(nb):
        q0, kend = n * BLOCK, (n + 1) * BLOCK
        qs = q[:, q0:kend]
        ks, vs = k[:, :kend], v[:, :kend]
        s = jnp.einsum('bqhd,bkhd->bhqk', qs, ks).astype(jnp.float32) * (HEAD_DIM ** -0.5)
        decay = jnp.transpose(F[:, q0:kend], (0, 2, 1))[..., :, None] - jnp.transpose(F[:, :kend], (0, 2, 1))[..., None, :]
        tpos = q0 + jnp.arange(BLOCK)[:, None]
        spos = jnp.arange(kend)[None, :]
        p = jax.nn.softmax(jnp.where(spos <= tpos, s + decay, -jnp.inf), axis=-1)
        outs.append(jnp.einsum('bhqk,bkhd->bqhd', p.astype(v.dtype), vs))
    return jnp.concatenate(outs, axis=1).reshape(B, S, B_WIDTH)


def attention_mixer(h, w_in, sinks, b_f, w_out, rel_bias):
    B, S, _ = h.shape
    proj = h @ w_in
    idx = np.cumsum(ATTN_SPLITS)[:-1].tolist()
    a_q, a_k, a_v, b_q, b_k, b_v, f_logit, gate = jnp.split(proj, idx, axis=-1)
    a_out = swa_sink_attention(a_q.reshape(B, S, A_Q_HEADS, HEAD_DIM),
                               a_k.reshape(B, S, A_KV_HEADS, HEAD_DIM),
                               a_v.reshape(B, S, A_KV_HEADS, HEAD_DIM), sinks, rel_bias)
    log_f = jax.nn.log_sigmoid((f_logit + b_f).astype(jnp.float32))
    b_out = forgetting_attention(b_q.reshape(B, S, B_HEADS, HEAD_DIM),
                                 b_k.reshape(B, S, B_HEADS, HEAD_DIM),
                                 b_v.reshape(B, S, B_HEADS, HEAD_DIM), log_f)
    y = jnp.concatenate([a_out, b_out], axis=-1) * jax.nn.silu(gate)
    return y @ w_out


def rglru_mixer(h, w_in, conv_w, conv_b, w_a, b_a, w_x, b_x, lam, w_out):
    B, S, _ = h.shape
    proj = h @ w_in
    xr, gate = proj[..., :LRU_WIDTH], proj[..., LRU_WIDTH:]
    xp = jnp.pad(xr, ((0, 0), (CONV_WIDTH - 1, 0), (0, 0)))
    xc = conv_b
    for j in range(CONV_WIDTH):
        xc = xc + xp[:, j:j + S] * conv_w[j]
    xblk = xc.reshape(B, S, LRU_BLOCKS, LRU_BLOCK_W)
    r = jax.nn.sigmoid((jnp.einsum('bsnw,nwv->bsnv', xblk, w_a).reshape(B, S, LRU_WIDTH) + b_a).astype(jnp.float32))
    i = jax.nn.sigmoid((jnp.einsum('bsnw,nwv->bsnv', xblk, w_x).reshape(B, S, LRU_WIDTH) + b_x).astype(jnp.float32))
    log_a = -LRU_C * r * jax.nn.softplus(-lam.astype(jnp.float32))
    a = jnp.exp(log_a)
    u = jnp.sqrt(-jnp.expm1(2.0 * log_a)) * (i * xc.astype(jnp.float32))

    def step(state, inp):
        a_t, u_t = inp
        state = a_t * state + u_t
        return state, state

    _, hs = lax.scan(step, jnp.zeros((B, LRU_WIDTH), jnp.float32),
                     (jnp.transpose(a, (1, 0, 2)), jnp.transpose(u, (1, 0, 2))))
    y = jnp.transpose(hs, (1, 0, 2)).astype(h.dtype) * jax.nn.silu(gate)
    return y @ w_out


def setup_inputs(seed: int = 0) -> dict:
    key = jax.random.key(seed)
    ks = jax.random.split(key, 22)
    nrm = lambda k, shape, s: jax.random.normal(k, shape, jnp.float32) * s
    a0 = jax.random.uniform(ks[19], (N_LRU_LAYERS, LRU_WIDTH), jnp.float32, 0.9, 0.999) ** (1.0 / LRU_C)
    return {
        "x": nrm(ks[0], (BATCH, SEQ, D_MODEL), 1.0),
        "c": nrm(ks[1], (BATCH, D_MODEL), 1.0),
        "rel_bias": nrm(ks[2], (REL_BUCKETS, A_Q_HEADS), 0.2),
        "norm_g": 1.0 + nrm(ks[3], (DEPTH, D_MODEL), 0.05),
        "ada_w": nrm(ks[4], (DEPTH, D_MODEL, 3 * D_MODEL), 0.3 * D_MODEL ** -0.5),
        "ada_b": nrm(ks[5], (DEPTH, 3 * D_MODEL), 0.02),
        "attn_w_in": nrm(ks[6], (N_ATTN_LAYERS, D_MODEL, ATTN_IN), D_MODEL ** -0.5),
        "attn_sinks": nrm(ks[7], (N_ATTN_LAYERS, A_Q_HEADS), 0.5),
        "attn_b_f": jax.random.uniform(ks[8], (N_ATTN_LAYERS, B_HEADS), jnp.float32, 1.0, 4.0),
        "attn_w_out": nrm(ks[9], (N_ATTN_LAYERS, ATTN_WIDTH, D_MODEL), ATTN_WIDTH ** -0.5),
        "lru_w_in": nrm(ks[10], (N_LRU_LAYERS, D_MODEL, 2 * LRU_WIDTH), D_MODEL ** -0.5),
        "lru_conv_w": nrm(ks[11], (N_LRU_LAYERS, CONV_WIDTH, LRU_WIDTH), CONV_WIDTH ** -0.5),
        "lru_conv_b": nrm(ks[12], (N_LRU_LAYERS, LRU_WIDTH), 0.02),
        "lru_w_a": nrm(ks[13], (N_LRU_LAYERS, LRU_BLOCKS, LRU_BLOCK_W, LRU_BLOCK_W), LRU_BLOCK_W ** -0.5),
        "lru_b_a": nrm(ks[14], (N_LRU_LAYERS, LRU_WIDTH), 0.02),
        "lru_w_x": nrm(ks[15], (N_LRU_LAYERS, LRU_BLOCKS, LRU_BLOCK_W, LRU_BLOCK_W), LRU_BLOCK_W ** -0.5),
        "lru_b_x": nrm(ks[16], (N_LRU_LAYERS, LRU_WIDTH), 0.02),
        "lru_lambda": jnp.log(a0) - jnp.log1p(-a0),
        "lru_w_out": nrm(ks[17], (N_LRU_LAYERS, LRU_WIDTH, D_MODEL), LRU_WIDTH ** -0.5),
        "final_g": 1.0 + nrm(ks[18], (D_MODEL,), 0.05),
    }


def reference(x, c, rel_bias, norm_g, ada_w, ada_b, attn_w_in, attn_sinks, attn_b_f, attn_w_out,
              lru_w_in, lru_conv_w, lru_conv_b, lru_w_a, lru_b_a, lru_w_x, lru_b_x, lru_lambda,
              lru_w_out, final_g):
    c_act = jax.nn.silu(c)
    for layer in range(DEPTH):
        mod = c_act @ ada_w[layer] + ada_b[layer]
        shift, scale, gate = jnp.split(mod, 3, axis=-1)
        h = rmsnorm(x, norm_g[layer]) * (1.0 + scale[:, None, :]) + shift[:, None, :]
        if layer % 2 == 0:
            j = layer // 2
            y = attention_mixer(h, attn_w_in[j], attn_sinks[j], attn_b_f[j], attn_w_out[j], rel_bias)
        else:
            j = layer // 2
            y = rglru_mixer(h, lru_w_in[j], lru_conv_w[j], lru_conv_b[j], lru_w_a[j], lru_b_a[j],
                            lru_w_x[j], lru_b_x[j], lru_lambda[j], lru_w_out[j])
        x = x + gate[:, None, :] * y
    return rmsnorm(x, final_g)
```

```python
import math
from contextlib import ExitStack

import numpy as np
import concourse.bass as bass
import concourse.mybir as mybir
from concourse.bass_utils import run_bass_kernel_spmd

F32 = mybir.dt.float32
BF16 = mybir.dt.bfloat16
AF = mybir.ActivationFunctionType
ALU = mybir.AluOpType

NCORES = 8
P = 128
D = 1024
KC = 8
S = 2048
G = 256
GT = G // P
NB = S // P
NGRP = S // G
SEQ_PER_CORE = 4
EPS = 1e-6
VW = 66

AQ, AK, BQ, BK, FL, AV, BV, GA, C0 = 0, 512, 640, 1152, 1664, 1672, 1800, 2312, 3336
CP_NG, CP_AB, CP_CW, CP_CB, CP_BA, CP_BX, CP_LAM, CP_SINK, CP_CT, CP_N = 0, 16, 64, 96, 104, 112, 120, 128, 136, 168

SAME_ENGINE_SYNC = True
RSTD_POW = True
EARLY_L1_NORM = False


class _Rec:
    def __init__(self):
        self.call = None

    def __getattr__(self, name):
        def f(*a, **k):
            self.call = (name, a, k)
            return self
        return f


class _XS:
    def __init__(self, bufs):
        self.bufs = bufs
        self.cur = 0

    def __getitem__(self, idx):
        return self.bufs[self.cur][idx]


class Sched:
    STREAMS = ("pe", "act", "dve", "pool", "sp")

    def __init__(self, nc, es):
        self.nc = nc
        self.es = es
        self.sems = {}
        self.cnt = {}
        self.ops = {s: [] for s in self.STREAMS}
        self.lastw = {}
        self.rds = {}
        self.excl = set()
        for s in ("pe", "act", "dve", "pool"):
            self.new_sem("E_" + s)

    def new_sem(self, name):
        self.sems[name] = self.es.enter_context(self.nc.semaphore(name))
        self.cnt[name] = 0
        return name

    def add(self, stream, fn, r=(), w=(), dsem=None, extra=()):
        deps = set(extra)
        own = dsem if dsem is not None else "E_" + stream
        for k in r:
            if k in self.lastw:
                deps.add(self.lastw[k])
            if k in self.excl:
                deps.update(d for d in self.rds.get(k, ()) if d[0] != own)
        for k in w:
            if k in self.lastw:
                deps.add(self.lastw[k])
            deps.update(self.rds.get(k, ()))
        if dsem is None:
            sname = "E_" + stream
            self.cnt[sname] += 1
            inc = 1
        else:
            sname = dsem
            self.cnt[sname] += 16
            inc = 16
        done = (sname, self.cnt[sname])
        for k in r:
            self.rds.setdefault(k, []).append(done)
        for k in w:
            self.lastw[k] = done
            self.rds[k] = []
        rec = _Rec()
        fn(rec)
        assert rec.call is not None
        self.ops[stream].append((deps, rec.call, sname, inc))
        return done

    def emit(self, stream, eng):
        known = {}
        own = "E_" + stream
        for deps, fn, sname, inc in self.ops[stream]:
            best = {}
            for (s, v) in deps:
                if v > best.get(s, 0):
                    best[s] = v
            for s, v in best.items():
                if s == own and (stream == "pe" or not SAME_ENGINE_SYNC):
                    continue
                if known.get(s, 0) >= v:
                    continue
                eng.wait_ge(self.sems[s], v)
                known[s] = v
            name, a, k = fn
            ins = getattr(eng, name)(*a, **k)
            ins.then_inc(self.sems[sname], inc)

    def run(self, final_conds=()):
        nc = self.nc
        with nc.Block() as block:
            @block.tensor
            def _(e):
                self.emit("pe", e)

            @block.scalar
            def _(e):
                self.emit("act", e)

            @block.vector
            def _(e):
                self.emit("dve", e)

            @block.gpsimd
            def _(e):
                self.emit("pool", e)

            @block.sync
            def _(e):
                self.emit("sp", e)
                best = {}
                for (s, v) in final_conds:
                    best[s] = max(best.get(s, 0), v)
                for s, v in best.items():
                    e.wait_ge(self.sems[s], v)


def build(nseq=SEQ_PER_CORE, ngrp=NGRP, dbg=False, stop=None):
    nc = bass.Bass("TRN2", target_bir_lowering=False, dynamic_dma_scratch_size=2048)
    x_d = nc.dram_tensor("x", [nseq, S, D], F32, kind="ExternalInput").ap()
    out_d = nc.dram_tensor("out", [nseq, S, D], F32, kind="ExternalOutput").ap()
    w0_d = nc.dram_tensor("w0", [P, KC, C0], F32, kind="ExternalInput").ap()
    w1_d = nc.dram_tensor("w1", [P, KC, D], F32, kind="ExternalInput").ap()
    w2_d = nc.dram_tensor("w2", [P, KC, 2 * D], F32, kind="ExternalInput").ap()
    w3_d = nc.dram_tensor("w3", [P, KC, D], F32, kind="ExternalInput").ap()
    w4_d = nc.dram_tensor("w4", [P, 2, KC, P], F32, kind="ExternalInput").ap()
    ada_d = nc.dram_tensor("ada_w", [2, D, 3 * D], F32, kind="ExternalInput").ap()
    colp_d = nc.dram_tensor("colp", [P, CP_N], F32, kind="ExternalInput").ap()
    cmat_d = nc.dram_tensor("cmat", [P, 3 * P], F32, kind="ExternalInput").ap()
    rb_d = nc.dram_tensor("rb", [32, 8 + 383], F32, kind="ExternalInput").ap()
    v8_d = nc.dram_tensor("v8", [8, 384], F32, kind="ExternalInput").ap()
    fg_d = nc.dram_tensor("final_g", [1, D], F32, kind="ExternalInput")
    gscr = nc.dram_tensor("gscr", [4, 2, D], F32, kind="Internal")
    ebscr = nc.dram_tensor("ebscr", [8, 384], F32, kind="Internal")
    if dbg:
        dbg_d = nc.dram_tensor("dbg", [nseq, S, D], F32, kind="ExternalOutput").ap()

    with ExitStack() as es:
        SC = Sched(nc, es)
        add = SC.add

        def sb(name, shape, dt):
            return es.enter_context(nc.sbuf_tensor("s_" + name, shape, dt))

        W0 = sb("W0", [P, KC, C0], BF16)
        W1 = sb("W1", [P, KC, D], BF16)
        W2 = sb("W2", [P, KC, 2 * D], BF16)
        W3 = sb("W3", [P, KC, D], BF16)
        W4 = sb("W4", [P, 2, KC, P], BF16)
        bKT = sb("bKT", [P, 4, S], BF16)
        bV = sb("bV", [P, NB, 8, VW], BF16)
        aKT = sb("aKT", [P, P + G], BF16)
        aV = sb("aV", [P, GT + 1, 2, VW], BF16)
        xs = _XS([sb("xs0", [P, GT, D], F32), sb("xs1", [P, GT, D], F32)])
        xn = sb("xn", [P, D], F32)
        bc = sb("bc", [P, D], F32)
        hT = sb("hT", [P, KC, G], BF16)
        yT = sb("yT", [P, KC, G], BF16)
        aQT = sb("aQT", [P, 4, G], BF16)
        bQT = sb("bQT", [P, 4, G], BF16)
        th = sb("th", [P, D], F32)
        osb = sb("osb", [P, 512], F32)
        ysb = sb("ysb", [P, D], BF16)
        NPB = 8
        pb = sb("pb", [P, NPB, P], BF16)
        swt = sb("swt", [P, 2, 2, P], F32)
        pa = sb("pa", [P, 2, 2, P], BF16)
        EB = sb("EB", [P, 8, 2, P], BF16)
        colp = sb("colp", [P, CP_N], F32)
        cmat = sb("cmat", [P, 3 * P], F32)
        identb = sb("identb", [P, P], BF16)
        cmaskb = sb("cmaskb", [P, P], BF16)
        rbs = ysb[:].bitcast(F32)[0:32, 0:8 + 383]
        RBS_K = [("ysb", 0), ("ysb", 1)]
        v8 = pb[:].rearrange("p a b -> p (a b)").bitcast(F32)[0:8, 0:384]
        V8_K = [("pb", i) for i in range(NPB)]
        gtab = swt[:].rearrange("p a b c -> p (a b c)")[0:8, 0:384]
        GT_K = [("swt", 0), ("swt", 1)]
        scT = sb("scT", [P, KC, 4], F32)
        modT = sb("modT", [P, 48, 4], F32)
        gsT = sb("gsT", [P, 2, 8, 4], F32)
        small = sb("small", [P, 64], F32)
        lrp = sb("lrp", [P, 5, 8], F32)
        spt = sb("spt", [P, 5, 8], F32)
        FnegT = sb("FnegT", [P, NB, 8], F32)
        FendBC = sb("FendBC", [P, GT, 8], F32)
        l1b = sb("l1b", [P, 5 * G], F32)
        xrb1 = sb("xrb1", [P, G + 4], F32)
        xcb1 = sb("xcb1", [P, G], BF16)
        fdiag = sb("fdiag", [8, GT, 8], F32)
        ones8 = sb("ones8", [8, G], F32)
        fcar = sb("fcar", [8, 2], F32)
        den = sb("den", [P, 16], F32)
        tails = sb("tails", [P, KC, 4], F32)
        state = sb("state", [P, KC], F32)
        xrb = sb("xrb", [P, G + 4], F32)
        xcb = sb("xcb", [P, G], BF16)
        L1S = [
            dict(xc=osb[:, 0:G], xcK=("osb", 0), tha=th[:, 0:G], thaK=("th", 0), thx=th[:, G:2 * G], thxK=("th", 0),
                 ab=th[:, 2 * G:3 * G], abK=("th", 1), sg=th[:, 3 * G:4 * G], sgK=("th", 1),
                 xrb=xrb[:], xrbK="xrb0", xrtK="xrbt0", xcb=xcb[:], xcbK="xcb0", banks=(2, 3)),
            dict(xc=l1b[:, 0:G], xcK="l1xc", tha=l1b[:, G:2 * G], thaK="l1tha", thx=l1b[:, 2 * G:3 * G], thxK="l1thx",
                 ab=l1b[:, 3 * G:4 * G], abK="l1ab", sg=l1b[:, 4 * G:5 * G], sgK="l1sg",
                 xrb=xrb1[:], xrbK="xrb1", xrtK="xrbt1", xcb=xcb1[:], xcbK="xcb1", banks=(6, 7)),
        ]
        frow = [th[0:8, 0:G], th[0:8, G:2 * G], th[0:8, 2 * G:3 * G], th[0:8, 3 * G:4 * G], osb[0:8, 0:G]]
        FRK = [("th", 0), ("th", 0), ("th", 1), ("th", 1), ("osb", 0)]

        ps = [es.enter_context(nc.psum_tensor("psum%d" % i, [P, 512], F32)) for i in range(8)]
        PSK = ["ps%d" % i for i in range(8)]
        SC.excl.update(PSK)

        ident = cmat[:, 0:P]
        BT = cmat[:, P:3 * P].rearrange("p (j n h) -> p j n h", n=GT, h=8)
        jrev = cmat[:, P:2 * P]
        cmaskf = cmat[:, 2 * P:3 * P]

        def cp(off, n):
            return colp[:, off:off + n]

        for nm, dst, src, wk in (("colp", colp[:], colp_d, ["colp"]), ("cmat", cmat[:], cmat_d, ["cmat", "cmat_j"]), ("rbs", rbs, rb_d, RBS_K), ("v8", v8, v8_d, V8_K)):
            sem = SC.new_sem("ld_" + nm)
            add("sp", lambda e, dst=dst, src=src: e.dma_start(out=dst, in_=src), w=wk, dsem=sem)

        NLANE = 4
        for i in range(NLANE):
            SC.new_sem("wl%d" % i)
        WK = {nm: [] for nm in ("W0", "W1", "W2", "W3", "W4")}
        lane_last = [None] * NLANE
        wq = [0]

        def wload(nm, dst, src):
            i = wq[0]
            wq[0] += 1
            key = (nm, len(WK[nm]))
            WK[nm].append(key)
            ln = i % NLANE
            ex = [lane_last[ln]] if lane_last[ln] is not None else []
            lane_last[ln] = add("pool", lambda e: e.dma_start(out=dst, in_=src), w=[key], dsem="wl%d" % ln, extra=ex)

        for kc in range(KC):
            for (c0, c1) in ((0, 1668), (1668, C0)):
                wload("W0", W0[:, kc, c0:c1], w0_d[:, kc, c0:c1])
        for kc in range(0, KC, 2):
            wload("W1", W1[:, kc:kc + 2, :], w1_d[:, kc:kc + 2, :])
        for kc in range(KC):
            wload("W2", W2[:, kc, :], w2_d[:, kc, :])
        for kc in range(0, KC, 2):
            wload("W3", W3[:, kc:kc + 2, :], w3_d[:, kc:kc + 2, :])
        for i in range(2):
            for k2 in range(0, KC, 2):
                wload("W4", W4[:, i, k2:k2 + 2, :], w4_d[:, i, k2:k2 + 2, :])

        add("dve", lambda e: e.tensor_copy(out=identb[:], in_=ident), r=["cmat"], w=["identb"])
        add("dve", lambda e: e.tensor_copy(out=cmaskb[:], in_=cmaskf), r=["cmat_j"], w=["cmaskb"])
        add("dve", lambda e: e.memset(ones8[:], 1.0), w=["ones8"])
        add("dve", lambda e: e.memset(small[:, 0:1], -0.5), w=["negh"])
        add("dve", lambda e: e.memset(bV[:, :, :, 64:VW], 1.0), w=["bVones"])
        add("dve", lambda e: e.memset(aV[:, :, :, 64:VW], 1.0), w=["aVones"])
        negh = small[:, 0:1]

        cT = cp(CP_CT, 32)
        scf = scT[:].rearrange("p k b -> p (k b)")
        add("act", lambda e: e.activation(out=scf, in_=cT, func=AF.Tanh, scale=0.5), r=["colp"], w=["scT"])
        add("dve", lambda e: e.scalar_tensor_tensor(out=scf, in0=scf, scalar=1.0, in1=cT, op0=ALU.add, op1=ALU.mult), r=["scT", "colp"], w=["scT"])
        add("dve", lambda e: e.tensor_scalar(out=scf, in0=scf, scalar1=0.5, scalar2=None, op0=ALU.mult), r=["scT"], w=["scT"])

        stag = [(xn[:].rearrange("p (k j) -> p k j", k=KC), ("xn", 0)), (bc[:].rearrange("p (k j) -> p k j", k=KC), "bc")]
        SC.new_sem("ad0")
        SC.new_sem("ad1")
        idx = 0
        for l in range(2):
            for part in range(3):
                for c in range(KC):
                    slot, skey = stag[idx % 2]
                    col0 = part * D + c * P
                    src = ada_d[l, :, col0:col0 + P].rearrange("(k p) j -> p k j", p=P)
                    add("sp", lambda e, slot=slot, src=src: e.dma_start(out=slot, in_=src), w=[skey], dsem="ad%d" % (idx % 2))
                    for kc in range(KC):
                        add("pe", lambda e, slot=slot, kc=kc, idx=idx: e.matmul(ps[2][:, idx * 4:idx * 4 + 4], lhsT=slot[:, kc, :], rhs=scT[:, kc, :], start=(kc == 0), stop=(kc == KC - 1)),
                            r=[skey, "scT"], w=[PSK[2]])
                    idx += 1
        add("dve", lambda e: e.tensor_tensor(out=modT[:], in0=ps[2][:, 0:192].rearrange("p (i b) -> p i b", b=4),
                                             in1=cp(CP_AB, 48).unsqueeze(2).to_broadcast([P, 48, 4]), op=ALU.add),
            r=[PSK[2], "colp"], w=["modT"])
        for l in range(2):
            sc_l = modT[:, l * 24 + 8:l * 24 + 16, :]
            add("dve", lambda e, l=l, sc_l=sc_l: e.scalar_tensor_tensor(out=gsT[:, l, :, :], in0=sc_l, scalar=1.0,
                                                                    in1=cp(CP_NG + l * 8, 8).unsqueeze(2).to_broadcast([P, 8, 4]),
                                                                    op0=ALU.add, op1=ALU.mult), r=["modT", "colp"], w=["gsT"])
        add("dve", lambda e: e.tensor_scalar(out=modT[:, 40:48, :], in0=modT[:, 40:48, :], scalar1=0.25, scalar2=None, op0=ALU.mult), r=["modT"], w=["modT"])
        SC.new_sem("gs_w")
        for l in range(2):
            for bb in range(4):
                dst = bass.AP(tensor=gscr, offset=bb * 2 * D + l * D, ap=[[1, P], [P, KC]])
                add("sp", lambda e, l=l, bb=bb, dst=dst: e.dma_start(out=dst, in_=modT[:, l * 24 + 16:l * 24 + 24, bb], allow_slow_non_contiguous=True),
                    r=["modT"], w=["gscr"], dsem="gs_w")

        def shiftc(l, c, b):
            return modT[:, l * 24 + c, b:b + 1]

        def gsc(l, c, b):
            return gsT[:, l, c, b:b + 1]

        def softplus_from_e(e_ap, eK, out_ap, oK, t_z, zK, t_z2, z2K, t_ln, lnK):
            t_p = out_ap
            add("act", lambda e: e.activation(out=t_ln, in_=e_ap, func=AF.Ln, bias=1.0), r=[eK], w=[lnK])
            add("dve", lambda e: e.tensor_scalar(out=t_z, in0=e_ap, scalar1=2.0, scalar2=None, op0=ALU.add), r=[eK], w=[zK])
            add("dve", lambda e: e.reciprocal(out=t_z, in_=t_z), r=[zK], w=[zK])
            add("dve", lambda e: e.tensor_tensor(out=t_z, in0=t_z, in1=e_ap, op=ALU.mult), r=[zK, eK], w=[zK])
            add("dve", lambda e: e.tensor_tensor(out=t_z2, in0=t_z, in1=t_z, op=ALU.mult), r=[zK], w=[z2K])
            add("dve", lambda e: e.tensor_scalar(out=t_p, in0=t_z2, scalar1=1.0 / 9, scalar2=1.0 / 7, op0=ALU.mult, op1=ALU.add), r=[z2K], w=[oK])
            for cst in (1.0 / 5, 1.0 / 3, 1.0):
                add("dve", lambda e: e.tensor_tensor(out=t_p, in0=t_p, in1=t_z2, op=ALU.mult), r=[oK, z2K], w=[oK])
                add("dve", lambda e, cst=cst: e.tensor_scalar(out=t_p, in0=t_p, scalar1=cst, scalar2=None, op0=ALU.add), r=[oK], w=[oK])
            add("dve", lambda e: e.scalar_tensor_tensor(out=t_p, in0=t_z, scalar=2.0, in1=t_p, op0=ALU.mult, op1=ALU.mult), r=[oK, zK], w=[oK])
            add("dve", lambda e: e.tensor_scalar(out=t_z2, in0=e_ap, scalar1=0.5, scalar2=None, op0=ALU.is_lt), r=[eK, oK], w=[z2K])
            add("dve", lambda e: e.tensor_tensor(out=t_p, in0=t_p, in1=t_ln, op=ALU.subtract), r=[oK, lnK], w=[oK])
            add("dve", lambda e: e.tensor_tensor(out=t_p, in0=t_p, in1=t_z2, op=ALU.mult), r=[oK, z2K], w=[oK])
            add("dve", lambda e: e.tensor_tensor(out=t_p, in0=t_p, in1=t_ln, op=ALU.add), r=[oK, lnK], w=[oK])

        add("act", lambda e: e.activation(out=spt[:, 3, :], in_=cp(CP_LAM, 8), func=AF.Exp, scale=-1.0), r=["colp"], w=["spt_e"])
        softplus_from_e(spt[:, 3, :], "spt_e", spt[:, 2, :], "sptout", spt[:, 0, :], "sptz", spt[:, 1, :], "sptz2", spt[:, 4, :], "sptln")
        add("dve", lambda e: e.tensor_scalar(out=lrp[:, 0, :], in0=spt[:, 2, :], scalar1=-8.0, scalar2=None, op0=ALU.mult), r=["sptout"], w=["lrp0"])
        add("dve", lambda e: e.tensor_scalar(out=lrp[:, 1, :], in0=spt[:, 2, :], scalar1=-4.0, scalar2=None, op0=ALU.mult), r=["sptout"], w=["lrp1"])
        add("dve", lambda e: e.tensor_scalar(out=lrp[:, 2, :], in0=cp(CP_BA, 8), scalar1=0.5, scalar2=None, op0=ALU.mult), r=["colp"], w=["lrp2"])
        add("dve", lambda e: e.tensor_scalar(out=lrp[:, 3, :], in0=cp(CP_BX, 8), scalar1=0.5, scalar2=None, op0=ALU.mult), r=["colp"], w=["lrp3"])
        add("act", lambda e: e.activation(out=lrp[:, 4, :], in_=cp(CP_SINK, 8), func=AF.Exp), r=["colp"], w=["lrp4"])
        LRK = ["lrp0", "lrp1", "lrp2", "lrp3"]
        add("dve", lambda e: e.tensor_scalar(out=fcar[:, 1:2], in0=v8[:, 383:384], scalar1=-1.0, scalar2=None, op0=ALU.mult), r=V8_K, w=["nbf"])
        nbf = fcar[:, 1:2]

        add("pe", lambda e: e.matmul(ps[3][0:8, 0:383], lhsT=rbs[:, 0:8], rhs=rbs[:, 8:391], start=True, stop=True), r=RBS_K, w=[PSK[3]])
        add("act", lambda e: e.activation(out=gtab[:, 0:383], in_=ps[3][0:8, 0:383], func=AF.Exp), r=[PSK[3]], w=GT_K)
        add("dve", lambda e: e.memset(gtab[:, 383:384], 0.0), w=GT_K)
        add("dve", lambda e: e.tensor_tensor(out=gtab[:, 0:383], in0=gtab[:, 0:383], in1=v8[:, 0:383], op=ALU.mult), r=GT_K + V8_K, w=GT_K)
        SC.new_sem("eb_w")
        SC.new_sem("eb_r")
        add("sp", lambda e: e.dma_start(out=ebscr.ap(), in_=gtab), r=GT_K, w=["ebscr"], dsem="eb_w")
        hbuf = th[:].rearrange("p (h t) -> p h t", h=8)
        for kb in range(2):
            src = bass.AP(tensor=ebscr, offset=(1 - kb) * P, ap=[[1, P], [384, 8], [1, P]])
            add("sp", lambda e, src=src: e.dma_start(out=hbuf, in_=src), r=["ebscr"], w=[("th", 0), ("th", 1)], dsem="eb_r")
            for q in range(2):
                add("pe", lambda e, q=q: e.matmul(ps[4 + q][:, :], lhsT=jrev, rhs=th[:, q * 512:(q + 1) * 512], start=True, stop=True),
                    r=[("th", 0), ("th", 1), "cmat_j"], w=[PSK[4 + q]])
                add("dve", lambda e, q=q, kb=kb: e.tensor_copy(out=EB[:, q * 4:(q + 1) * 4, kb, :], in_=ps[4 + q][:, :].rearrange("p (h t) -> p h t", h=4)),
                    r=[PSK[4 + q]], w=["EB"])

        for pq in ("00", "01", "10", "11"):
            SC.new_sem("xl" + pq)
        SC.new_sem("xst")
        SC.new_sem("xst0")
        SC.new_sem("xst1")
        SC.new_sem("bcs")
        if dbg:
            SC.new_sem("dbgs0")
            SC.new_sem("dbgs1")
        evq = [0]

        def xk(ti):
            return ("x", xs.cur, ti)

        def rmsnorm_rstd(ti, rcol):
            add("act", lambda e: e.activation(out=ysb[:], in_=xs[:, ti, :], func=AF.Square, accum_out=small[:, rcol:rcol + 1]),
                r=[xk(ti)], w=[("ysb", 0), ("ysb", 1), ("sm", rcol)])
            if RSTD_POW:
                add("dve", lambda e: e.tensor_scalar(out=small[:, rcol:rcol + 1], in0=small[:, rcol:rcol + 1], scalar1=1.0 / D, scalar2=EPS, op0=ALU.mult, op1=ALU.add),
                    r=[("sm", rcol)], w=[("sm", rcol)])
                add("pool", lambda e: e.tensor_tensor(out=small[:, rcol:rcol + 1], in0=small[:, rcol:rcol + 1], in1=negh, op=ALU.pow),
                    r=[("sm", rcol), "negh"], w=[("sm", rcol)])
            else:
                add("act", lambda e: e.activation(out=small[:, rcol:rcol + 1], in_=small[:, rcol:rcol + 1], func=AF.Ln, scale=1.0 / D, bias=EPS),
                    r=[("sm", rcol)], w=[("sm", rcol)])
                add("act", lambda e: e.activation(out=small[:, rcol:rcol + 1], in_=small[:, rcol:rcol + 1], func=AF.Exp, scale=-0.5),
                    r=[("sm", rcol)], w=[("sm", rcol)])

        def norm_to_hT(l, b, skip_stats=False):
            for ti in range(GT):
                if not skip_stats:
                    rmsnorm_rstd(ti, 2 + ti)
                if stop == "na":
                    continue
                scr, scrK = (xn, [("xn", 0), ("xn", 1)]) if ti == 0 else (bc, ["bc"])
                add("dve", lambda e, ti=ti, scr=scr: e.tensor_scalar(out=scr[:], in0=xs[:, ti, :], scalar1=small[:, 2 + ti:3 + ti], scalar2=None, op0=ALU.mult),
                    r=[xk(ti), ("sm", 2 + ti)], w=scrK)
                if stop == "nb":
                    continue
                for c in range(KC):
                    add("pe", lambda e, ti=ti, c=c, scr=scr: e.transpose(ps[c // 2][:, (c % 2) * G + ti * P:(c % 2) * G + (ti + 1) * P], scr[:, c * P:(c + 1) * P], ident),
                        r=scrK + ["cmat"], w=[PSK[c // 2]])
            if stop in ("na", "nb", "nc"):
                return
            for c in range(KC):
                src = ps[c // 2][:, (c % 2) * G:(c % 2 + 1) * G]
                if (c // 2) % 2 == 0:
                    add("act", lambda e, c=c, src=src: e.activation(out=hT[:, c, :], in_=src, func=AF.Identity, scale=gsc(l, c, b), bias=shiftc(l, c, b)),
                        r=[PSK[c // 2], "gsT", "modT"], w=[("hT", c)])
                else:
                    add("dve", lambda e, c=c, src=src: e.tensor_scalar(out=hT[:, c, :], in0=src, scalar1=gsc(l, c, b), scalar2=shiftc(l, c, b), op0=ALU.mult, op1=ALU.add),
                        r=[PSK[c // 2], "gsT", "modT"], w=[("hT", c)])

        def norm_tile_to_hT(l, b, ti, bank):
            rmsnorm_rstd(ti, 2 + ti)
            add("dve", lambda e: e.tensor_scalar(out=xn[:], in0=xs[:, ti, :], scalar1=small[:, 2 + ti:3 + ti], scalar2=None, op0=ALU.mult),
                r=[xk(ti), ("sm", 2 + ti)], w=[("xn", 0), ("xn", 1)])
            for half in range(2):
                for cc in range(4):
                    c = half * 4 + cc
                    add("pe", lambda e, c=c, cc=cc: e.transpose(ps[bank][:, cc * P:(cc + 1) * P], xn[:, c * P:(c + 1) * P], ident),
                        r=[("xn", 0), ("xn", 1), "cmat"], w=[PSK[bank]])
                for cc in range(4):
                    c = half * 4 + cc
                    src = ps[bank][:, cc * P:(cc + 1) * P]
                    if half == 0:
                        add("act", lambda e, c=c, src=src: e.activation(out=hT[:, c, ti * P:(ti + 1) * P], in_=src, func=AF.Identity, scale=gsc(l, c, b), bias=shiftc(l, c, b)),
                            r=[PSK[bank], "gsT", "modT"], w=[("hT", c)])
                    else:
                        add("dve", lambda e, c=c, src=src: e.tensor_scalar(out=hT[:, c, ti * P:(ti + 1) * P], in0=src, scalar1=gsc(l, c, b), scalar2=shiftc(l, c, b), op0=ALU.mult, op1=ALU.add),
                            r=[PSK[bank], "gsT", "modT"], w=[("hT", c)])

        HTK = [("hT", c) for c in range(KC)]

        def proj_fm(Wt, wkey, col0, M, evac):
            bi = 2 + (evq[0] % 2)
            evq[0] += 1
            for kc in range(KC):
                add("pe", lambda e, kc=kc, bi=bi: e.matmul(ps[bi][0:M, 0:G], lhsT=Wt[:, kc, col0:col0 + M], rhs=hT[:, kc, :], start=(kc == 0), stop=(kc == KC - 1)),
                    r=WK[wkey] + HTK, w=[PSK[bi]])
            evac(ps[bi][0:M, 0:G], PSK[bi])

        def load_bc(src_ap, rkeys):
            add("sp", lambda e: e.dma_start(out=bc[:], in_=src_ap), r=rkeys, w=["bc"], dsem="bcs")

        def residual_from(ti, hf, acc_bank):
            add("dve", lambda e: e.tensor_tensor(out=xn[:, hf * 512:(hf + 1) * 512], in0=ps[acc_bank][:, :], in1=bc[:, hf * 512:(hf + 1) * 512], op=ALU.mult),
                r=[PSK[acc_bank], "bc"], w=[("xn", hf)])
            add("dve", lambda e: e.tensor_tensor(out=xs[:, ti, hf * 512:(hf + 1) * 512], in0=xs[:, ti, hf * 512:(hf + 1) * 512], in1=xn[:, hf * 512:(hf + 1) * 512], op=ALU.add),
                r=[("xn", hf), xk(ti)], w=[xk(ti)])

        def out_proj_residual(Wt, wkey, srcT, skey_fn, ti):
            for hf in range(2):
                for kc in range(KC):
                    add("pe", lambda e, kc=kc, hf=hf: e.matmul(ps[hf][:, :], lhsT=srcT[:, kc, ti * P:(ti + 1) * P], rhs=Wt[:, kc, hf * 512:(hf + 1) * 512],
                                                               start=(kc == 0), stop=(kc == KC - 1)),
                        r=WK[wkey] + skey_fn(kc), w=[PSK[hf]])
                residual_from(ti, hf, hf)

        sbank = [0]
        pbq = [0]
        swq = [0]

        last_store = []
        pre_stats = [False]
        for b in range(nseq):
            add("pool", lambda e: e.memset(tails[:], 0.0), w=["tails"])
            add("pool", lambda e: e.memset(state[:], 0.0), w=["state"])
            add("pool", lambda e: e.memset(fcar[:, 0:1], 0.0), w=["fcar"])
            for g in range(ngrp):
                t0 = g * G
                gi = b * ngrp + g
                xs.cur = gi % 2

                def load_x(gj):
                    bj, gg = divmod(gj, ngrp)
                    keep = xs.cur
                    xs.cur = gj % 2
                    for ti in range(GT):
                        add("sp", lambda e, ti=ti: e.dma_start(out=xs[:, ti, :], in_=x_d[bj, gg * G + ti * P:gg * G + (ti + 1) * P, :]), w=[xk(ti)], dsem="xl%d%d" % (gj % 2, ti))
                    xs.cur = keep
                if gi == 0:
                    load_x(0)
                if gi + 1 < nseq * ngrp:
                    load_x(gi + 1)

                def store_x():
                    for ti in range(GT):
                        d = add("sp", lambda e, ti=ti: e.dma_start(out=out_d[b, t0 + ti * P:t0 + (ti + 1) * P, :], in_=xs[:, ti, :]), r=[xk(ti)], dsem="xst")
                    return [d]
                if stop == "pro":
                    last_store = store_x()
                    continue
                norm_to_hT(0, b, skip_stats=pre_stats[0])
                pre_stats[0] = False
                if stop in ("n0", "na", "nb", "nc"):
                    last_store = store_x()
                    continue
                load_bc(gscr.ap()[b, 0, :].partition_broadcast(P), ["gscr"])
                if stop == "n1":
                    last_store = store_x()
                    continue
                if g > 0:
                    add("pool", lambda e: e.tensor_copy(out=aKT[:, 0:P], in_=aKT[:, G:G + P]), r=["aK"], w=["aK"])
                    add("pool", lambda e: e.tensor_copy(out=aV[:, 0, :, 0:64], in_=aV[:, GT, :, 0:64]), r=[("aV", GT)], w=[("aV", 0)])
                proj_fm(W0, "W0", FL, 8, lambda src, k: add("act", lambda e: e.activation(out=frow[3], in_=src, func=AF.Exp, scale=-1.0, bias=nbf), r=[k, "nbf"], w=[FRK[3]]))
                softplus_from_e(frow[3], FRK[3], frow[2], FRK[2], frow[0], FRK[0], frow[1], FRK[1], frow[4], FRK[4])
                add("dve", lambda e: e.tensor_tensor_scan(out=frow[4], data0=ones8[:], data1=frow[2], initial=fcar[:, 0:1], op0=ALU.mult, op1=ALU.add),
                    r=[FRK[2], "ones8", "fcar"], w=[FRK[4]])
                add("dve", lambda e: e.tensor_copy(out=fcar[:, 0:1], in_=frow[4][:, G - 1:G]), r=[FRK[4]], w=["fcar"])
                for c in range(4):
                    proj_fm(W0, "W0", AQ + c * P, P, lambda src, k, c=c: add("act", lambda e: e.copy(out=aQT[:, c, :], in_=src), r=[k], w=[("aQ", c)]))
                proj_fm(W0, "W0", AK, P, lambda src, k: add("act", lambda e: e.copy(out=aKT[:, P:P + G], in_=src), r=[k], w=["aK"]))
                for c in range(4):
                    proj_fm(W0, "W0", BQ + c * P, P, lambda src, k, c=c: add("act", lambda e: e.copy(out=bQT[:, c, :], in_=src), r=[k], w=[("bQ", c)]))
                for c in range(4):
                    proj_fm(W0, "W0", BK + c * P, P, lambda src, k, c=c: add("act", lambda e: e.copy(out=bKT[:, c, t0:t0 + G], in_=src), r=[k], w=[("bK", c, g)]))
                for nl in range(GT):
                    add("pe", lambda e, nl=nl: e.transpose(ps[7][:, 300 + nl * 8:300 + nl * 8 + 8], frow[4][:, nl * P:(nl + 1) * P], ident[0:8, 0:8]),
                        r=[FRK[4], "cmat"], w=[PSK[7]])
                    add("dve", lambda e, nl=nl: e.tensor_scalar(out=fdiag[:, nl, :], in0=ident[0:8, 0:8], scalar1=frow[4][:, (nl + 1) * P - 1:(nl + 1) * P], scalar2=None, op0=ALU.mult),
                        r=[FRK[4], "cmat"], w=["fdiag"])
                add("dve", lambda e: e.tensor_copy(out=FnegT[:, g * GT:(g + 1) * GT, :], in_=ps[7][:, 300:300 + GT * 8].rearrange("p (n h) -> p n h", h=8)),
                    r=[PSK[7]], w=["FnegT"])
                add("pe", lambda e: e.matmul(ps[7][:, 400:400 + GT * 8], lhsT=ones8[:, 0:P], rhs=fdiag[:].rearrange("k n h -> k (n h)"), start=True, stop=True),
                    r=["ones8", "fdiag"], w=[PSK[7]])
                add("dve", lambda e: e.tensor_copy(out=FendBC[:], in_=ps[7][:, 400:400 + GT * 8].rearrange("p (n h) -> p n h", h=8)), r=[PSK[7]], w=["FendBC"])
                nj = (g + 1) * GT
                add("dve", lambda e, nj=nj: e.tensor_tensor(out=BT[:, 0:nj, :, :], in0=FnegT[:, 0:nj, :].unsqueeze(2).to_broadcast([P, nj, GT, 8]),
                                                            in1=FendBC[:].unsqueeze(1).to_broadcast([P, nj, GT, 8]), op=ALU.subtract),
                    r=["FnegT", "FendBC"], w=["BT", "cmat_j"])
                for ti in range(GT):
                    blk = g * GT + ti
                    bi = 2 + (evq[0] % 2)
                    evq[0] += 1
                    for kc in range(KC):
                        add("pe", lambda e, kc=kc, bi=bi, ti=ti: e.matmul(ps[bi][:, 0:P], lhsT=hT[:, kc, ti * P:(ti + 1) * P], rhs=W0[:, kc, AV:AV + P], start=(kc == 0), stop=(kc == KC - 1)),
                            r=WK["W0"] + HTK, w=[PSK[bi]])
                    add("act", lambda e, bi=bi, ti=ti: e.copy(out=aV[:, 1 + ti, :, 0:64], in_=ps[bi][:, 0:P].rearrange("p (g d) -> p g d", g=2)), r=[PSK[bi]], w=[("aV", 1 + ti)])
                    bi = 2 + (evq[0] % 2)
                    evq[0] += 1
                    for kc in range(KC):
                        add("pe", lambda e, kc=kc, bi=bi, ti=ti: e.matmul(ps[bi][:, :], lhsT=hT[:, kc, ti * P:(ti + 1) * P], rhs=W0[:, kc, BV:BV + 512], start=(kc == 0), stop=(kc == KC - 1)),
                            r=WK["W0"] + HTK, w=[PSK[bi]])
                    add("act", lambda e, bi=bi, blk=blk: e.copy(out=bV[:, blk, :, 0:64], in_=ps[bi][:, :].rearrange("p (h d) -> p h d", h=8)), r=[PSK[bi]], w=[("bV", blk)])

                def load_bc_l1():
                    load_bc(gscr.ap()[b, 1, :].partition_broadcast(P), ["gscr"])

                def make_block(nl):
                    n = g * GT + nl
                    q0 = nl * P
                    kbs = [1] if n == 0 else [0, 1]

                    def gate_mm():
                        for hf in range(2):
                            for kc in range(KC):
                                add("pe", lambda e, kc=kc, hf=hf: e.matmul(ps[hf][:, :], lhsT=hT[:, kc, q0:q0 + P], rhs=W0[:, kc, GA + hf * 512:GA + (hf + 1) * 512], start=(kc == 0), stop=(kc == KC - 1)),
                                    r=WK["W0"] + HTK, w=[PSK[hf]])

                    def gate_act():
                        for hf in range(2):
                            add("act", lambda e, hf=hf: e.activation(out=th[:, hf * 512:(hf + 1) * 512], in_=ps[hf][:, :], func=AF.Tanh, scale=0.5), r=[PSK[hf]], w=[("th", hf)])
                            add("dve", lambda e, hf=hf: e.scalar_tensor_tensor(out=th[:, hf * 512:(hf + 1) * 512], in0=th[:, hf * 512:(hf + 1) * 512], scalar=1.0, in1=ps[hf][:, :], op0=ALU.add, op1=ALU.mult),
                                r=[PSK[hf], ("th", hf)], w=[("th", hf)])

                    def swa_S(h):
                        bi = 4 + (sbank[0] % 2)
                        sbank[0] += 1
                        c, base = h % 4, (h // 4) * 64
                        for kb in kbs:
                            add("pe", lambda e, kb=kb, bi=bi: e.matmul(ps[bi][:, kb * P:(kb + 1) * P], lhsT=aKT[base:base + 64, (nl + kb) * P:(nl + kb + 1) * P],
                                                                      rhs=aQT[base:base + 64, c, q0:q0 + P], start=True, stop=True),
                                r=["aK", ("aQ", c)], w=[PSK[bi]])
                        return bi

                    def swa_rest(h, bi):
                        kvh = h // 4
                        si = swq[0] % 2
                        swq[0] += 1
                        k0, k1 = kbs[0], kbs[-1] + 1
                        add("act", lambda e: e.activation(out=swt[:, si, k0:k1, :], in_=ps[bi][:, k0 * P:k1 * P].rearrange("p (k t) -> p k t", t=P), func=AF.Exp, scale=0.125),
                            r=[PSK[bi]], w=[("swt", si)])
                        add("pool" if h % 2 == 0 else "dve", lambda e: e.tensor_tensor(out=pa[:, si, k0:k1, :], in0=swt[:, si, k0:k1, :], in1=EB[:, h, k0:k1, :], op=ALU.mult),
                            r=[("swt", si), "EB"], w=[("pa", si)])
                        pvb = 6 + h // 4
                        for kb in kbs:
                            add("pe", lambda e, kb=kb: e.matmul(ps[pvb][:, (h % 4) * 65:(h % 4) * 65 + 65], lhsT=pa[:, si, kb, :], rhs=aV[:, nl + kb, kvh, 0:65],
                                                                start=(kb == kbs[0]), stop=(kb == kbs[-1])),
                                r=[("pa", si), ("aV", nl + kb), "aVones"], w=[PSK[pvb]])

                    def normalise(dcol, sink):
                        for hb in range(2):
                            pv = ps[6 + hb][:, 0:260].rearrange("p (h d) -> p h d", d=65)
                            dn = den[:, dcol + hb * 4:dcol + hb * 4 + 4]
                            dk = ("den", dcol // 4 + hb)
                            if sink:
                                add("dve", lambda e: e.tensor_tensor(out=dn, in0=pv[:, :, 64], in1=lrp[:, 4, hb * 4:hb * 4 + 4], op=ALU.add), r=[PSK[6 + hb], "lrp4"], w=[dk])
                                add("dve", lambda e: e.reciprocal(out=dn, in_=dn), r=[dk], w=[dk])
                            else:
                                add("dve", lambda e: e.reciprocal(out=dn, in_=pv[:, :, 64]), r=[PSK[6 + hb]], w=[dk])
                            add("dve", lambda e: e.tensor_scalar(out=dn, in0=dn, scalar1=0.5, scalar2=None, op0=ALU.mult), r=[dk], w=[dk])
                            add("dve", lambda e: e.tensor_tensor(out=osb[:, hb * 256:(hb + 1) * 256].rearrange("p (h d) -> p h d", d=64), in0=pv[:, :, 0:64],
                                                                in1=dn.unsqueeze(2).to_broadcast([P, 4, 64]), op=ALU.mult),
                                r=[PSK[6 + hb], dk], w=[("osb", hb)])

                    def swa():
                        prev = swa_S(0)
                        for h in range(8):
                            nxt = swa_S(h + 1) if h + 1 < 8 else None
                            swa_rest(h, prev)
                            prev = nxt
                        normalise(0, True)
                        add("pool", lambda e: e.tensor_tensor(out=ysb[:, 0:512], in0=th[:, 0:512], in1=osb[:], op=ALU.mult), r=[("th", 0), ("osb", 0), ("osb", 1)], w=[("ysb", 0)])

                    def fox_S(ch):
                        h, j0, j1 = ch
                        bi = 4 + (sbank[0] % 2)
                        sbank[0] += 1
                        c, base = h // 2, (h % 2) * 64
                        for j in range(j0, j1):
                            add("pe", lambda e, j=j, bi=bi: e.matmul(ps[bi][:, (j - j0) * P:(j - j0 + 1) * P], lhsT=bKT[base:base + 64, c, j * P:(j + 1) * P],
                                                                    rhs=bQT[base:base + 64, c, q0:q0 + P], start=True, stop=True),
                                r=[("bK", c, j // GT), ("bQ", c)], w=[PSK[bi]])
                        return bi

                    def fox_rest(ch, bi):
                        h, j0, j1 = ch
                        pvb = 6 + h // 4
                        for j in range(j0, j1):
                            si = pbq[0] % NPB
                            pbq[0] += 1
                            add("act", lambda e, j=j, si=si: e.activation(out=pb[:, si, :], in_=ps[bi][:, (j - j0) * P:(j - j0 + 1) * P], func=AF.Exp, scale=0.125, bias=BT[:, j, nl, h:h + 1]),
                                r=[PSK[bi], "BT"], w=[("pb", si)])
                            if j == n:
                                add("pool", lambda e, si=si: e.tensor_tensor(out=pb[:, si, :], in0=pb[:, si, :], in1=cmaskb[:], op=ALU.mult), r=[("pb", si), "cmaskb"], w=[("pb", si)])
                            add("pe", lambda e, j=j, si=si: e.matmul(ps[pvb][:, (h % 4) * 65:(h % 4) * 65 + 65], lhsT=pb[:, si, :], rhs=bV[:, j, h, 0:65], start=(j == 0), stop=(j == n)),
                                r=[("pb", si), ("bV", j), "bVones"], w=[PSK[pvb]])

                    def fox():
                        chunks = []
                        for h in range(8):
                            for j0 in range(0, n + 1, 4):
                                chunks.append((h, j0, min(n + 1, j0 + 4)))
                        prev = fox_S(chunks[0])
                        for i, ch in enumerate(chunks):
                            nxt = fox_S(chunks[i + 1]) if i + 1 < len(chunks) else None
                            fox_rest(ch, prev)
                            prev = nxt
                        normalise(8, False)
                        add("pool", lambda e: e.tensor_tensor(out=ysb[:, 512:1024], in0=th[:, 512:1024], in1=osb[:], op=ALU.mult), r=[("th", 1), ("osb", 0), ("osb", 1)], w=[("ysb", 1)])

                    def tail_a():
                        ytp = ps[2][:, :].bitcast(BF16)
                        for kc in range(KC):
                            add("pe", lambda e, kc=kc: e.transpose(ytp[:, kc * P:(kc + 1) * P], ysb[:, kc * P:(kc + 1) * P], identb[:]),
                                r=[("ysb", kc // 4), "identb"], w=[PSK[2]])
                        add("dve", lambda e: e.tensor_copy(out=yT[:, :, q0:q0 + P], in_=ytp.rearrange("p (k t) -> p k t", t=P)), r=[PSK[2]], w=[("yT", nl)] + Y1K)

                    def tail_b():
                        out_proj_residual(W1, "W1", yT, lambda kc: [("yT", nl)], nl)

                    return gate_mm, gate_act, swa, fox, tail_a, tail_b

                Y1K = [("y1", c) for c in range(KC)]
                blocks = [make_block(nl) for nl in range(GT)]
                for nl in range(GT):
                    gm, ga, sw, fx, ta, tb = blocks[nl]
                    if nl == 0:
                        gm()
                        ga()
                    sw()
                    fx()
                    if nl + 1 < GT:
                        blocks[nl + 1][0]()
                    ta()
                    if nl + 1 < GT:
                        blocks[nl + 1][1]()
                    tb()
                    if EARLY_L1_NORM and stop is None:
                        norm_tile_to_hT(1, b, nl, 3)
                        if nl == GT - 1:
                            load_bc_l1()

                if dbg:
                    for ti in range(GT):
                        add("sp", lambda e, ti=ti: e.dma_start(out=dbg_d[b, t0 + ti * P:t0 + (ti + 1) * P, :], in_=xs[:, ti, :]), r=[xk(ti)], dsem="dbgs%d" % ti)

                if stop == "l0":
                    last_store = store_x()
                    continue
                if not (EARLY_L1_NORM and stop is None):
                    norm_to_hT(1, b)
                    load_bc_l1()
                YTK = [("yT", nl) for nl in range(GT)]

                def l1_A(c, T):
                    bA = T["banks"][0]
                    for kc in range(KC):
                        add("pe", lambda e, kc=kc: e.matmul(ps[bA][:, 0:G], lhsT=W2[:, kc, c * P:(c + 1) * P], rhs=hT[:, kc, :], start=(kc == 0), stop=(kc == KC - 1)),
                            r=WK["W2"] + HTK, w=[PSK[bA]])
                    for kc in range(KC):
                        add("pe", lambda e, kc=kc: e.matmul(ps[bA][:, G:2 * G], lhsT=W2[:, kc, D + c * P:D + (c + 1) * P], rhs=hT[:, kc, :], start=(kc == 0), stop=(kc == KC - 1)),
                            r=WK["W2"] + HTK, w=[PSK[bA]])

                def l1_B1(c, T):
                    bA = T["banks"][0]
                    add("pool", lambda e: e.tensor_copy(out=T["xrb"][:, 0:3], in_=tails[:, c, 0:3]), r=["tails"], w=[T["xrtK"]])
                    add("act", lambda e: e.copy(out=T["xrb"][:, 3:3 + G], in_=ps[bA][:, 0:G]), r=[PSK[bA]], w=[T["xrbK"]])
                    add("pool", lambda e: e.tensor_copy(out=tails[:, c, 0:3], in_=T["xrb"][:, G:G + 3]), r=[T["xrbK"]], w=["tails"])
                    add("act", lambda e: e.activation(out=T["sg"], in_=ps[bA][:, G:2 * G], func=AF.Tanh, scale=0.5), r=[PSK[bA]], w=[T["sgK"]])

                def l1_B2(c, T):
                    bA = T["banks"][0]
                    add("dve", lambda e: e.scalar_tensor_tensor(out=T["sg"], in0=T["sg"], scalar=1.0, in1=ps[bA][:, G:2 * G], op0=ALU.add, op1=ALU.mult), r=[PSK[bA], T["sgK"]], w=[T["sgK"]])
                    cw = lambda j: colp[:, CP_CW + c * 4 + j:CP_CW + c * 4 + j + 1]
                    add("pool", lambda e: e.tensor_scalar(out=T["xc"], in0=T["xrb"][:, 0:G], scalar1=cw(0), scalar2=colp[:, CP_CB + c:CP_CB + c + 1], op0=ALU.mult, op1=ALU.add),
                        r=[T["xrbK"], T["xrtK"], "colp"], w=[T["xcK"]])
                    for j in range(1, 4):
                        add("dve", lambda e, j=j: e.scalar_tensor_tensor(out=T["xc"], in0=T["xrb"][:, j:j + G], scalar=cw(j), in1=T["xc"], op0=ALU.mult, op1=ALU.add),
                            r=[T["xrbK"], T["xrtK"], "colp", T["xcK"]], w=[T["xcK"]])
                    add("dve", lambda e: e.tensor_copy(out=T["xcb"], in_=T["xc"]), r=[T["xcK"]], w=[T["xcbK"]])

                def l1_C(c, T):
                    bB = T["banks"][1]
                    add("pe", lambda e: e.matmul(ps[bB][:, 0:G], lhsT=W4[:, 0, c, :], rhs=T["xcb"], start=True, stop=True), r=WK["W4"] + [T["xcbK"]], w=[PSK[bB]])
                    add("pe", lambda e: e.matmul(ps[bB][:, G:2 * G], lhsT=W4[:, 1, c, :], rhs=T["xcb"], start=True, stop=True), r=WK["W4"] + [T["xcbK"]], w=[PSK[bB]])

                def l1_D1(c, T):
                    bB = T["banks"][1]
                    add("act", lambda e: e.activation(out=T["tha"], in_=ps[bB][:, 0:G], func=AF.Tanh, scale=0.5, bias=lrp[:, 2, c:c + 1]), r=[PSK[bB]] + LRK, w=[T["thaK"]])
                    add("act", lambda e: e.activation(out=T["thx"], in_=ps[bB][:, G:2 * G], func=AF.Tanh, scale=0.5, bias=lrp[:, 3, c:c + 1]), r=[PSK[bB]] + LRK, w=[T["thxK"]])

                def l1_D2(c, T):
                    add("act", lambda e: e.activation(out=T["ab"], in_=T["tha"], func=AF.Exp, scale=lrp[:, 1, c:c + 1], bias=lrp[:, 1, c:c + 1]), r=[T["thaK"]] + LRK, w=[T["abK"]])
                    add("act", lambda e: e.activation(out=T["tha"], in_=T["tha"], func=AF.Exp, scale=lrp[:, 0, c:c + 1], bias=lrp[:, 0, c:c + 1]), r=[T["thaK"]] + LRK, w=[T["thaK"]])
                    add("dve", lambda e: e.scalar_tensor_tensor(out=T["thx"], in0=T["thx"], scalar=1.0, in1=T["xc"], op0=ALU.add, op1=ALU.mult), r=[T["thxK"], T["xcK"]], w=[T["thxK"]])

                def l1_D3(c, T):
                    add("act", lambda e: e.activation(out=T["tha"], in_=T["tha"], func=AF.Sqrt, scale=-1.0, bias=1.0), r=[T["thaK"]], w=[T["thaK"]])

                def l1_D4(c, T):
                    add("pool", lambda e: e.tensor_tensor(out=T["tha"], in0=T["tha"], in1=T["thx"], op=ALU.mult), r=[T["thaK"], T["thxK"]], w=[T["thaK"]])
                    add("dve", lambda e: e.tensor_tensor_scan(out=T["xc"], data0=T["ab"], data1=T["tha"], initial=state[:, c:c + 1], op0=ALU.mult, op1=ALU.add),
                        r=[T["abK"], T["thaK"], "state", T["thxK"]], w=[T["xcK"]])
                    add("pool", lambda e: e.tensor_copy(out=state[:, c:c + 1], in_=T["xc"][:, G - 1:G]), r=[T["xcK"]], w=["state"])
                    add("pool", lambda e: e.tensor_tensor(out=yT[:, c, :], in0=T["xc"], in1=T["sg"], op=ALU.mult),
                        r=[T["xcK"], T["sgK"]], w=YTK + [("y1", c)])

                ACCB = (0, 1, 4, 5)

                def l1_O(c, T):
                    for ti in range(GT):
                        for hf in range(2):
                            ab_ = ACCB[ti * 2 + hf]
                            add("pe", lambda e, ti=ti, hf=hf, ab_=ab_: e.matmul(ps[ab_][:, :], lhsT=yT[:, c, ti * P:(ti + 1) * P], rhs=W3[:, c, hf * 512:(hf + 1) * 512],
                                                                           start=(c == 0), stop=(c == KC - 1)),
                                r=WK["W3"] + [("y1", c)], w=[PSK[ab_]])

                pairs = [(2 * k, 2 * k + 1) for k in range(KC // 2)]

                def stage(fn, pr):
                    for i, c in enumerate(pr):
                        fn(c, L1S[i])

                stage(l1_A, pairs[0])
                if stop is None and gi + 1 < nseq * ngrp:
                    xs.cur = (gi + 1) % 2
                    for ti in range(GT):
                        rmsnorm_rstd(ti, 2 + ti)
                    xs.cur = gi % 2
                    pre_stats[0] = True
                for k, pr in enumerate(pairs):
                    stage(l1_B1, pr)
                    stage(l1_B2, pr)
                    stage(l1_C, pr)
                    stage(l1_D1, pr)
                    if k + 1 < len(pairs):
                        stage(l1_A, pairs[k + 1])
                    stage(l1_D2, pr)
                    stage(l1_D3, pr)
                    stage(l1_D4, pr)
                    stage(l1_O, pr)
                if stop == "l1":
                    for ti in range(GT):
                        for hf in range(2):
                            residual_from(ti, hf, ACCB[ti * 2 + hf])
                    last_store = store_x()
                    continue
                for ti in range(GT):
                    for hf in range(2):
                        residual_from(ti, hf, ACCB[ti * 2 + hf])
                load_bc(fg_d.ap()[0, :].partition_broadcast(P), [])
                fin = [(th[:], [("th", 0), ("th", 1)]), (l1b[:, 0:D], ["l1xc", "l1tha", "l1thx", "l1ab"])]
                for ti in range(GT):
                    rmsnorm_rstd(ti, 4 + ti)
                    fo, fk = fin[ti]
                    add("dve", lambda e, ti=ti, fo=fo: e.scalar_tensor_tensor(out=fo, in0=xs[:, ti, :], scalar=small[:, 4 + ti:5 + ti], in1=bc[:], op0=ALU.mult, op1=ALU.mult),
                        r=[xk(ti), ("sm", 4 + ti), "bc"], w=fk)
                    add("sp", lambda e, ti=ti, fo=fo: e.dma_start(out=out_d[b, t0 + ti * P:t0 + (ti + 1) * P, :], in_=fo), r=fk, dsem="xst%d" % ti)
                    last_store = [("xst0", SC.cnt["xst0"]), ("xst1", SC.cnt["xst1"])]
        finals = list(last_store)
        if dbg:
            finals.append(("dbgs0", SC.cnt["dbgs0"]))
            finals.append(("dbgs1", SC.cnt["dbgs1"]))
        SC.run(final_conds=finals)
    return nc


def _t5_bucket(rel):
    n = np.maximum(rel, 0)
    nf = np.maximum(n, 1).astype(np.float32)
    large = 16 + (np.log(nf / np.float32(16)) / np.float32(math.log(128 / 16)) * np.float32(32 - 16)).astype(np.int32)
    large = np.minimum(large, 31)
    return np.where(n < 16, n, large)


def _host_layout(inp, nseq_core=SEQ_PER_CORE, ncores=NCORES):
    f = lambda a: np.ascontiguousarray(np.asarray(a, dtype=np.float32))
    w_in = f(inp["attn_w_in"])[0]
    cols = np.concatenate([
        np.concatenate([np.concatenate([np.arange(c * 64, (c + 1) * 64), np.arange((4 + c) * 64, (5 + c) * 64)]) for c in range(4)]),
        np.arange(512, 640),
        np.arange(768, 1280),
        np.arange(1280, 1792),
        np.arange(2304, 2312),
        np.arange(640, 768),
        np.arange(1792, 2304),
        np.arange(2312, 3336),
    ])
    assert cols.size == C0
    kcl = lambda w: np.ascontiguousarray(w.reshape(KC, P, w.shape[1]).transpose(1, 0, 2))
    shared = {
        "w0": kcl(w_in[:, cols]),
        "w1": kcl(f(inp["attn_w_out"])[0]),
        "w2": kcl(f(inp["lru_w_in"])[0]),
        "w3": kcl(f(inp["lru_w_out"])[0]),
        "w4": np.ascontiguousarray(np.stack([f(inp["lru_w_a"])[0], f(inp["lru_w_x"])[0]], 0).transpose(2, 0, 1, 3)),
        "ada_w": f(inp["ada_w"]),
        "final_g": f(inp["final_g"]).reshape(1, D),
    }
    colv = lambda v: np.ascontiguousarray(v.reshape(KC, P).T)
    colp = np.zeros((P, CP_N), np.float32)
    ng = f(inp["norm_g"])
    ab = f(inp["ada_b"])
    for l in range(2):
        colp[:, CP_NG + l * 8:CP_NG + (l + 1) * 8] = colv(ng[l])
        for part in range(3):
            colp[:, CP_AB + (l * 3 + part) * 8:CP_AB + (l * 3 + part + 1) * 8] = colv(ab[l, part * D:(part + 1) * D])
    cw = f(inp["lru_conv_w"])[0]
    for j in range(4):
        colp[:, CP_CW + j:CP_CW + 32:4] = colv(cw[j])
    colp[:, CP_CB:CP_CB + 8] = colv(f(inp["lru_conv_b"])[0])
    colp[:, CP_BA:CP_BA + 8] = colv(f(inp["lru_b_a"])[0])
    colp[:, CP_BX:CP_BX + 8] = colv(f(inp["lru_b_x"])[0])
    colp[:, CP_LAM:CP_LAM + 8] = colv(f(inp["lru_lambda"])[0])
    colp[:, CP_SINK:CP_SINK + 8] = np.broadcast_to(f(inp["attn_sinks"])[0][None, :], (P, 8))
    eye = np.eye(P, dtype=np.float32)
    sidx = np.arange(P)
    cmask = (sidx[:, None] <= sidx[None, :]).astype(np.float32)
    shared["cmat"] = np.ascontiguousarray(np.concatenate([eye, eye[::-1], cmask], axis=1))
    rel = np.arange(383) - 127
    onehot = np.zeros((32, 383), np.float32)
    onehot[_t5_bucket(rel), np.arange(383)] = 1.0
    valid = ((rel >= 0) & (rel < 128)).astype(np.float32)
    shared["rb"] = np.ascontiguousarray(np.concatenate([f(inp["rel_bias"]), onehot], axis=1))
    v8 = np.zeros((8, 384), np.float32)
    v8[:, 0:383] = valid[None, :]
    v8[:, 383] = f(inp["attn_b_f"])[0]
    shared["v8"] = v8
    x = f(inp["x"])
    c = f(inp["c"])
    maps = []
    for core in range(ncores):
        b0 = core * nseq_core
        m = dict(shared)
        m["x"] = x[b0:b0 + nseq_core]
        cp_ = colp.copy()
        ct = c[b0:b0 + nseq_core].reshape(nseq_core, KC, P).transpose(2, 1, 0)
        ctf = np.zeros((P, KC, 4), np.float32)
        ctf[:, :, 0:nseq_core] = ct
        cp_[:, CP_CT:CP_CT + KC * 4] = ctf.reshape(P, KC * 4)
        m["colp"] = cp_
        maps.append(m)
    return maps


def kernel(**inputs):
    maps = _host_layout(inputs)
    nc = build()
    res = run_bass_kernel_spmd(nc, maps, core_ids=list(range(NCORES)))
    out = np.concatenate([np.asarray(r["out"], dtype=np.float32) for r in res.results], axis=0)
    return out
```

```python
import math
from contextlib import ExitStack

import numpy as np
import concourse.bass as bass
import concourse.mybir as mybir
from concourse.bass_utils import run_bass_kernel_spmd

F32 = mybir.dt.float32
BF16 = mybir.dt.bfloat16
AF = mybir.ActivationFunctionType
ALU = mybir.AluOpType

NCORES = 8
P = 128
D = 1024
KC = 8
S = 2048
G = 256
GT = G // P
NB = S // P
NGRP = S // G
SEQ_PER_CORE = 4
EPS = 1e-6
VW = 66

AQ, AK, BQ, BK, FL, AV, BV, GA, C0 = 0, 512, 640, 1152, 1664, 1672, 1800, 2312, 3336
CP_NG, CP_AB, CP_CW, CP_CB, CP_BA, CP_BX, CP_LAM, CP_SINK, CP_CT, CP_N = 0, 16, 64, 96, 104, 112, 120, 128, 136, 168

SAME_ENGINE_SYNC = True
RSTD_POW = True
EARLY_L1_NORM = False


class _Rec:
    def __init__(self):
        self.call = None

    def __getattr__(self, name):
        def f(*a, **k):
            self.call = (name, a, k)
            return self
        return f


class _XS:
    def __init__(self, bufs):
        self.bufs = bufs
        self.cur = 0

    def __getitem__(self, idx):
        return self.bufs[self.cur][idx]


class Sched:
    STREAMS = ("pe", "act", "dve", "pool", "sp")

    def __init__(self, nc, es):
        self.nc = nc
        self.es = es
        self.sems = {}
        self.cnt = {}
        self.ops = {s: [] for s in self.STREAMS}
        self.lastw = {}
        self.rds = {}
        self.excl = set()
        for s in ("pe", "act", "dve", "pool"):
            self.new_sem("E_" + s)

    def new_sem(self, name):
        self.sems[name] = self.es.enter_context(self.nc.semaphore(name))
        self.cnt[name] = 0
        return name

    def add(self, stream, fn, r=(), w=(), dsem=None, extra=()):
        deps = set(extra)
        own = dsem if dsem is not None else "E_" + stream
        for k in r:
            if k in self.lastw:
                deps.add(self.lastw[k])
            if k in self.excl:
                deps.update(d for d in self.rds.get(k, ()) if d[0] != own)
        for k in w:
            if k in self.lastw:
                deps.add(self.lastw[k])
            deps.update(self.rds.get(k, ()))
        if dsem is None:
            sname = "E_" + stream
            self.cnt[sname] += 1
            inc = 1
        else:
            sname = dsem
            self.cnt[sname] += 16
            inc = 16
        done = (sname, self.cnt[sname])
        for k in r:
            self.rds.setdefault(k, []).append(done)
        for k in w:
            self.lastw[k] = done
            self.rds[k] = []
        rec = _Rec()
        fn(rec)
        assert rec.call is not None
        self.ops[stream].append((deps, rec.call, sname, inc))
        return done

    def emit(self, stream, eng):
        known = {}
        own = "E_" + stream
        for deps, fn, sname, inc in self.ops[stream]:
            best = {}
            for (s, v) in deps:
                if v > best.get(s, 0):
                    best[s] = v
            for s, v in best.items():
                if s == own and (stream == "pe" or not SAME_ENGINE_SYNC):
                    continue
                if known.get(s, 0) >= v:
                    continue
                eng.wait_ge(self.sems[s], v)
                known[s] = v
            name, a, k = fn
            ins = getattr(eng, name)(*a, **k)
            ins.then_inc(self.sems[sname], inc)

    def run(self, final_conds=()):
        nc = self.nc
        with nc.Block() as block:
            @block.tensor
            def _(e):
                self.emit("pe", e)

            @block.scalar
            def _(e):
                self.emit("act", e)

            @block.vector
            def _(e):
                self.emit("dve", e)

            @block.gpsimd
            def _(e):
                self.emit("pool", e)

            @block.sync
            def _(e):
                self.emit("sp", e)
                best = {}
                for (s, v) in final_conds:
                    best[s] = max(best.get(s, 0), v)
                for s, v in best.items():
                    e.wait_ge(self.sems[s], v)


def build(nseq=SEQ_PER_CORE, ngrp=NGRP, dbg=False, stop=None):
    nc = bass.Bass("TRN2", target_bir_lowering=False, dynamic_dma_scratch_size=2048)
    x_d = nc.dram_tensor("x", [nseq, S, D], F32, kind="ExternalInput").ap()
    out_d = nc.dram_tensor("out", [nseq, S, D], F32, kind="ExternalOutput").ap()
    w0_d = nc.dram_tensor("w0", [P, KC, C0], F32, kind="ExternalInput").ap()
    w1_d = nc.dram_tensor("w1", [P, KC, D], F32, kind="ExternalInput").ap()
    w2_d = nc.dram_tensor("w2", [P, KC, 2 * D], F32, kind="ExternalInput").ap()
    w3_d = nc.dram_tensor("w3", [P, KC, D], F32, kind="ExternalInput").ap()
    w4_d = nc.dram_tensor("w4", [P, 2, KC, P], F32, kind="ExternalInput").ap()
    ada_d = nc.dram_tensor("ada_w", [2, D, 3 * D], F32, kind="ExternalInput").ap()
    colp_d = nc.dram_tensor("colp", [P, CP_N], F32, kind="ExternalInput").ap()
    cmat_d = nc.dram_tensor("cmat", [P, 3 * P], F32, kind="ExternalInput").ap()
    rb_d = nc.dram_tensor("rb", [32, 8 + 383], F32, kind="ExternalInput").ap()
    v8_d = nc.dram_tensor("v8", [8, 384], F32, kind="ExternalInput").ap()
    fg_d = nc.dram_tensor("final_g", [1, D], F32, kind="ExternalInput")
    gscr = nc.dram_tensor("gscr", [4, 2, D], F32, kind="Internal")
    ebscr = nc.dram_tensor("ebscr", [8, 384], F32, kind="Internal")
    if dbg:
        dbg_d = nc.dram_tensor("dbg", [nseq, S, D], F32, kind="ExternalOutput").ap()

    with ExitStack() as es:
        SC = Sched(nc, es)
        add = SC.add

        def sb(name, shape, dt):
            return es.enter_context(nc.sbuf_tensor("s_" + name, shape, dt))

        W0 = sb("W0", [P, KC, C0], BF16)
        W1 = sb("W1", [P, KC, D], BF16)
        W2 = sb("W2", [P, KC, 2 * D], BF16)
        W3 = sb("W3", [P, KC, D], BF16)
        W4 = sb("W4", [P, 2, KC, P], BF16)
        bKT = sb("bKT", [P, 4, S], BF16)
        bV = sb("bV", [P, NB, 8, VW], BF16)
        aKT = sb("aKT", [P, P + G], BF16)
        aV = sb("aV", [P, GT + 1, 2, VW], BF16)
        xs = _XS([sb("xs0", [P, GT, D], F32), sb("xs1", [P, GT, D], F32)])
        xn = sb("xn", [P, D], F32)
        bc = sb("bc", [P, D], F32)
        hT = sb("hT", [P, KC, G], BF16)
        yT = sb("yT", [P, KC, G], BF16)
        aQT = sb("aQT", [P, 4, G], BF16)
        bQT = sb("bQT", [P, 4, G], BF16)
        th = sb("th", [P, D], F32)
        osb = sb("osb", [P, 512], F32)
        ysb = sb("ysb", [P, D], BF16)
        NPB = 8
        pb = sb("pb", [P, NPB, P], BF16)
        swt = sb("swt", [P, 2, 2, P], F32)
        pa = sb("pa", [P, 2, 2, P], BF16)
        EB = sb("EB", [P, 8, 2, P], BF16)
        colp = sb("colp", [P, CP_N], F32)
        cmat = sb("cmat", [P, 3 * P], F32)
        identb = sb("identb", [P, P], BF16)
        cmaskb = sb("cmaskb", [P, P], BF16)
        rbs = ysb[:].bitcast(F32)[0:32, 0:8 + 383]
        RBS_K = [("ysb", 0), ("ysb", 1)]
        v8 = pb[:].rearrange("p a b -> p (a b)").bitcast(F32)[0:8, 0:384]
        V8_K = [("pb", i) for i in range(NPB)]
        gtab = swt[:].rearrange("p a b c -> p (a b c)")[0:8, 0:384]
        GT_K = [("swt", 0), ("swt", 1)]
        scT = sb("scT", [P, KC, 4], F32)
        modT = sb("modT", [P, 48, 4], F32)
        gsT = sb("gsT", [P, 2, 8, 4], F32)
        small = sb("small", [P, 64], F32)
        lrp = sb("lrp", [P, 5, 8], F32)
        spt = sb("spt", [P, 5, 8], F32)
        FnegT = sb("FnegT", [P, NB, 8], F32)
        FendBC = sb("FendBC", [P, GT, 8], F32)
        l1b = sb("l1b", [P, 5 * G], F32)
        xrb1 = sb("xrb1", [P, G + 4], F32)
        xcb1 = sb("xcb1", [P, G], BF16)
        fdiag = sb("fdiag", [8, GT, 8], F32)
        ones8 = sb("ones8", [8, G], F32)
        fcar = sb("fcar", [8, 2], F32)
        den = sb("den", [P, 16], F32)
        tails = sb("tails", [P, KC, 4], F32)
        state = sb("state", [P, KC], F32)
        xrb = sb("xrb", [P, G + 4], F32)
        xcb = sb("xcb", [P, G], BF16)
        L1S = [
            dict(xc=osb[:, 0:G], xcK=("osb", 0), tha=th[:, 0:G], thaK=("th", 0), thx=th[:, G:2 * G], thxK=("th", 0),
                 ab=th[:, 2 * G:3 * G], abK=("th", 1), sg=th[:, 3 * G:4 * G], sgK=("th", 1),
                 xrb=xrb[:], xrbK="xrb0", xrtK="xrbt0", xcb=xcb[:], xcbK="xcb0", banks=(2, 3)),
            dict(xc=l1b[:, 0:G], xcK="l1xc", tha=l1b[:, G:2 * G], thaK="l1tha", thx=l1b[:, 2 * G:3 * G], thxK="l1thx",
                 ab=l1b[:, 3 * G:4 * G], abK="l1ab", sg=l1b[:, 4 * G:5 * G], sgK="l1sg",
                 xrb=xrb1[:], xrbK="xrb1", xrtK="xrbt1", xcb=xcb1[:], xcbK="xcb1", banks=(6, 7)),
        ]
        frow = [th[0:8, 0:G], th[0:8, G:2 * G], th[0:8, 2 * G:3 * G], th[0:8, 3 * G:4 * G], osb[0:8, 0:G]]
        FRK = [("th", 0), ("th", 0), ("th", 1), ("th", 1), ("osb", 0)]

        ps = [es.enter_context(nc.psum_tensor("psum%d" % i, [P, 512], F32)) for i in range(8)]
        PSK = ["ps%d" % i for i in range(8)]
        SC.excl.update(PSK)

        ident = cmat[:, 0:P]
        BT = cmat[:, P:3 * P].rearrange("p (j n h) -> p j n h", n=GT, h=8)
        jrev = cmat[:, P:2 * P]
        cmaskf = cmat[:, 2 * P:3 * P]

        def cp(off, n):
            return colp[:, off:off + n]

        for nm, dst, src, wk in (("colp", colp[:], colp_d, ["colp"]), ("cmat", cmat[:], cmat_d, ["cmat", "cmat_j"]), ("rbs", rbs, rb_d, RBS_K), ("v8", v8, v8_d, V8_K)):
            sem = SC.new_sem("ld_" + nm)
            add("sp", lambda e, dst=dst, src=src: e.dma_start(out=dst, in_=src), w=wk, dsem=sem)

        NLANE = 6
        for i in range(NLANE):
            SC.new_sem("wl%d" % i)
        WK = {nm: [] for nm in ("W0", "W1", "W2", "W3", "W4")}
        lane_last = [None] * NLANE
        wq = [0]

        def wload(nm, dst, src):
            i = wq[0]
            wq[0] += 1
            key = (nm, len(WK[nm]))
            WK[nm].append(key)
            ln = i % NLANE
            ex = [lane_last[ln]] if lane_last[ln] is not None else []
            lane_last[ln] = add("pool", lambda e: e.dma_start(out=dst, in_=src), w=[key], dsem="wl%d" % ln, extra=ex)

        for kc in range(KC):
            for (c0, c1) in ((0, 1668), (1668, C0)):
                wload("W0", W0[:, kc, c0:c1], w0_d[:, kc, c0:c1])
        for kc in range(0, KC, 2):
            wload("W1", W1[:, kc:kc + 2, :], w1_d[:, kc:kc + 2, :])
        for kc in range(KC):
            wload("W2", W2[:, kc, :], w2_d[:, kc, :])
        for kc in range(0, KC, 2):
            wload("W3", W3[:, kc:kc + 2, :], w3_d[:, kc:kc + 2, :])
        for i in range(2):
            for k2 in range(0, KC, 2):
                wload("W4", W4[:, i, k2:k2 + 2, :], w4_d[:, i, k2:k2 + 2, :])

        add("dve", lambda e: e.tensor_copy(out=identb[:], in_=ident), r=["cmat"], w=["identb"])
        add("dve", lambda e: e.tensor_copy(out=cmaskb[:], in_=cmaskf), r=["cmat_j"], w=["cmaskb"])
        add("dve", lambda e: e.memset(ones8[:], 1.0), w=["ones8"])
        add("dve", lambda e: e.memset(small[:, 0:1], -0.5), w=["negh"])
        add("dve", lambda e: e.memset(bV[:, :, :, 64:VW], 1.0), w=["bVones"])
        add("dve", lambda e: e.memset(aV[:, :, :, 64:VW], 1.0), w=["aVones"])
        negh = small[:, 0:1]

        cT = cp(CP_CT, 32)
        scf = scT[:].rearrange("p k b -> p (k b)")
        add("act", lambda e: e.activation(out=scf, in_=cT, func=AF.Tanh, scale=0.5), r=["colp"], w=["scT"])
        add("dve", lambda e: e.scalar_tensor_tensor(out=scf, in0=scf, scalar=1.0, in1=cT, op0=ALU.add, op1=ALU.mult), r=["scT", "colp"], w=["scT"])
        add("dve", lambda e: e.tensor_scalar(out=scf, in0=scf, scalar1=0.5, scalar2=None, op0=ALU.mult), r=["scT"], w=["scT"])

        stag = [(xn[:].rearrange("p (k j) -> p k j", k=KC), ("xn", 0)), (bc[:].rearrange("p (k j) -> p k j", k=KC), "bc"),
                (th[:].rearrange("p (k j) -> p k j", k=KC), ("th", 0)), (l1b[:, 0:D].rearrange("p (k j) -> p k j", k=KC), "l1xc")]
        NST = len(stag)
        for i in range(NST):
            SC.new_sem("ad%d" % i)
        idx = 0
        for l in range(2):
            for part in range(3):
                for c in range(KC):
                    slot, skey = stag[idx % NST]
                    col0 = part * D + c * P
                    src = ada_d[l, :, col0:col0 + P].rearrange("(k p) j -> p k j", p=P)
                    add("sp", lambda e, slot=slot, src=src: e.dma_start(out=slot, in_=src), w=[skey], dsem="ad%d" % (idx % NST))
                    for kc in range(KC):
                        add("pe", lambda e, slot=slot, kc=kc, idx=idx: e.matmul(ps[2][:, idx * 4:idx * 4 + 4], lhsT=slot[:, kc, :], rhs=scT[:, kc, :], start=(kc == 0), stop=(kc == KC - 1)),
                            r=[skey, "scT"], w=[PSK[2]])
                    idx += 1
        add("dve", lambda e: e.tensor_tensor(out=modT[:], in0=ps[2][:, 0:192].rearrange("p (i b) -> p i b", b=4),
                                             in1=cp(CP_AB, 48).unsqueeze(2).to_broadcast([P, 48, 4]), op=ALU.add),
            r=[PSK[2], "colp"], w=["modT"])
        for l in range(2):
            sc_l = modT[:, l * 24 + 8:l * 24 + 16, :]
            add("dve", lambda e, l=l, sc_l=sc_l: e.scalar_tensor_tensor(out=gsT[:, l, :, :], in0=sc_l, scalar=1.0,
                                                                    in1=cp(CP_NG + l * 8, 8).unsqueeze(2).to_broadcast([P, 8, 4]),
                                                                    op0=ALU.add, op1=ALU.mult), r=["modT", "colp"], w=["gsT"])
        SC.new_sem("gs_w")
        for l in range(2):
            for bb in range(4):
                dst = bass.AP(tensor=gscr, offset=bb * 2 * D + l * D, ap=[[1, P], [P, KC]])
                add("sp", lambda e, l=l, bb=bb, dst=dst: e.dma_start(out=dst, in_=modT[:, l * 24 + 16:l * 24 + 24, bb], allow_slow_non_contiguous=True),
                    r=["modT"], w=["gscr"], dsem="gs_w")

        def shiftc(l, c, b):
            return modT[:, l * 24 + c, b:b + 1]

        def gsc(l, c, b):
            return gsT[:, l, c, b:b + 1]

        def softplus_from_e(e_ap, eK, out_ap, oK, t_z, zK, t_z2, z2K, t_ln, lnK):
            t_p = out_ap
            add("act", lambda e: e.activation(out=t_ln, in_=e_ap, func=AF.Ln, bias=1.0), r=[eK], w=[lnK])
            add("dve", lambda e: e.tensor_scalar(out=t_z, in0=e_ap, scalar1=2.0, scalar2=None, op0=ALU.add), r=[eK], w=[zK])
            add("dve", lambda e: e.reciprocal(out=t_z, in_=t_z), r=[zK], w=[zK])
            add("dve", lambda e: e.tensor_tensor(out=t_z, in0=t_z, in1=e_ap, op=ALU.mult), r=[zK, eK], w=[zK])
            add("dve", lambda e: e.tensor_tensor(out=t_z2, in0=t_z, in1=t_z, op=ALU.mult), r=[zK], w=[z2K])
            add("dve", lambda e: e.tensor_scalar(out=t_p, in0=t_z2, scalar1=1.0 / 9, scalar2=1.0 / 7, op0=ALU.mult, op1=ALU.add), r=[z2K], w=[oK])
            for cst in (1.0 / 5, 1.0 / 3, 1.0):
                add("dve", lambda e: e.tensor_tensor(out=t_p, in0=t_p, in1=t_z2, op=ALU.mult), r=[oK, z2K], w=[oK])
                add("dve", lambda e, cst=cst: e.tensor_scalar(out=t_p, in0=t_p, scalar1=cst, scalar2=None, op0=ALU.add), r=[oK], w=[oK])
            add("dve", lambda e: e.scalar_tensor_tensor(out=t_p, in0=t_z, scalar=2.0, in1=t_p, op0=ALU.mult, op1=ALU.mult), r=[oK, zK], w=[oK])
            add("dve", lambda e: e.tensor_scalar(out=t_z2, in0=e_ap, scalar1=0.5, scalar2=None, op0=ALU.is_lt), r=[eK, oK], w=[z2K])
            add("dve", lambda e: e.tensor_tensor(out=t_p, in0=t_p, in1=t_ln, op=ALU.subtract), r=[oK, lnK], w=[oK])
            add("dve", lambda e: e.tensor_tensor(out=t_p, in0=t_p, in1=t_z2, op=ALU.mult), r=[oK, z2K], w=[oK])
            add("dve", lambda e: e.tensor_tensor(out=t_p, in0=t_p, in1=t_ln, op=ALU.add), r=[oK, lnK], w=[oK])

        add("act", lambda e: e.activation(out=spt[:, 3, :], in_=cp(CP_LAM, 8), func=AF.Exp, scale=-1.0), r=["colp"], w=["spt_e"])
        softplus_from_e(spt[:, 3, :], "spt_e", spt[:, 2, :], "sptout", spt[:, 0, :], "sptz", spt[:, 1, :], "sptz2", spt[:, 4, :], "sptln")
        add("dve", lambda e: e.tensor_scalar(out=lrp[:, 0, :], in0=spt[:, 2, :], scalar1=-8.0, scalar2=None, op0=ALU.mult), r=["sptout"], w=["lrp0"])
        add("dve", lambda e: e.tensor_scalar(out=lrp[:, 1, :], in0=spt[:, 2, :], scalar1=-4.0, scalar2=None, op0=ALU.mult), r=["sptout"], w=["lrp1"])
        add("dve", lambda e: e.tensor_scalar(out=lrp[:, 2, :], in0=cp(CP_BA, 8), scalar1=0.5, scalar2=None, op0=ALU.mult), r=["colp"], w=["lrp2"])
        add("dve", lambda e: e.tensor_scalar(out=lrp[:, 3, :], in0=cp(CP_BX, 8), scalar1=0.5, scalar2=None, op0=ALU.mult), r=["colp"], w=["lrp3"])
        add("act", lambda e: e.activation(out=lrp[:, 4, :], in_=cp(CP_SINK, 8), func=AF.Exp), r=["colp"], w=["lrp4"])
        LRK = ["lrp0", "lrp1", "lrp2", "lrp3"]
        add("dve", lambda e: e.tensor_scalar(out=fcar[:, 1:2], in0=v8[:, 383:384], scalar1=-1.0, scalar2=None, op0=ALU.mult), r=V8_K, w=["nbf"])
        nbf = fcar[:, 1:2]

        add("pe", lambda e: e.matmul(ps[3][0:8, 0:383], lhsT=rbs[:, 0:8], rhs=rbs[:, 8:391], start=True, stop=True), r=RBS_K, w=[PSK[3]])
        add("act", lambda e: e.activation(out=gtab[:, 0:383], in_=ps[3][0:8, 0:383], func=AF.Exp), r=[PSK[3]], w=GT_K)
        add("dve", lambda e: e.memset(gtab[:, 383:384], 0.0), w=GT_K)
        add("dve", lambda e: e.tensor_tensor(out=gtab[:, 0:383], in0=gtab[:, 0:383], in1=v8[:, 0:383], op=ALU.mult), r=GT_K + V8_K, w=GT_K)
        SC.new_sem("eb_w")
        SC.new_sem("eb_r")
        add("sp", lambda e: e.dma_start(out=ebscr.ap(), in_=gtab), r=GT_K, w=["ebscr"], dsem="eb_w")
        hbuf = th[:].rearrange("p (h t) -> p h t", h=8)
        for kb in range(2):
            src = bass.AP(tensor=ebscr, offset=(1 - kb) * P, ap=[[1, P], [384, 8], [1, P]])
            add("sp", lambda e, src=src: e.dma_start(out=hbuf, in_=src), r=["ebscr"], w=[("th", 0), ("th", 1)], dsem="eb_r")
            for q in range(2):
                add("pe", lambda e, q=q: e.matmul(ps[4 + q][:, :], lhsT=jrev, rhs=th[:, q * 512:(q + 1) * 512], start=True, stop=True),
                    r=[("th", 0), ("th", 1), "cmat_j"], w=[PSK[4 + q]])
                add("dve", lambda e, q=q, kb=kb: e.tensor_copy(out=EB[:, q * 4:(q + 1) * 4, kb, :], in_=ps[4 + q][:, :].rearrange("p (h t) -> p h t", h=4)),
                    r=[PSK[4 + q]], w=["EB"])

        for pq in ("00", "01", "10", "11"):
            SC.new_sem("xl" + pq)
        SC.new_sem("xst")
        SC.new_sem("xst0")
        SC.new_sem("xst1")
        SC.new_sem("bcs")
        if dbg:
            SC.new_sem("dbgs0")
            SC.new_sem("dbgs1")
        evq = [0]

        def xk(ti):
            return ("x", xs.cur, ti)

        def rmsnorm_rstd(ti, rcol):
            add("act", lambda e: e.activation(out=ysb[:], in_=xs[:, ti, :], func=AF.Square, accum_out=small[:, rcol:rcol + 1]),
                r=[xk(ti)], w=[("ysb", 0), ("ysb", 1), ("sm", rcol)])
            if RSTD_POW:
                add("dve", lambda e: e.tensor_scalar(out=small[:, rcol:rcol + 1], in0=small[:, rcol:rcol + 1], scalar1=1.0 / D, scalar2=EPS, op0=ALU.mult, op1=ALU.add),
                    r=[("sm", rcol)], w=[("sm", rcol)])
                add("pool", lambda e: e.tensor_tensor(out=small[:, rcol:rcol + 1], in0=small[:, rcol:rcol + 1], in1=negh, op=ALU.pow),
                    r=[("sm", rcol), "negh"], w=[("sm", rcol)])
            else:
                add("act", lambda e: e.activation(out=small[:, rcol:rcol + 1], in_=small[:, rcol:rcol + 1], func=AF.Ln, scale=1.0 / D, bias=EPS),
                    r=[("sm", rcol)], w=[("sm", rcol)])
                add("act", lambda e: e.activation(out=small[:, rcol:rcol + 1], in_=small[:, rcol:rcol + 1], func=AF.Exp, scale=-0.5),
                    r=[("sm", rcol)], w=[("sm", rcol)])

        def norm_to_hT(l, b, skip_stats=False):
            for ti in range(GT):
                if not skip_stats:
                    rmsnorm_rstd(ti, 2 + ti)
                if stop == "na":
                    continue
                scr, scrK = (xn, [("xn", 0), ("xn", 1)]) if ti == 0 else (bc, ["bc"])
                add("dve", lambda e, ti=ti, scr=scr: e.tensor_scalar(out=scr[:], in0=xs[:, ti, :], scalar1=small[:, 2 + ti:3 + ti], scalar2=None, op0=ALU.mult),
                    r=[xk(ti), ("sm", 2 + ti)], w=scrK)
                if stop == "nb":
                    continue
                for c in range(KC):
                    add("pe", lambda e, ti=ti, c=c, scr=scr: e.transpose(ps[c // 2][:, (c % 2) * G + ti * P:(c % 2) * G + (ti + 1) * P], scr[:, c * P:(c + 1) * P], ident),
                        r=scrK + ["cmat"], w=[PSK[c // 2]])
            if stop in ("na", "nb", "nc"):
                return
            for c in range(KC):
                src = ps[c // 2][:, (c % 2) * G:(c % 2 + 1) * G]
                if (c // 2) % 2 == 0:
                    add("act", lambda e, c=c, src=src: e.activation(out=hT[:, c, :], in_=src, func=AF.Identity, scale=gsc(l, c, b), bias=shiftc(l, c, b)),
                        r=[PSK[c // 2], "gsT", "modT"], w=[("hT", c)])
                else:
                    add("dve", lambda e, c=c, src=src: e.tensor_scalar(out=hT[:, c, :], in0=src, scalar1=gsc(l, c, b), scalar2=shiftc(l, c, b), op0=ALU.mult, op1=ALU.add),
                        r=[PSK[c // 2], "gsT", "modT"], w=[("hT", c)])

        def norm_tile_to_hT(l, b, ti, bank):
            rmsnorm_rstd(ti, 2 + ti)
            add("dve", lambda e: e.tensor_scalar(out=xn[:], in0=xs[:, ti, :], scalar1=small[:, 2 + ti:3 + ti], scalar2=None, op0=ALU.mult),
                r=[xk(ti), ("sm", 2 + ti)], w=[("xn", 0), ("xn", 1)])
            for half in range(2):
                for cc in range(4):
                    c = half * 4 + cc
                    add("pe", lambda e, c=c, cc=cc: e.transpose(ps[bank][:, cc * P:(cc + 1) * P], xn[:, c * P:(c + 1) * P], ident),
                        r=[("xn", 0), ("xn", 1), "cmat"], w=[PSK[bank]])
                for cc in range(4):
                    c = half * 4 + cc
                    src = ps[bank][:, cc * P:(cc + 1) * P]
                    if half == 0:
                        add("act", lambda e, c=c, src=src: e.activation(out=hT[:, c, ti * P:(ti + 1) * P], in_=src, func=AF.Identity, scale=gsc(l, c, b), bias=shiftc(l, c, b)),
                            r=[PSK[bank], "gsT", "modT"], w=[("hT", c)])
                    else:
                        add("dve", lambda e, c=c, src=src: e.tensor_scalar(out=hT[:, c, ti * P:(ti + 1) * P], in0=src, scalar1=gsc(l, c, b), scalar2=shiftc(l, c, b), op0=ALU.mult, op1=ALU.add),
                            r=[PSK[bank], "gsT", "modT"], w=[("hT", c)])

        HTK = [("hT", c) for c in range(KC)]

        def proj_fm(Wt, wkey, col0, M, evac):
            bi = 2 + (evq[0] % 2)
            evq[0] += 1
            for kc in range(KC):
                add("pe", lambda e, kc=kc, bi=bi: e.matmul(ps[bi][0:M, 0:G], lhsT=Wt[:, kc, col0:col0 + M], rhs=hT[:, kc, :], start=(kc == 0), stop=(kc == KC - 1)),
                    r=WK[wkey] + HTK, w=[PSK[bi]])
            evac(ps[bi][0:M, 0:G], PSK[bi])

        def load_bc(src_ap, rkeys):
            add("sp", lambda e: e.dma_start(out=bc[:], in_=src_ap), r=rkeys, w=["bc"], dsem="bcs")

        def residual_from(ti, hf, acc_bank):
            add("dve", lambda e: e.tensor_tensor(out=xn[:, hf * 512:(hf + 1) * 512], in0=ps[acc_bank][:, :], in1=bc[:, hf * 512:(hf + 1) * 512], op=ALU.mult),
                r=[PSK[acc_bank], "bc"], w=[("xn", hf)])
            add("dve", lambda e: e.tensor_tensor(out=xs[:, ti, hf * 512:(hf + 1) * 512], in0=xs[:, ti, hf * 512:(hf + 1) * 512], in1=xn[:, hf * 512:(hf + 1) * 512], op=ALU.add),
                r=[("xn", hf), xk(ti)], w=[xk(ti)])

        def out_proj_residual(Wt, wkey, srcT, skey_fn, ti):
            for hf in range(2):
                for kc in range(KC):
                    add("pe", lambda e, kc=kc, hf=hf: e.matmul(ps[hf][:, :], lhsT=srcT[:, kc, ti * P:(ti + 1) * P], rhs=Wt[:, kc, hf * 512:(hf + 1) * 512],
                                                               start=(kc == 0), stop=(kc == KC - 1)),
                        r=WK[wkey] + skey_fn(kc), w=[PSK[hf]])
                residual_from(ti, hf, hf)

        sbank = [0]
        pbq = [0]
        swq = [0]

        last_store = []
        pre_stats = [False]
        for b in range(nseq):
            add("pool", lambda e: e.memset(tails[:], 0.0), w=["tails"])
            add("pool", lambda e: e.memset(state[:], 0.0), w=["state"])
            add("pool", lambda e: e.memset(fcar[:, 0:1], 0.0), w=["fcar"])
            for g in range(ngrp):
                t0 = g * G
                gi = b * ngrp + g
                xs.cur = gi % 2

                def load_x(gj):
                    bj, gg = divmod(gj, ngrp)
                    keep = xs.cur
                    xs.cur = gj % 2
                    for ti in range(GT):
                        add("sp", lambda e, ti=ti: e.dma_start(out=xs[:, ti, :], in_=x_d[bj, gg * G + ti * P:gg * G + (ti + 1) * P, :]), w=[xk(ti)], dsem="xl%d%d" % (gj % 2, ti))
                    xs.cur = keep
                if gi == 0:
                    load_x(0)
                if gi + 1 < nseq * ngrp:
                    load_x(gi + 1)

                def store_x():
                    for ti in range(GT):
                        d = add("sp", lambda e, ti=ti: e.dma_start(out=out_d[b, t0 + ti * P:t0 + (ti + 1) * P, :], in_=xs[:, ti, :]), r=[xk(ti)], dsem="xst")
                    return [d]
                if stop == "pro":
                    last_store = store_x()
                    continue
                norm_to_hT(0, b, skip_stats=pre_stats[0])
                pre_stats[0] = False
                if stop in ("n0", "na", "nb", "nc"):
                    last_store = store_x()
                    continue
                load_bc(gscr.ap()[b, 0, :].partition_broadcast(P), ["gscr"])
                if stop == "n1":
                    last_store = store_x()
                    continue
                if g > 0:
                    add("pool", lambda e: e.tensor_copy(out=aKT[:, 0:P], in_=aKT[:, G:G + P]), r=["aK"], w=["aK"])
                    add("pool", lambda e: e.tensor_copy(out=aV[:, 0, :, 0:64], in_=aV[:, GT, :, 0:64]), r=[("aV", GT)], w=[("aV", 0)])
                proj_fm(W0, "W0", FL, 8, lambda src, k: add("act", lambda e: e.activation(out=frow[3], in_=src, func=AF.Exp, scale=-1.0, bias=nbf), r=[k, "nbf"], w=[FRK[3]]))
                softplus_from_e(frow[3], FRK[3], frow[2], FRK[2], frow[0], FRK[0], frow[1], FRK[1], frow[4], FRK[4])
                add("dve", lambda e: e.tensor_tensor_scan(out=frow[4], data0=ones8[:], data1=frow[2], initial=fcar[:, 0:1], op0=ALU.mult, op1=ALU.add),
                    r=[FRK[2], "ones8", "fcar"], w=[FRK[4]])
                add("dve", lambda e: e.tensor_copy(out=fcar[:, 0:1], in_=frow[4][:, G - 1:G]), r=[FRK[4]], w=["fcar"])
                for c in range(4):
                    proj_fm(W0, "W0", AQ + c * P, P, lambda src, k, c=c: add("act", lambda e: e.copy(out=aQT[:, c, :], in_=src), r=[k], w=[("aQ", c)]))
                proj_fm(W0, "W0", AK, P, lambda src, k: add("act", lambda e: e.copy(out=aKT[:, P:P + G], in_=src), r=[k], w=["aK"]))
                for c in range(4):
                    proj_fm(W0, "W0", BQ + c * P, P, lambda src, k, c=c: add("act", lambda e: e.copy(out=bQT[:, c, :], in_=src), r=[k], w=[("bQ", c)]))
                for c in range(4):
                    proj_fm(W0, "W0", BK + c * P, P, lambda src, k, c=c: add("act", lambda e: e.copy(out=bKT[:, c, t0:t0 + G], in_=src), r=[k], w=[("bK", c, g)]))
                for nl in range(GT):
                    add("pe", lambda e, nl=nl: e.transpose(ps[7][:, 300 + nl * 8:300 + nl * 8 + 8], frow[4][:, nl * P:(nl + 1) * P], ident[0:8, 0:8]),
                        r=[FRK[4], "cmat"], w=[PSK[7]])
                    add("dve", lambda e, nl=nl: e.tensor_scalar(out=fdiag[:, nl, :], in0=ident[0:8, 0:8], scalar1=frow[4][:, (nl + 1) * P - 1:(nl + 1) * P], scalar2=None, op0=ALU.mult),
                        r=[FRK[4], "cmat"], w=["fdiag"])
                add("dve", lambda e: e.tensor_copy(out=FnegT[:, g * GT:(g + 1) * GT, :], in_=ps[7][:, 300:300 + GT * 8].rearrange("p (n h) -> p n h", h=8)),
                    r=[PSK[7]], w=["FnegT"])
                add("pe", lambda e: e.matmul(ps[7][:, 400:400 + GT * 8], lhsT=ones8[:, 0:P], rhs=fdiag[:].rearrange("k n h -> k (n h)"), start=True, stop=True),
                    r=["ones8", "fdiag"], w=[PSK[7]])
                add("dve", lambda e: e.tensor_copy(out=FendBC[:], in_=ps[7][:, 400:400 + GT * 8].rearrange("p (n h) -> p n h", h=8)), r=[PSK[7]], w=["FendBC"])
                nj = (g + 1) * GT
                add("dve", lambda e, nj=nj: e.tensor_tensor(out=BT[:, 0:nj, :, :], in0=FnegT[:, 0:nj, :].unsqueeze(2).to_broadcast([P, nj, GT, 8]),
                                                            in1=FendBC[:].unsqueeze(1).to_broadcast([P, nj, GT, 8]), op=ALU.subtract),
                    r=["FnegT", "FendBC"], w=["BT", "cmat_j"])
                for ti in range(GT):
                    blk = g * GT + ti
                    bi = 2 + (evq[0] % 2)
                    evq[0] += 1
                    for kc in range(KC):
                        add("pe", lambda e, kc=kc, bi=bi, ti=ti: e.matmul(ps[bi][:, 0:P], lhsT=hT[:, kc, ti * P:(ti + 1) * P], rhs=W0[:, kc, AV:AV + P], start=(kc == 0), stop=(kc == KC - 1)),
                            r=WK["W0"] + HTK, w=[PSK[bi]])
                    add("act", lambda e, bi=bi, ti=ti: e.copy(out=aV[:, 1 + ti, :, 0:64], in_=ps[bi][:, 0:P].rearrange("p (g d) -> p g d", g=2)), r=[PSK[bi]], w=[("aV", 1 + ti)])
                    bi = 2 + (evq[0] % 2)
                    evq[0] += 1
                    for kc in range(KC):
                        add("pe", lambda e, kc=kc, bi=bi, ti=ti: e.matmul(ps[bi][:, :], lhsT=hT[:, kc, ti * P:(ti + 1) * P], rhs=W0[:, kc, BV:BV + 512], start=(kc == 0), stop=(kc == KC - 1)),
                            r=WK["W0"] + HTK, w=[PSK[bi]])
                    add("act", lambda e, bi=bi, blk=blk: e.copy(out=bV[:, blk, :, 0:64], in_=ps[bi][:, :].rearrange("p (h d) -> p h d", h=8)), r=[PSK[bi]], w=[("bV", blk)])

                def load_bc_l1():
                    load_bc(gscr.ap()[b, 1, :].partition_broadcast(P), ["gscr"])

                def make_block(nl):
                    n = g * GT + nl
                    q0 = nl * P
                    kbs = [1] if n == 0 else [0, 1]

                    def gate_mm():
                        for hf in range(2):
                            for kc in range(KC):
                                add("pe", lambda e, kc=kc, hf=hf: e.matmul(ps[hf][:, :], lhsT=hT[:, kc, q0:q0 + P], rhs=W0[:, kc, GA + hf * 512:GA + (hf + 1) * 512], start=(kc == 0), stop=(kc == KC - 1)),
                                    r=WK["W0"] + HTK, w=[PSK[hf]])

                    def gate_act():
                        for hf in range(2):
                            add("act", lambda e, hf=hf: e.activation(out=th[:, hf * 512:(hf + 1) * 512], in_=ps[hf][:, :], func=AF.Tanh, scale=0.5), r=[PSK[hf]], w=[("th", hf)])
                            add("dve", lambda e, hf=hf: e.scalar_tensor_tensor(out=th[:, hf * 512:(hf + 1) * 512], in0=th[:, hf * 512:(hf + 1) * 512], scalar=1.0, in1=ps[hf][:, :], op0=ALU.add, op1=ALU.mult),
                                r=[PSK[hf], ("th", hf)], w=[("th", hf)])

                    def swa_S(h):
                        bi = 4 + (sbank[0] % 2)
                        sbank[0] += 1
                        c, base = h % 4, (h // 4) * 64
                        for kb in kbs:
                            add("pe", lambda e, kb=kb, bi=bi: e.matmul(ps[bi][:, kb * P:(kb + 1) * P], lhsT=aKT[base:base + 64, (nl + kb) * P:(nl + kb + 1) * P],
                                                                      rhs=aQT[base:base + 64, c, q0:q0 + P], start=True, stop=True),
                                r=["aK", ("aQ", c)], w=[PSK[bi]])
                        return bi

                    def swa_rest(h, bi):
                        kvh = h // 4
                        si = swq[0] % 2
                        swq[0] += 1
                        k0, k1 = kbs[0], kbs[-1] + 1
                        add("act", lambda e: e.activation(out=swt[:, si, k0:k1, :], in_=ps[bi][:, k0 * P:k1 * P].rearrange("p (k t) -> p k t", t=P), func=AF.Exp, scale=0.125),
                            r=[PSK[bi]], w=[("swt", si)])
                        add("pool" if h % 2 == 0 else "dve", lambda e: e.tensor_tensor(out=pa[:, si, k0:k1, :], in0=swt[:, si, k0:k1, :], in1=EB[:, h, k0:k1, :], op=ALU.mult),
                            r=[("swt", si), "EB"], w=[("pa", si)])
                        pvb = 6 + h // 4
                        for kb in kbs:
                            add("pe", lambda e, kb=kb: e.matmul(ps[pvb][:, (h % 4) * 65:(h % 4) * 65 + 65], lhsT=pa[:, si, kb, :], rhs=aV[:, nl + kb, kvh, 0:65],
                                                                start=(kb == kbs[0]), stop=(kb == kbs[-1])),
                                r=[("pa", si), ("aV", nl + kb), "aVones"], w=[PSK[pvb]])

                    def normalise(dcol, sink):
                        for hb in range(2):
                            pv = ps[6 + hb][:, 0:260].rearrange("p (h d) -> p h d", d=65)
                            dn = den[:, dcol + hb * 4:dcol + hb * 4 + 4]
                            dk = ("den", dcol // 4 + hb)
                            if sink:
                                add("dve", lambda e: e.tensor_tensor(out=dn, in0=pv[:, :, 64], in1=lrp[:, 4, hb * 4:hb * 4 + 4], op=ALU.add), r=[PSK[6 + hb], "lrp4"], w=[dk])
                                add("dve", lambda e: e.reciprocal(out=dn, in_=dn), r=[dk], w=[dk])
                            else:
                                add("dve", lambda e: e.reciprocal(out=dn, in_=pv[:, :, 64]), r=[PSK[6 + hb]], w=[dk])
                            add("dve", lambda e: e.tensor_scalar(out=dn, in0=dn, scalar1=0.5, scalar2=None, op0=ALU.mult), r=[dk], w=[dk])
                            add("dve", lambda e: e.tensor_tensor(out=osb[:, hb * 256:(hb + 1) * 256].rearrange("p (h d) -> p h d", d=64), in0=pv[:, :, 0:64],
                                                                in1=dn.unsqueeze(2).to_broadcast([P, 4, 64]), op=ALU.mult),
                                r=[PSK[6 + hb], dk], w=[("osb", hb)])

                    def swa():
                        prev = swa_S(0)
                        for h in range(8):
                            nxt = swa_S(h + 1) if h + 1 < 8 else None
                            swa_rest(h, prev)
                            prev = nxt
                        normalise(0, True)
                        add("pool", lambda e: e.tensor_tensor(out=ysb[:, 0:512], in0=th[:, 0:512], in1=osb[:], op=ALU.mult), r=[("th", 0), ("osb", 0), ("osb", 1)], w=[("ysb", 0)])

                    def fox_S(ch):
                        h, j0, j1 = ch
                        bi = 4 + (sbank[0] % 2)
                        sbank[0] += 1
                        c, base = h // 2, (h % 2) * 64
                        for j in range(j0, j1):
                            add("pe", lambda e, j=j, bi=bi: e.matmul(ps[bi][:, (j - j0) * P:(j - j0 + 1) * P], lhsT=bKT[base:base + 64, c, j * P:(j + 1) * P],
                                                                    rhs=bQT[base:base + 64, c, q0:q0 + P], start=True, stop=True),
                                r=[("bK", c, j // GT), ("bQ", c)], w=[PSK[bi]])
                        return bi

                    def fox_rest(ch, bi):
                        h, j0, j1 = ch
                        pvb = 6 + h // 4
                        for j in range(j0, j1):
                            si = pbq[0] % NPB
                            pbq[0] += 1
                            add("act", lambda e, j=j, si=si: e.activation(out=pb[:, si, :], in_=ps[bi][:, (j - j0) * P:(j - j0 + 1) * P], func=AF.Exp, scale=0.125, bias=BT[:, j, nl, h:h + 1]),
                                r=[PSK[bi], "BT"], w=[("pb", si)])
                            if j == n:
                                add("pool", lambda e, si=si: e.tensor_tensor(out=pb[:, si, :], in0=pb[:, si, :], in1=cmaskb[:], op=ALU.mult), r=[("pb", si), "cmaskb"], w=[("pb", si)])
                            add("pe", lambda e, j=j, si=si: e.matmul(ps[pvb][:, (h % 4) * 65:(h % 4) * 65 + 65], lhsT=pb[:, si, :], rhs=bV[:, j, h, 0:65], start=(j == 0), stop=(j == n)),
                                r=[("pb", si), ("bV", j), "bVones"], w=[PSK[pvb]])

                    def fox():
                        chunks = []
                        for h in range(8):
                            for j0 in range(0, n + 1, 4):
                                chunks.append((h, j0, min(n + 1, j0 + 4)))
                        prev = fox_S(chunks[0])
                        for i, ch in enumerate(chunks):
                            nxt = fox_S(chunks[i + 1]) if i + 1 < len(chunks) else None
                            fox_rest(ch, prev)
                            prev = nxt
                        normalise(8, False)
                        add("pool", lambda e: e.tensor_tensor(out=ysb[:, 512:1024], in0=th[:, 512:1024], in1=osb[:], op=ALU.mult), r=[("th", 1), ("osb", 0), ("osb", 1)], w=[("ysb", 1)])

                    def tail_a():
                        ytp = ps[2][:, :].bitcast(BF16)
                        for kc in range(KC):
                            add("pe", lambda e, kc=kc: e.transpose(ytp[:, kc * P:(kc + 1) * P], ysb[:, kc * P:(kc + 1) * P], identb[:]),
                                r=[("ysb", kc // 4), "identb"], w=[PSK[2]])
                        add("dve", lambda e: e.tensor_copy(out=yT[:, :, q0:q0 + P], in_=ytp.rearrange("p (k t) -> p k t", t=P)), r=[PSK[2]], w=[("yT", nl)] + Y1K)

                    def tail_b():
                        out_proj_residual(W1, "W1", yT, lambda kc: [("yT", nl)], nl)

                    return gate_mm, gate_act, swa, fox, tail_a, tail_b

                Y1K = [("y1", c) for c in range(KC)]
                blocks = [make_block(nl) for nl in range(GT)]
                for nl in range(GT):
                    gm, ga, sw, fx, ta, tb = blocks[nl]
                    if nl == 0:
                        gm()
                        ga()
                    sw()
                    fx()
                    if nl + 1 < GT:
                        blocks[nl + 1][0]()
                    ta()
                    if nl + 1 < GT:
                        blocks[nl + 1][1]()
                    tb()
                    if EARLY_L1_NORM and stop is None:
                        norm_tile_to_hT(1, b, nl, 3)
                        if nl == GT - 1:
                            load_bc_l1()

                if dbg:
                    for ti in range(GT):
                        add("sp", lambda e, ti=ti: e.dma_start(out=dbg_d[b, t0 + ti * P:t0 + (ti + 1) * P, :], in_=xs[:, ti, :]), r=[xk(ti)], dsem="dbgs%d" % ti)

                if stop == "l0":
                    last_store = store_x()
                    continue
                if not (EARLY_L1_NORM and stop is None):
                    norm_to_hT(1, b)
                    load_bc_l1()
                YTK = [("yT", nl) for nl in range(GT)]

                def l1_A(c, T):
                    bA = T["banks"][0]
                    for kc in range(KC):
                        add("pe", lambda e, kc=kc: e.matmul(ps[bA][:, 0:G], lhsT=W2[:, kc, c * P:(c + 1) * P], rhs=hT[:, kc, :], start=(kc == 0), stop=(kc == KC - 1)),
                            r=WK["W2"] + HTK, w=[PSK[bA]])
                    for kc in range(KC):
                        add("pe", lambda e, kc=kc: e.matmul(ps[bA][:, G:2 * G], lhsT=W2[:, kc, D + c * P:D + (c + 1) * P], rhs=hT[:, kc, :], start=(kc == 0), stop=(kc == KC - 1)),
                            r=WK["W2"] + HTK, w=[PSK[bA]])

                def l1_B1(c, T):
                    bA = T["banks"][0]
                    add("pool", lambda e: e.tensor_copy(out=T["xrb"][:, 0:3], in_=tails[:, c, 0:3]), r=["tails"], w=[T["xrtK"]])
                    add("act", lambda e: e.copy(out=T["xrb"][:, 3:3 + G], in_=ps[bA][:, 0:G]), r=[PSK[bA]], w=[T["xrbK"]])
                    add("pool", lambda e: e.tensor_copy(out=tails[:, c, 0:3], in_=T["xrb"][:, G:G + 3]), r=[T["xrbK"]], w=["tails"])
                    add("act", lambda e: e.activation(out=T["sg"], in_=ps[bA][:, G:2 * G], func=AF.Tanh, scale=0.5), r=[PSK[bA]], w=[T["sgK"]])

                def l1_B2(c, T):
                    bA = T["banks"][0]
                    add("dve", lambda e: e.scalar_tensor_tensor(out=T["sg"], in0=T["sg"], scalar=1.0, in1=ps[bA][:, G:2 * G], op0=ALU.add, op1=ALU.mult), r=[PSK[bA], T["sgK"]], w=[T["sgK"]])
                    cw = lambda j: colp[:, CP_CW + c * 4 + j:CP_CW + c * 4 + j + 1]
                    add("dve", lambda e: e.tensor_scalar(out=T["xc"], in0=T["xrb"][:, 0:G], scalar1=cw(0), scalar2=colp[:, CP_CB + c:CP_CB + c + 1], op0=ALU.mult, op1=ALU.add),
                        r=[T["xrbK"], T["xrtK"], "colp"], w=[T["xcK"]])
                    for j in range(1, 4):
                        add("dve", lambda e, j=j: e.scalar_tensor_tensor(out=T["xc"], in0=T["xrb"][:, j:j + G], scalar=cw(j), in1=T["xc"], op0=ALU.mult, op1=ALU.add),
                            r=[T["xrbK"], T["xrtK"], "colp", T["xcK"]], w=[T["xcK"]])
                    add("dve", lambda e: e.tensor_copy(out=T["xcb"], in_=T["xc"]), r=[T["xcK"]], w=[T["xcbK"]])

                def l1_C(c, T):
                    bB = T["banks"][1]
                    add("pe", lambda e: e.matmul(ps[bB][:, 0:G], lhsT=W4[:, 0, c, :], rhs=T["xcb"], start=True, stop=True), r=WK["W4"] + [T["xcbK"]], w=[PSK[bB]])
                    add("pe", lambda e: e.matmul(ps[bB][:, G:2 * G], lhsT=W4[:, 1, c, :], rhs=T["xcb"], start=True, stop=True), r=WK["W4"] + [T["xcbK"]], w=[PSK[bB]])

                def l1_D1(c, T):
                    bB = T["banks"][1]
                    add("act", lambda e: e.activation(out=T["tha"], in_=ps[bB][:, 0:G], func=AF.Tanh, scale=0.5, bias=lrp[:, 2, c:c + 1]), r=[PSK[bB]] + LRK, w=[T["thaK"]])
                    add("act", lambda e: e.activation(out=T["thx"], in_=ps[bB][:, G:2 * G], func=AF.Tanh, scale=0.5, bias=lrp[:, 3, c:c + 1]), r=[PSK[bB]] + LRK, w=[T["thxK"]])

                def l1_D2(c, T):
                    add("act", lambda e: e.activation(out=T["ab"], in_=T["tha"], func=AF.Exp, scale=lrp[:, 1, c:c + 1], bias=lrp[:, 1, c:c + 1]), r=[T["thaK"]] + LRK, w=[T["abK"]])
                    add("act", lambda e: e.activation(out=T["tha"], in_=T["tha"], func=AF.Exp, scale=lrp[:, 0, c:c + 1], bias=lrp[:, 0, c:c + 1]), r=[T["thaK"]] + LRK, w=[T["thaK"]])
                    add("dve", lambda e: e.scalar_tensor_tensor(out=T["thx"], in0=T["thx"], scalar=1.0, in1=T["xc"], op0=ALU.add, op1=ALU.mult), r=[T["thxK"], T["xcK"]], w=[T["thxK"]])

                def l1_D3(c, T):
                    add("act", lambda e: e.activation(out=T["tha"], in_=T["tha"], func=AF.Sqrt, scale=-1.0, bias=1.0), r=[T["thaK"]], w=[T["thaK"]])

                def l1_D4(c, T):
                    add("dve", lambda e: e.scalar_tensor_tensor(out=T["tha"], in0=T["tha"], scalar=0.5, in1=T["thx"], op0=ALU.mult, op1=ALU.mult), r=[T["thaK"], T["thxK"]], w=[T["thaK"]])
                    add("dve", lambda e: e.tensor_tensor_scan(out=T["xc"], data0=T["ab"], data1=T["tha"], initial=state[:, c:c + 1], op0=ALU.mult, op1=ALU.add),
                        r=[T["abK"], T["thaK"], "state", T["thxK"]], w=[T["xcK"]])
                    add("pool", lambda e: e.tensor_copy(out=state[:, c:c + 1], in_=T["xc"][:, G - 1:G]), r=[T["xcK"]], w=["state"])
                    add("dve", lambda e: e.scalar_tensor_tensor(out=yT[:, c, :], in0=T["xc"], scalar=0.5, in1=T["sg"], op0=ALU.mult, op1=ALU.mult),
                        r=[T["xcK"], T["sgK"]], w=YTK + [("y1", c)])

                ACCB = (0, 1, 4, 5)

                def l1_O(c, T):
                    for ti in range(GT):
                        for hf in range(2):
                            ab_ = ACCB[ti * 2 + hf]
                            add("pe", lambda e, ti=ti, hf=hf, ab_=ab_: e.matmul(ps[ab_][:, :], lhsT=yT[:, c, ti * P:(ti + 1) * P], rhs=W3[:, c, hf * 512:(hf + 1) * 512],
                                                                           start=(c == 0), stop=(c == KC - 1)),
                                r=WK["W3"] + [("y1", c)], w=[PSK[ab_]])

                pairs = [(2 * k, 2 * k + 1) for k in range(KC // 2)]

                def stage(fn, pr):
                    for i, c in enumerate(pr):
                        fn(c, L1S[i])

                stage(l1_A, pairs[0])
                if stop is None and gi + 1 < nseq * ngrp:
                    xs.cur = (gi + 1) % 2
                    for ti in range(GT):
                        rmsnorm_rstd(ti, 2 + ti)
                    xs.cur = gi % 2
                    pre_stats[0] = True
                for k, pr in enumerate(pairs):
                    stage(l1_B1, pr)
                    stage(l1_B2, pr)
                    stage(l1_C, pr)
                    stage(l1_D1, pr)
                    if k + 1 < len(pairs):
                        stage(l1_A, pairs[k + 1])
                    stage(l1_D2, pr)
                    stage(l1_D3, pr)
                    stage(l1_D4, pr)
                    stage(l1_O, pr)
                if stop == "l1":
                    for ti in range(GT):
                        for hf in range(2):
                            residual_from(ti, hf, ACCB[ti * 2 + hf])
                    last_store = store_x()
                    continue
                for ti in range(GT):
                    for hf in range(2):
                        residual_from(ti, hf, ACCB[ti * 2 + hf])
                load_bc(fg_d.ap()[0, :].partition_broadcast(P), [])
                fin = [(th[:], [("th", 0), ("th", 1)]), (l1b[:, 0:D], ["l1xc", "l1tha", "l1thx", "l1ab"])]
                for ti in range(GT):
                    rmsnorm_rstd(ti, 4 + ti)
                    fo, fk = fin[ti]
                    add("dve", lambda e, ti=ti, fo=fo: e.scalar_tensor_tensor(out=fo, in0=xs[:, ti, :], scalar=small[:, 4 + ti:5 + ti], in1=bc[:], op0=ALU.mult, op1=ALU.mult),
                        r=[xk(ti), ("sm", 4 + ti), "bc"], w=fk)
                    add("sp", lambda e, ti=ti, fo=fo: e.dma_start(out=out_d[b, t0 + ti * P:t0 + (ti + 1) * P, :], in_=fo), r=fk, dsem="xst%d" % ti)
                    last_store = [("xst0", SC.cnt["xst0"]), ("xst1", SC.cnt["xst1"])]
        finals = list(last_store)
        if dbg:
            finals.append(("dbgs0", SC.cnt["dbgs0"]))
            finals.append(("dbgs1", SC.cnt["dbgs1"]))
        SC.run(final_conds=finals)
    return nc


def _t5_bucket(rel):
    n = np.maximum(rel, 0)
    nf = np.maximum(n, 1).astype(np.float32)
    large = 16 + (np.log(nf / np.float32(16)) / np.float32(math.log(128 / 16)) * np.float32(32 - 16)).astype(np.int32)
    large = np.minimum(large, 31)
    return np.where(n < 16, n, large)


def _host_layout(inp, nseq_core=SEQ_PER_CORE, ncores=NCORES):
    f = lambda a: np.ascontiguousarray(np.asarray(a, dtype=np.float32))
    w_in = f(inp["attn_w_in"])[0]
    cols = np.concatenate([
        np.concatenate([np.concatenate([np.arange(c * 64, (c + 1) * 64), np.arange((4 + c) * 64, (5 + c) * 64)]) for c in range(4)]),
        np.arange(512, 640),
        np.arange(768, 1280),
        np.arange(1280, 1792),
        np.arange(2304, 2312),
        np.arange(640, 768),
        np.arange(1792, 2304),
        np.arange(2312, 3336),
    ])
    assert cols.size == C0
    kcl = lambda w: np.ascontiguousarray(w.reshape(KC, P, w.shape[1]).transpose(1, 0, 2))
    shared = {
        "w0": kcl(w_in[:, cols]),
        "w1": kcl(f(inp["attn_w_out"])[0]),
        "w2": kcl(f(inp["lru_w_in"])[0]),
        "w3": kcl(f(inp["lru_w_out"])[0]),
        "w4": np.ascontiguousarray(np.stack([f(inp["lru_w_a"])[0], f(inp["lru_w_x"])[0]], 0).transpose(2, 0, 1, 3)),
        "ada_w": f(inp["ada_w"]),
        "final_g": f(inp["final_g"]).reshape(1, D),
    }
    colv = lambda v: np.ascontiguousarray(v.reshape(KC, P).T)
    colp = np.zeros((P, CP_N), np.float32)
    ng = f(inp["norm_g"])
    ab = f(inp["ada_b"])
    for l in range(2):
        colp[:, CP_NG + l * 8:CP_NG + (l + 1) * 8] = colv(ng[l])
        for part in range(3):
            colp[:, CP_AB + (l * 3 + part) * 8:CP_AB + (l * 3 + part + 1) * 8] = colv(ab[l, part * D:(part + 1) * D])
    cw = f(inp["lru_conv_w"])[0]
    for j in range(4):
        colp[:, CP_CW + j:CP_CW + 32:4] = colv(cw[j])
    colp[:, CP_CB:CP_CB + 8] = colv(f(inp["lru_conv_b"])[0])
    colp[:, CP_BA:CP_BA + 8] = colv(f(inp["lru_b_a"])[0])
    colp[:, CP_BX:CP_BX + 8] = colv(f(inp["lru_b_x"])[0])
    colp[:, CP_LAM:CP_LAM + 8] = colv(f(inp["lru_lambda"])[0])
    colp[:, CP_SINK:CP_SINK + 8] = np.broadcast_to(f(inp["attn_sinks"])[0][None, :], (P, 8))
    eye = np.eye(P, dtype=np.float32)
    sidx = np.arange(P)
    cmask = (sidx[:, None] <= sidx[None, :]).astype(np.float32)
    shared["cmat"] = np.ascontiguousarray(np.concatenate([eye, eye[::-1], cmask], axis=1))
    rel = np.arange(383) - 127
    onehot = np.zeros((32, 383), np.float32)
    onehot[_t5_bucket(rel), np.arange(383)] = 1.0
    valid = ((rel >= 0) & (rel < 128)).astype(np.float32)
    shared["rb"] = np.ascontiguousarray(np.concatenate([f(inp["rel_bias"]), onehot], axis=1))
    v8 = np.zeros((8, 384), np.float32)
    v8[:, 0:383] = valid[None, :]
    v8[:, 383] = f(inp["attn_b_f"])[0]
    shared["v8"] = v8
    x = f(inp["x"])
    c = f(inp["c"])
    maps = []
    for core in range(ncores):
        b0 = core * nseq_core
        m = dict(shared)
        m["x"] = x[b0:b0 + nseq_core]
        cp_ = colp.copy()
        ct = c[b0:b0 + nseq_core].reshape(nseq_core, KC, P).transpose(2, 1, 0)
        ctf = np.zeros((P, KC, 4), np.float32)
        ctf[:, :, 0:nseq_core] = ct
        cp_[:, CP_CT:CP_CT + KC * 4] = ctf.reshape(P, KC * 4)
        m["colp"] = cp_
        maps.append(m)
    return maps


def kernel(**inputs):
    maps = _host_layout(inputs)
    nc = build()
    res = run_bass_kernel_spmd(nc, maps, core_ids=list(range(NCORES)))
    out = np.concatenate([np.asarray(r["out"], dtype=np.float32) for r in res.results], axis=0)
    return out
```

```python
import math
from contextlib import ExitStack

import numpy as np
import concourse.bass as bass
import concourse.mybir as mybir
from concourse.bass_utils import run_bass_kernel_spmd

F32 = mybir.dt.float32
BF16 = mybir.dt.bfloat16
AF = mybir.ActivationFunctionType
ALU = mybir.AluOpType

NCORES = 8
P = 128
D = 1024
KC = 8
S = 2048
G = 256
GT = G // P
NB = S // P
NGRP = S // G
SEQ_PER_CORE = 4
EPS = 1e-6
VW = 66

AQ, AK, BQ, BK, FL, AV, BV, GA, C0 = 0, 512, 640, 1152, 1664, 1672, 1800, 2312, 3336
CP_NG, CP_AB, CP_CW, CP_CB, CP_BA, CP_BX, CP_LAM, CP_SINK, CP_CT, CP_N = 0, 16, 64, 96, 104, 112, 120, 128, 136, 168

SAME_ENGINE_SYNC = True
RSTD_POW = True
EARLY_L1_NORM = False


class _Rec:
    def __init__(self):
        self.call = None

    def __getattr__(self, name):
        def f(*a, **k):
            self.call = (name, a, k)
            return self
        return f


class _XS:
    def __init__(self, bufs):
        self.bufs = bufs
        self.cur = 0

    def __getitem__(self, idx):
        return self.bufs[self.cur][idx]


class Sched:
    STREAMS = ("pe", "act", "dve", "pool", "sp")

    def __init__(self, nc, es):
        self.nc = nc
        self.es = es
        self.sems = {}
        self.cnt = {}
        self.ops = {s: [] for s in self.STREAMS}
        self.lastw = {}
        self.rds = {}
        self.excl = set()
        for s in ("pe", "act", "dve", "pool"):
            self.new_sem("E_" + s)

    def new_sem(self, name):
        self.sems[name] = self.es.enter_context(self.nc.semaphore(name))
        self.cnt[name] = 0
        return name

    def add(self, stream, fn, r=(), w=(), dsem=None, extra=()):
        deps = set(extra)
        own = dsem if dsem is not None else "E_" + stream
        for k in r:
            if k in self.lastw:
                deps.add(self.lastw[k])
            if k in self.excl:
                deps.update(d for d in self.rds.get(k, ()) if d[0] != own)
        for k in w:
            if k in self.lastw:
                deps.add(self.lastw[k])
            deps.update(self.rds.get(k, ()))
        if dsem is None:
            sname = "E_" + stream
            self.cnt[sname] += 1
            inc = 1
        else:
            sname = dsem
            self.cnt[sname] += 16
            inc = 16
        done = (sname, self.cnt[sname])
        for k in r:
            self.rds.setdefault(k, []).append(done)
        for k in w:
            self.lastw[k] = done
            self.rds[k] = []
        rec = _Rec()
        fn(rec)
        assert rec.call is not None
        self.ops[stream].append((deps, rec.call, sname, inc))
        return done

    def emit(self, stream, eng):
        known = {}
        own = "E_" + stream
        for deps, fn, sname, inc in self.ops[stream]:
            best = {}
            for (s, v) in deps:
                if v > best.get(s, 0):
                    best[s] = v
            for s, v in best.items():
                if s == own and (stream == "pe" or not SAME_ENGINE_SYNC):
                    continue
                if known.get(s, 0) >= v:
                    continue
                eng.wait_ge(self.sems[s], v)
                known[s] = v
            name, a, k = fn
            ins = getattr(eng, name)(*a, **k)
            ins.then_inc(self.sems[sname], inc)

    def run(self, final_conds=()):
        nc = self.nc
        with nc.Block() as block:
            @block.tensor
            def _(e):
                self.emit("pe", e)

            @block.scalar
            def _(e):
                self.emit("act", e)

            @block.vector
            def _(e):
                self.emit("dve", e)

            @block.gpsimd
            def _(e):
                self.emit("pool", e)

            @block.sync
            def _(e):
                self.emit("sp", e)
                best = {}
                for (s, v) in final_conds:
                    best[s] = max(best.get(s, 0), v)
                for s, v in best.items():
                    e.wait_ge(self.sems[s], v)


def build(nseq=SEQ_PER_CORE, ngrp=NGRP, dbg=False, stop=None):
    nc = bass.Bass("TRN2", target_bir_lowering=False, dynamic_dma_scratch_size=2048)
    x_d = nc.dram_tensor("x", [nseq, S, D], F32, kind="ExternalInput").ap()
    out_d = nc.dram_tensor("out", [nseq, S, D], F32, kind="ExternalOutput").ap()
    w0_d = nc.dram_tensor("w0", [P, KC, C0], F32, kind="ExternalInput").ap()
    w1_d = nc.dram_tensor("w1", [P, KC, D], F32, kind="ExternalInput").ap()
    w2_d = nc.dram_tensor("w2", [P, KC, 2 * D], F32, kind="ExternalInput").ap()
    w3_d = nc.dram_tensor("w3", [P, KC, D], F32, kind="ExternalInput").ap()
    w4_d = nc.dram_tensor("w4", [P, 2, KC, P], F32, kind="ExternalInput").ap()
    ada_d = nc.dram_tensor("ada_w", [2, D, 3 * D], F32, kind="ExternalInput").ap()
    colp_d = nc.dram_tensor("colp", [P, CP_N], F32, kind="ExternalInput").ap()
    cmat_d = nc.dram_tensor("cmat", [P, 3 * P], F32, kind="ExternalInput").ap()
    rb_d = nc.dram_tensor("rb", [32, 8 + 383], F32, kind="ExternalInput").ap()
    v8_d = nc.dram_tensor("v8", [8, 384], F32, kind="ExternalInput").ap()
    fg_d = nc.dram_tensor("final_g", [1, D], F32, kind="ExternalInput")
    gscr = nc.dram_tensor("gscr", [4, 2, D], F32, kind="Internal")
    ebscr = nc.dram_tensor("ebscr", [8, 384], F32, kind="Internal")
    if dbg:
        dbg_d = nc.dram_tensor("dbg", [nseq, S, D], F32, kind="ExternalOutput").ap()

    with ExitStack() as es:
        SC = Sched(nc, es)
        add = SC.add

        def sb(name, shape, dt):
            return es.enter_context(nc.sbuf_tensor("s_" + name, shape, dt))

        W0 = sb("W0", [P, KC, C0], BF16)
        W1 = sb("W1", [P, KC, D], BF16)
        W2 = sb("W2", [P, KC, 2 * D], BF16)
        W3 = sb("W3", [P, KC, D], BF16)
        W4 = sb("W4", [P, 2, KC, P], BF16)
        bKT = sb("bKT", [P, 4, S], BF16)
        bV = sb("bV", [P, NB, 8, VW], BF16)
        aKT = sb("aKT", [P, P + G], BF16)
        aV = sb("aV", [P, GT + 1, 2, VW], BF16)
        xs = _XS([sb("xs0", [P, GT, D], F32), sb("xs1", [P, GT, D], F32)])
        xn = sb("xn", [P, D], F32)
        bc = sb("bc", [P, D], F32)
        hT = sb("hT", [P, KC, G], BF16)
        yT = sb("yT", [P, KC, G], BF16)
        aQT = sb("aQT", [P, 4, G], BF16)
        bQT = sb("bQT", [P, 4, G], BF16)
        th = sb("th", [P, D], F32)
        osb = sb("osb", [P, 512], F32)
        ysb = sb("ysb", [P, D], BF16)
        NPB = 8
        pb = sb("pb", [P, NPB, P], BF16)
        swt = sb("swt", [P, 2, 2, P], F32)
        pa = sb("pa", [P, 2, 2, P], BF16)
        EB = sb("EB", [P, 8, 2, P], BF16)
        colp = sb("colp", [P, CP_N], F32)
        cmat = sb("cmat", [P, 3 * P], F32)
        identb = sb("identb", [P, P], BF16)
        cmaskb = sb("cmaskb", [P, P], BF16)
        rbs = ysb[:].bitcast(F32)[0:32, 0:8 + 383]
        RBS_K = [("ysb", 0), ("ysb", 1)]
        v8 = pb[:].rearrange("p a b -> p (a b)").bitcast(F32)[0:8, 0:384]
        V8_K = [("pb", i) for i in range(NPB)]
        gtab = swt[:].rearrange("p a b c -> p (a b c)")[0:8, 0:384]
        GT_K = [("swt", 0), ("swt", 1)]
        scT = sb("scT", [P, KC, 4], F32)
        modT = sb("modT", [P, 48, 4], F32)
        gsT = sb("gsT", [P, 2, 8, 4], F32)
        small = sb("small", [P, 64], F32)
        lrp = sb("lrp", [P, 5, 8], F32)
        spt = sb("spt", [P, 5, 8], F32)
        FnegT = sb("FnegT", [P, NB, 8], F32)
        FendBC = sb("FendBC", [P, GT, 8], F32)
        l1b = sb("l1b", [P, 5 * G], F32)
        xrb1 = sb("xrb1", [P, G + 4], F32)
        xcb1 = sb("xcb1", [P, G], BF16)
        fdiag = sb("fdiag", [8, GT, 8], F32)
        ones8 = sb("ones8", [8, G], F32)
        fcar = sb("fcar", [8, 2], F32)
        den = sb("den", [P, 16], F32)
        tails = sb("tails", [P, KC, 4], F32)
        state = sb("state", [P, KC], F32)
        xrb = sb("xrb", [P, G + 4], F32)
        xcb = sb("xcb", [P, G], BF16)
        L1S = [
            dict(xc=osb[:, 0:G], xcK=("osb", 0), tha=th[:, 0:G], thaK=("th", 0), thx=th[:, G:2 * G], thxK=("th", 0),
                 ab=th[:, 2 * G:3 * G], abK=("th", 1), sg=th[:, 3 * G:4 * G], sgK=("th", 1),
                 xrb=xrb[:], xrbK="xrb0", xrtK="xrbt0", xcb=xcb[:], xcbK="xcb0", banks=(2, 3)),
            dict(xc=l1b[:, 0:G], xcK="l1xc", tha=l1b[:, G:2 * G], thaK="l1tha", thx=l1b[:, 2 * G:3 * G], thxK="l1thx",
                 ab=l1b[:, 3 * G:4 * G], abK="l1ab", sg=l1b[:, 4 * G:5 * G], sgK="l1sg",
                 xrb=xrb1[:], xrbK="xrb1", xrtK="xrbt1", xcb=xcb1[:], xcbK="xcb1", banks=(6, 7)),
        ]
        frow = [th[0:8, 0:G], th[0:8, G:2 * G], th[0:8, 2 * G:3 * G], th[0:8, 3 * G:4 * G], osb[0:8, 0:G]]
        FRK = [("th", 0), ("th", 0), ("th", 1), ("th", 1), ("osb", 0)]

        ps = [es.enter_context(nc.psum_tensor("psum%d" % i, [P, 512], F32)) for i in range(8)]
        PSK = ["ps%d" % i for i in range(8)]
        SC.excl.update(PSK)

        ident = cmat[:, 0:P]
        BT = cmat[:, P:3 * P].rearrange("p (j n h) -> p j n h", n=GT, h=8)
        jrev = cmat[:, P:2 * P]
        cmaskf = cmat[:, 2 * P:3 * P]

        def cp(off, n):
            return colp[:, off:off + n]

        for nm, dst, src, wk in (("colp", colp[:], colp_d, ["colp"]), ("cmat", cmat[:], cmat_d, ["cmat", "cmat_j"]), ("rbs", rbs, rb_d, RBS_K), ("v8", v8, v8_d, V8_K)):
            sem = SC.new_sem("ld_" + nm)
            add("sp", lambda e, dst=dst, src=src: e.dma_start(out=dst, in_=src), w=wk, dsem=sem)

        NLANE = 6
        for i in range(NLANE):
            SC.new_sem("wl%d" % i)
        WK = {nm: [] for nm in ("W0", "W1", "W2", "W3", "W4")}
        lane_last = [None] * NLANE
        wq = [0]

        def wload(nm, dst, src):
            i = wq[0]
            wq[0] += 1
            key = (nm, len(WK[nm]))
            WK[nm].append(key)
            ln = i % NLANE
            ex = [lane_last[ln]] if lane_last[ln] is not None else []
            lane_last[ln] = add("pool", lambda e: e.dma_start(out=dst, in_=src), w=[key], dsem="wl%d" % ln, extra=ex)

        for kc in range(KC):
            for (c0, c1) in ((0, 1668), (1668, C0)):
                wload("W0", W0[:, kc, c0:c1], w0_d[:, kc, c0:c1])
        def load_rest_of_weights():
            for kc in range(0, KC, 2):
                wload("W1", W1[:, kc:kc + 2, :], w1_d[:, kc:kc + 2, :])
            for kc in range(KC):
                wload("W2", W2[:, kc, :], w2_d[:, kc, :])
            for kc in range(0, KC, 2):
                wload("W3", W3[:, kc:kc + 2, :], w3_d[:, kc:kc + 2, :])
            for i in range(2):
                for k2 in range(0, KC, 2):
                    wload("W4", W4[:, i, k2:k2 + 2, :], w4_d[:, i, k2:k2 + 2, :])

        add("dve", lambda e: e.tensor_copy(out=identb[:], in_=ident), r=["cmat"], w=["identb"])
        add("dve", lambda e: e.tensor_copy(out=cmaskb[:], in_=cmaskf), r=["cmat_j"], w=["cmaskb"])
        add("dve", lambda e: e.memset(ones8[:], 1.0), w=["ones8"])
        add("dve", lambda e: e.memset(small[:, 0:1], -0.5), w=["negh"])
        add("dve", lambda e: e.memset(bV[:, :, :, 64:VW], 1.0), w=["bVones"])
        add("dve", lambda e: e.memset(aV[:, :, :, 64:VW], 1.0), w=["aVones"])
        negh = small[:, 0:1]

        cT = cp(CP_CT, 32)
        scf = scT[:].rearrange("p k b -> p (k b)")
        add("act", lambda e: e.activation(out=scf, in_=cT, func=AF.Tanh, scale=0.5), r=["colp"], w=["scT"])
        add("dve", lambda e: e.scalar_tensor_tensor(out=scf, in0=scf, scalar=1.0, in1=cT, op0=ALU.add, op1=ALU.mult), r=["scT", "colp"], w=["scT"])
        add("dve", lambda e: e.tensor_scalar(out=scf, in0=scf, scalar1=0.5, scalar2=None, op0=ALU.mult), r=["scT"], w=["scT"])

        stag = [(xn[:].rearrange("p (k j) -> p k j", k=KC), ("xn", 0)), (bc[:].rearrange("p (k j) -> p k j", k=KC), "bc"),
                (th[:].rearrange("p (k j) -> p k j", k=KC), ("th", 0)), (l1b[:, 0:D].rearrange("p (k j) -> p k j", k=KC), "l1xc")]
        NST = len(stag)
        for i in range(NST):
            SC.new_sem("ad%d" % i)
        idx = 0
        for l in range(2):
            for part in range(3):
                for c in range(KC):
                    slot, skey = stag[idx % NST]
                    col0 = part * D + c * P
                    src = ada_d[l, :, col0:col0 + P].rearrange("(k p) j -> p k j", p=P)
                    add("sp", lambda e, slot=slot, src=src: e.dma_start(out=slot, in_=src), w=[skey], dsem="ad%d" % (idx % NST))
                    for kc in range(KC):
                        add("pe", lambda e, slot=slot, kc=kc, idx=idx: e.matmul(ps[2][:, idx * 4:idx * 4 + 4], lhsT=slot[:, kc, :], rhs=scT[:, kc, :], start=(kc == 0), stop=(kc == KC - 1)),
                            r=[skey, "scT"], w=[PSK[2]])
                    idx += 1
        add("dve", lambda e: e.tensor_tensor(out=modT[:], in0=ps[2][:, 0:192].rearrange("p (i b) -> p i b", b=4),
                                             in1=cp(CP_AB, 48).unsqueeze(2).to_broadcast([P, 48, 4]), op=ALU.add),
            r=[PSK[2], "colp"], w=["modT"])
        for l in range(2):
            sc_l = modT[:, l * 24 + 8:l * 24 + 16, :]
            add("dve", lambda e, l=l, sc_l=sc_l: e.scalar_tensor_tensor(out=gsT[:, l, :, :], in0=sc_l, scalar=1.0,
                                                                    in1=cp(CP_NG + l * 8, 8).unsqueeze(2).to_broadcast([P, 8, 4]),
                                                                    op0=ALU.add, op1=ALU.mult), r=["modT", "colp"], w=["gsT"])
        SC.new_sem("gs_w")
        for l in range(2):
            for bb in range(4):
                dst = bass.AP(tensor=gscr, offset=bb * 2 * D + l * D, ap=[[1, P], [P, KC]])
                add("sp", lambda e, l=l, bb=bb, dst=dst: e.dma_start(out=dst, in_=modT[:, l * 24 + 16:l * 24 + 24, bb], allow_slow_non_contiguous=True),
                    r=["modT"], w=["gscr"], dsem="gs_w")

        def shiftc(l, c, b):
            return modT[:, l * 24 + c, b:b + 1]

        def gsc(l, c, b):
            return gsT[:, l, c, b:b + 1]

        def softplus_from_e(e_ap, eK, out_ap, oK, t_z, zK, t_z2, z2K, t_ln, lnK):
            t_p = out_ap
            add("act", lambda e: e.activation(out=t_ln, in_=e_ap, func=AF.Ln, bias=1.0), r=[eK], w=[lnK])
            add("dve", lambda e: e.tensor_scalar(out=t_z, in0=e_ap, scalar1=2.0, scalar2=None, op0=ALU.add), r=[eK], w=[zK])
            add("dve", lambda e: e.reciprocal(out=t_z, in_=t_z), r=[zK], w=[zK])
            add("dve", lambda e: e.tensor_tensor(out=t_z, in0=t_z, in1=e_ap, op=ALU.mult), r=[zK, eK], w=[zK])
            add("dve", lambda e: e.tensor_tensor(out=t_z2, in0=t_z, in1=t_z, op=ALU.mult), r=[zK], w=[z2K])
            add("dve", lambda e: e.tensor_scalar(out=t_p, in0=t_z2, scalar1=1.0 / 9, scalar2=1.0 / 7, op0=ALU.mult, op1=ALU.add), r=[z2K], w=[oK])
            for cst in (1.0 / 5, 1.0 / 3, 1.0):
                add("dve", lambda e: e.tensor_tensor(out=t_p, in0=t_p, in1=t_z2, op=ALU.mult), r=[oK, z2K], w=[oK])
                add("dve", lambda e, cst=cst: e.tensor_scalar(out=t_p, in0=t_p, scalar1=cst, scalar2=None, op0=ALU.add), r=[oK], w=[oK])
            add("dve", lambda e: e.scalar_tensor_tensor(out=t_p, in0=t_z, scalar=2.0, in1=t_p, op0=ALU.mult, op1=ALU.mult), r=[oK, zK], w=[oK])
            add("dve", lambda e: e.tensor_scalar(out=t_z2, in0=e_ap, scalar1=0.5, scalar2=None, op0=ALU.is_lt), r=[eK, oK], w=[z2K])
            add("dve", lambda e: e.tensor_tensor(out=t_p, in0=t_p, in1=t_ln, op=ALU.subtract), r=[oK, lnK], w=[oK])
            add("dve", lambda e: e.tensor_tensor(out=t_p, in0=t_p, in1=t_z2, op=ALU.mult), r=[oK, z2K], w=[oK])
            add("dve", lambda e: e.tensor_tensor(out=t_p, in0=t_p, in1=t_ln, op=ALU.add), r=[oK, lnK], w=[oK])

        add("act", lambda e: e.activation(out=spt[:, 3, :], in_=cp(CP_LAM, 8), func=AF.Exp, scale=-1.0), r=["colp"], w=["spt_e"])
        softplus_from_e(spt[:, 3, :], "spt_e", spt[:, 2, :], "sptout", spt[:, 0, :], "sptz", spt[:, 1, :], "sptz2", spt[:, 4, :], "sptln")
        add("dve", lambda e: e.tensor_scalar(out=lrp[:, 0, :], in0=spt[:, 2, :], scalar1=-8.0, scalar2=None, op0=ALU.mult), r=["sptout"], w=["lrp0"])
        add("dve", lambda e: e.tensor_scalar(out=lrp[:, 1, :], in0=spt[:, 2, :], scalar1=-4.0, scalar2=None, op0=ALU.mult), r=["sptout"], w=["lrp1"])
        add("dve", lambda e: e.tensor_scalar(out=lrp[:, 2, :], in0=cp(CP_BA, 8), scalar1=0.5, scalar2=None, op0=ALU.mult), r=["colp"], w=["lrp2"])
        add("dve", lambda e: e.tensor_scalar(out=lrp[:, 3, :], in0=cp(CP_BX, 8), scalar1=0.5, scalar2=None, op0=ALU.mult), r=["colp"], w=["lrp3"])
        add("act", lambda e: e.activation(out=lrp[:, 4, :], in_=cp(CP_SINK, 8), func=AF.Exp), r=["colp"], w=["lrp4"])
        LRK = ["lrp0", "lrp1", "lrp2", "lrp3"]
        add("dve", lambda e: e.tensor_scalar(out=fcar[:, 1:2], in0=v8[:, 383:384], scalar1=-1.0, scalar2=None, op0=ALU.mult), r=V8_K, w=["nbf"])
        nbf = fcar[:, 1:2]

        add("pe", lambda e: e.matmul(ps[3][0:8, 0:383], lhsT=rbs[:, 0:8], rhs=rbs[:, 8:391], start=True, stop=True), r=RBS_K, w=[PSK[3]])
        add("act", lambda e: e.activation(out=gtab[:, 0:383], in_=ps[3][0:8, 0:383], func=AF.Exp), r=[PSK[3]], w=GT_K)
        add("dve", lambda e: e.memset(gtab[:, 383:384], 0.0), w=GT_K)
        add("dve", lambda e: e.tensor_tensor(out=gtab[:, 0:383], in0=gtab[:, 0:383], in1=v8[:, 0:383], op=ALU.mult), r=GT_K + V8_K, w=GT_K)
        SC.new_sem("eb_w")
        SC.new_sem("eb_r")
        add("sp", lambda e: e.dma_start(out=ebscr.ap(), in_=gtab), r=GT_K, w=["ebscr"], dsem="eb_w")
        hbuf = th[:].rearrange("p (h t) -> p h t", h=8)
        for kb in range(2):
            src = bass.AP(tensor=ebscr, offset=(1 - kb) * P, ap=[[1, P], [384, 8], [1, P]])
            add("sp", lambda e, src=src: e.dma_start(out=hbuf, in_=src), r=["ebscr"], w=[("th", 0), ("th", 1)], dsem="eb_r")
            for q in range(2):
                add("pe", lambda e, q=q: e.matmul(ps[4 + q][:, :], lhsT=jrev, rhs=th[:, q * 512:(q + 1) * 512], start=True, stop=True),
                    r=[("th", 0), ("th", 1), "cmat_j"], w=[PSK[4 + q]])
                add("dve", lambda e, q=q, kb=kb: e.tensor_copy(out=EB[:, q * 4:(q + 1) * 4, kb, :], in_=ps[4 + q][:, :].rearrange("p (h t) -> p h t", h=4)),
                    r=[PSK[4 + q]], w=["EB"])

        for pq in ("00", "01", "10", "11"):
            SC.new_sem("xl" + pq)
        SC.new_sem("xst")
        SC.new_sem("xst0")
        SC.new_sem("xst1")
        SC.new_sem("bcs")
        if dbg:
            SC.new_sem("dbgs0")
            SC.new_sem("dbgs1")
        evq = [0]

        def xk(ti):
            return ("x", xs.cur, ti)

        def rmsnorm_rstd(ti, rcol):
            add("act", lambda e: e.activation(out=ysb[:], in_=xs[:, ti, :], func=AF.Square, accum_out=small[:, rcol:rcol + 1]),
                r=[xk(ti)], w=[("ysb", 0), ("ysb", 1), ("sm", rcol)])
            if RSTD_POW:
                add("dve", lambda e: e.tensor_scalar(out=small[:, rcol:rcol + 1], in0=small[:, rcol:rcol + 1], scalar1=1.0 / D, scalar2=EPS, op0=ALU.mult, op1=ALU.add),
                    r=[("sm", rcol)], w=[("sm", rcol)])
                add("pool", lambda e: e.tensor_tensor(out=small[:, rcol:rcol + 1], in0=small[:, rcol:rcol + 1], in1=negh, op=ALU.pow),
                    r=[("sm", rcol), "negh"], w=[("sm", rcol)])
            else:
                add("act", lambda e: e.activation(out=small[:, rcol:rcol + 1], in_=small[:, rcol:rcol + 1], func=AF.Ln, scale=1.0 / D, bias=EPS),
                    r=[("sm", rcol)], w=[("sm", rcol)])
                add("act", lambda e: e.activation(out=small[:, rcol:rcol + 1], in_=small[:, rcol:rcol + 1], func=AF.Exp, scale=-0.5),
                    r=[("sm", rcol)], w=[("sm", rcol)])

        def norm_to_hT(l, b, skip_stats=False):
            for ti in range(GT):
                if not skip_stats:
                    rmsnorm_rstd(ti, 2 + ti)
                if stop == "na":
                    continue
                scr, scrK = (xn, [("xn", 0), ("xn", 1)]) if ti == 0 else (bc, ["bc"])
                add("dve", lambda e, ti=ti, scr=scr: e.tensor_scalar(out=scr[:], in0=xs[:, ti, :], scalar1=small[:, 2 + ti:3 + ti], scalar2=None, op0=ALU.mult),
                    r=[xk(ti), ("sm", 2 + ti)], w=scrK)
                if stop == "nb":
                    continue
                for c in range(KC):
                    add("pe", lambda e, ti=ti, c=c, scr=scr: e.transpose(ps[c // 2][:, (c % 2) * G + ti * P:(c % 2) * G + (ti + 1) * P], scr[:, c * P:(c + 1) * P], ident),
                        r=scrK + ["cmat"], w=[PSK[c // 2]])
            if stop in ("na", "nb", "nc"):
                return
            for c in range(KC):
                src = ps[c // 2][:, (c % 2) * G:(c % 2 + 1) * G]
                if (c // 2) % 2 == 0:
                    add("act", lambda e, c=c, src=src: e.activation(out=hT[:, c, :], in_=src, func=AF.Identity, scale=gsc(l, c, b), bias=shiftc(l, c, b)),
                        r=[PSK[c // 2], "gsT", "modT"], w=[("hT", c)])
                else:
                    add("dve", lambda e, c=c, src=src: e.tensor_scalar(out=hT[:, c, :], in0=src, scalar1=gsc(l, c, b), scalar2=shiftc(l, c, b), op0=ALU.mult, op1=ALU.add),
                        r=[PSK[c // 2], "gsT", "modT"], w=[("hT", c)])

        def norm_tile_to_hT(l, b, ti, bank):
            rmsnorm_rstd(ti, 2 + ti)
            add("dve", lambda e: e.tensor_scalar(out=xn[:], in0=xs[:, ti, :], scalar1=small[:, 2 + ti:3 + ti], scalar2=None, op0=ALU.mult),
                r=[xk(ti), ("sm", 2 + ti)], w=[("xn", 0), ("xn", 1)])
            for half in range(2):
                for cc in range(4):
                    c = half * 4 + cc
                    add("pe", lambda e, c=c, cc=cc: e.transpose(ps[bank][:, cc * P:(cc + 1) * P], xn[:, c * P:(c + 1) * P], ident),
                        r=[("xn", 0), ("xn", 1), "cmat"], w=[PSK[bank]])
                for cc in range(4):
                    c = half * 4 + cc
                    src = ps[bank][:, cc * P:(cc + 1) * P]
                    if half == 0:
                        add("act", lambda e, c=c, src=src: e.activation(out=hT[:, c, ti * P:(ti + 1) * P], in_=src, func=AF.Identity, scale=gsc(l, c, b), bias=shiftc(l, c, b)),
                            r=[PSK[bank], "gsT", "modT"], w=[("hT", c)])
                    else:
                        add("dve", lambda e, c=c, src=src: e.tensor_scalar(out=hT[:, c, ti * P:(ti + 1) * P], in0=src, scalar1=gsc(l, c, b), scalar2=shiftc(l, c, b), op0=ALU.mult, op1=ALU.add),
                            r=[PSK[bank], "gsT", "modT"], w=[("hT", c)])

        HTK = [("hT", c) for c in range(KC)]

        def proj_fm(Wt, wkey, col0, M, evac):
            bi = 2 + (evq[0] % 2)
            evq[0] += 1
            for kc in range(KC):
                add("pe", lambda e, kc=kc, bi=bi: e.matmul(ps[bi][0:M, 0:G], lhsT=Wt[:, kc, col0:col0 + M], rhs=hT[:, kc, :], start=(kc == 0), stop=(kc == KC - 1)),
                    r=WK[wkey] + HTK, w=[PSK[bi]])
            evac(ps[bi][0:M, 0:G], PSK[bi])

        def load_bc(src_ap, rkeys):
            add("sp", lambda e: e.dma_start(out=bc[:], in_=src_ap), r=rkeys, w=["bc"], dsem="bcs")

        def residual_from(ti, hf, acc_bank):
            add("dve", lambda e: e.tensor_tensor(out=xn[:, hf * 512:(hf + 1) * 512], in0=ps[acc_bank][:, :], in1=bc[:, hf * 512:(hf + 1) * 512], op=ALU.mult),
                r=[PSK[acc_bank], "bc"], w=[("xn", hf)])
            add("dve", lambda e: e.tensor_tensor(out=xs[:, ti, hf * 512:(hf + 1) * 512], in0=xs[:, ti, hf * 512:(hf + 1) * 512], in1=xn[:, hf * 512:(hf + 1) * 512], op=ALU.add),
                r=[("xn", hf), xk(ti)], w=[xk(ti)])

        def out_proj_residual(Wt, wkey, srcT, skey_fn, ti):
            for hf in range(2):
                for kc in range(KC):
                    add("pe", lambda e, kc=kc, hf=hf: e.matmul(ps[hf][:, :], lhsT=srcT[:, kc, ti * P:(ti + 1) * P], rhs=Wt[:, kc, hf * 512:(hf + 1) * 512],
                                                               start=(kc == 0), stop=(kc == KC - 1)),
                        r=WK[wkey] + skey_fn(kc), w=[PSK[hf]])
                residual_from(ti, hf, hf)

        sbank = [0]
        pbq = [0]
        swq = [0]

        last_store = []
        pre_stats = [False]
        for b in range(nseq):
            add("pool", lambda e: e.memset(tails[:], 0.0), w=["tails"])
            add("pool", lambda e: e.memset(state[:], 0.0), w=["state"])
            add("pool", lambda e: e.memset(fcar[:, 0:1], 0.0), w=["fcar"])
            for g in range(ngrp):
                t0 = g * G
                gi = b * ngrp + g
                xs.cur = gi % 2

                def load_x(gj):
                    bj, gg = divmod(gj, ngrp)
                    keep = xs.cur
                    xs.cur = gj % 2
                    for ti in range(GT):
                        add("sp", lambda e, ti=ti: e.dma_start(out=xs[:, ti, :], in_=x_d[bj, gg * G + ti * P:gg * G + (ti + 1) * P, :]), w=[xk(ti)], dsem="xl%d%d" % (gj % 2, ti))
                    xs.cur = keep
                if gi == 0:
                    load_x(0)
                if gi + 1 < nseq * ngrp:
                    load_x(gi + 1)

                def store_x():
                    for ti in range(GT):
                        d = add("sp", lambda e, ti=ti: e.dma_start(out=out_d[b, t0 + ti * P:t0 + (ti + 1) * P, :], in_=xs[:, ti, :]), r=[xk(ti)], dsem="xst")
                    return [d]
                if stop == "pro":
                    last_store = store_x()
                    continue
                norm_to_hT(0, b, skip_stats=pre_stats[0])
                pre_stats[0] = False
                if gi == 0:
                    load_rest_of_weights()
                if stop in ("n0", "na", "nb", "nc"):
                    last_store = store_x()
                    continue
                load_bc(gscr.ap()[b, 0, :].partition_broadcast(P), ["gscr"])
                if stop == "n1":
                    last_store = store_x()
                    continue
                if g > 0:
                    add("pool", lambda e: e.tensor_copy(out=aKT[:, 0:P], in_=aKT[:, G:G + P]), r=["aK"], w=["aK"])
                    add("pool", lambda e: e.tensor_copy(out=aV[:, 0, :, 0:64], in_=aV[:, GT, :, 0:64]), r=[("aV", GT)], w=[("aV", 0)])
                proj_fm(W0, "W0", FL, 8, lambda src, k: add("act", lambda e: e.activation(out=frow[3], in_=src, func=AF.Exp, scale=-1.0, bias=nbf), r=[k, "nbf"], w=[FRK[3]]))
                softplus_from_e(frow[3], FRK[3], frow[2], FRK[2], frow[0], FRK[0], frow[1], FRK[1], frow[4], FRK[4])
                add("dve", lambda e: e.tensor_tensor_scan(out=frow[4], data0=ones8[:], data1=frow[2], initial=fcar[:, 0:1], op0=ALU.mult, op1=ALU.add),
                    r=[FRK[2], "ones8", "fcar"], w=[FRK[4]])
                add("dve", lambda e: e.tensor_copy(out=fcar[:, 0:1], in_=frow[4][:, G - 1:G]), r=[FRK[4]], w=["fcar"])
                for c in range(4):
                    proj_fm(W0, "W0", AQ + c * P, P, lambda src, k, c=c: add("act", lambda e: e.copy(out=aQT[:, c, :], in_=src), r=[k], w=[("aQ", c)]))
                proj_fm(W0, "W0", AK, P, lambda src, k: add("act", lambda e: e.copy(out=aKT[:, P:P + G], in_=src), r=[k], w=["aK"]))
                for c in range(4):
                    proj_fm(W0, "W0", BQ + c * P, P, lambda src, k, c=c: add("act", lambda e: e.copy(out=bQT[:, c, :], in_=src), r=[k], w=[("bQ", c)]))
                for c in range(4):
                    proj_fm(W0, "W0", BK + c * P, P, lambda src, k, c=c: add("act", lambda e: e.copy(out=bKT[:, c, t0:t0 + G], in_=src), r=[k], w=[("bK", c, g)]))
                for nl in range(GT):
                    add("pe", lambda e, nl=nl: e.transpose(ps[7][:, 300 + nl * 8:300 + nl * 8 + 8], frow[4][:, nl * P:(nl + 1) * P], ident[0:8, 0:8]),
                        r=[FRK[4], "cmat"], w=[PSK[7]])
                    add("dve", lambda e, nl=nl: e.tensor_scalar(out=fdiag[:, nl, :], in0=ident[0:8, 0:8], scalar1=frow[4][:, (nl + 1) * P - 1:(nl + 1) * P], scalar2=None, op0=ALU.mult),
                        r=[FRK[4], "cmat"], w=["fdiag"])
                add("dve", lambda e: e.tensor_copy(out=FnegT[:, g * GT:(g + 1) * GT, :], in_=ps[7][:, 300:300 + GT * 8].rearrange("p (n h) -> p n h", h=8)),
                    r=[PSK[7]], w=["FnegT"])
                add("pe", lambda e: e.matmul(ps[7][:, 400:400 + GT * 8], lhsT=ones8[:, 0:P], rhs=fdiag[:].rearrange("k n h -> k (n h)"), start=True, stop=True),
                    r=["ones8", "fdiag"], w=[PSK[7]])
                add("dve", lambda e: e.tensor_copy(out=FendBC[:], in_=ps[7][:, 400:400 + GT * 8].rearrange("p (n h) -> p n h", h=8)), r=[PSK[7]], w=["FendBC"])
                nj = (g + 1) * GT
                add("dve", lambda e, nj=nj: e.tensor_tensor(out=BT[:, 0:nj, :, :], in0=FnegT[:, 0:nj, :].unsqueeze(2).to_broadcast([P, nj, GT, 8]),
                                                            in1=FendBC[:].unsqueeze(1).to_broadcast([P, nj, GT, 8]), op=ALU.subtract),
                    r=["FnegT", "FendBC"], w=["BT", "cmat_j"])
                for ti in range(GT):
                    blk = g * GT + ti
                    bi = 2 + (evq[0] % 2)
                    evq[0] += 1
                    for kc in range(KC):
                        add("pe", lambda e, kc=kc, bi=bi, ti=ti: e.matmul(ps[bi][:, 0:P], lhsT=hT[:, kc, ti * P:(ti + 1) * P], rhs=W0[:, kc, AV:AV + P], start=(kc == 0), stop=(kc == KC - 1)),
                            r=WK["W0"] + HTK, w=[PSK[bi]])
                    add("act", lambda e, bi=bi, ti=ti: e.copy(out=aV[:, 1 + ti, :, 0:64], in_=ps[bi][:, 0:P].rearrange("p (g d) -> p g d", g=2)), r=[PSK[bi]], w=[("aV", 1 + ti)])
                    bi = 2 + (evq[0] % 2)
                    evq[0] += 1
                    for kc in range(KC):
                        add("pe", lambda e, kc=kc, bi=bi, ti=ti: e.matmul(ps[bi][:, :], lhsT=hT[:, kc, ti * P:(ti + 1) * P], rhs=W0[:, kc, BV:BV + 512], start=(kc == 0), stop=(kc == KC - 1)),
                            r=WK["W0"] + HTK, w=[PSK[bi]])
                    add("act", lambda e, bi=bi, blk=blk: e.copy(out=bV[:, blk, :, 0:64], in_=ps[bi][:, :].rearrange("p (h d) -> p h d", h=8)), r=[PSK[bi]], w=[("bV", blk)])

                def load_bc_l1():
                    load_bc(gscr.ap()[b, 1, :].partition_broadcast(P), ["gscr"])

                def make_block(nl):
                    n = g * GT + nl
                    q0 = nl * P
                    kbs = [1] if n == 0 else [0, 1]

                    def gate_mm():
                        for hf in range(2):
                            for kc in range(KC):
                                add("pe", lambda e, kc=kc, hf=hf: e.matmul(ps[hf][:, :], lhsT=hT[:, kc, q0:q0 + P], rhs=W0[:, kc, GA + hf * 512:GA + (hf + 1) * 512], start=(kc == 0), stop=(kc == KC - 1)),
                                    r=WK["W0"] + HTK, w=[PSK[hf]])

                    def gate_act():
                        for hf in range(2):
                            add("act", lambda e, hf=hf: e.activation(out=th[:, hf * 512:(hf + 1) * 512], in_=ps[hf][:, :], func=AF.Tanh, scale=0.5), r=[PSK[hf]], w=[("th", hf)])
                            add("dve", lambda e, hf=hf: e.scalar_tensor_tensor(out=th[:, hf * 512:(hf + 1) * 512], in0=th[:, hf * 512:(hf + 1) * 512], scalar=1.0, in1=ps[hf][:, :], op0=ALU.add, op1=ALU.mult),
                                r=[PSK[hf], ("th", hf)], w=[("th", hf)])

                    def swa_S(h):
                        bi = 4 + (sbank[0] % 2)
                        sbank[0] += 1
                        c, base = h % 4, (h // 4) * 64
                        for kb in kbs:
                            add("pe", lambda e, kb=kb, bi=bi: e.matmul(ps[bi][:, kb * P:(kb + 1) * P], lhsT=aKT[base:base + 64, (nl + kb) * P:(nl + kb + 1) * P],
                                                                      rhs=aQT[base:base + 64, c, q0:q0 + P], start=True, stop=True),
                                r=["aK", ("aQ", c)], w=[PSK[bi]])
                        return bi

                    def swa_rest(h, bi):
                        kvh = h // 4
                        si = swq[0] % 2
                        swq[0] += 1
                        k0, k1 = kbs[0], kbs[-1] + 1
                        add("act", lambda e: e.activation(out=swt[:, si, k0:k1, :], in_=ps[bi][:, k0 * P:k1 * P].rearrange("p (k t) -> p k t", t=P), func=AF.Exp, scale=0.125),
                            r=[PSK[bi]], w=[("swt", si)])
                        add("pool" if h % 2 == 0 else "dve", lambda e: e.tensor_tensor(out=pa[:, si, k0:k1, :], in0=swt[:, si, k0:k1, :], in1=EB[:, h, k0:k1, :], op=ALU.mult),
                            r=[("swt", si), "EB"], w=[("pa", si)])
                        pvb = 6 + h // 4
                        for kb in kbs:
                            add("pe", lambda e, kb=kb: e.matmul(ps[pvb][:, (h % 4) * 65:(h % 4) * 65 + 65], lhsT=pa[:, si, kb, :], rhs=aV[:, nl + kb, kvh, 0:65],
                                                                start=(kb == kbs[0]), stop=(kb == kbs[-1])),
                                r=[("pa", si), ("aV", nl + kb), "aVones"], w=[PSK[pvb]])

                    def normalise(dcol, sink):
                        for hb in range(2):
                            pv = ps[6 + hb][:, 0:260].rearrange("p (h d) -> p h d", d=65)
                            dn = den[:, dcol + hb * 4:dcol + hb * 4 + 4]
                            dk = ("den", dcol // 4 + hb)
                            if sink:
                                add("dve", lambda e: e.tensor_tensor(out=dn, in0=pv[:, :, 64], in1=lrp[:, 4, hb * 4:hb * 4 + 4], op=ALU.add), r=[PSK[6 + hb], "lrp4"], w=[dk])
                                add("dve", lambda e: e.reciprocal(out=dn, in_=dn), r=[dk], w=[dk])
                            else:
                                add("dve", lambda e: e.reciprocal(out=dn, in_=pv[:, :, 64]), r=[PSK[6 + hb]], w=[dk])
                            add("dve", lambda e: e.tensor_scalar(out=dn, in0=dn, scalar1=0.5, scalar2=None, op0=ALU.mult), r=[dk], w=[dk])
                            add("dve", lambda e: e.tensor_tensor(out=osb[:, hb * 256:(hb + 1) * 256].rearrange("p (h d) -> p h d", d=64), in0=pv[:, :, 0:64],
                                                                in1=dn.unsqueeze(2).to_broadcast([P, 4, 64]), op=ALU.mult),
                                r=[PSK[6 + hb], dk], w=[("osb", hb)])

                    def swa():
                        prev = swa_S(0)
                        for h in range(8):
                            nxt = swa_S(h + 1) if h + 1 < 8 else None
                            swa_rest(h, prev)
                            prev = nxt
                        normalise(0, True)
                        add("pool", lambda e: e.tensor_tensor(out=ysb[:, 0:512], in0=th[:, 0:512], in1=osb[:], op=ALU.mult), r=[("th", 0), ("osb", 0), ("osb", 1)], w=[("ysb", 0)])

                    def fox_S(ch):
                        h, j0, j1 = ch
                        bi = 4 + (sbank[0] % 2)
                        sbank[0] += 1
                        c, base = h // 2, (h % 2) * 64
                        for j in range(j0, j1):
                            add("pe", lambda e, j=j, bi=bi: e.matmul(ps[bi][:, (j - j0) * P:(j - j0 + 1) * P], lhsT=bKT[base:base + 64, c, j * P:(j + 1) * P],
                                                                    rhs=bQT[base:base + 64, c, q0:q0 + P], start=True, stop=True),
                                r=[("bK", c, j // GT), ("bQ", c)], w=[PSK[bi]])
                        return bi

                    def fox_rest(ch, bi):
                        h, j0, j1 = ch
                        pvb = 6 + h // 4
                        for j in range(j0, j1):
                            si = pbq[0] % NPB
                            pbq[0] += 1
                            add("act", lambda e, j=j, si=si: e.activation(out=pb[:, si, :], in_=ps[bi][:, (j - j0) * P:(j - j0 + 1) * P], func=AF.Exp, scale=0.125, bias=BT[:, j, nl, h:h + 1]),
                                r=[PSK[bi], "BT"], w=[("pb", si)])
                            if j == n:
                                add("pool", lambda e, si=si: e.tensor_tensor(out=pb[:, si, :], in0=pb[:, si, :], in1=cmaskb[:], op=ALU.mult), r=[("pb", si), "cmaskb"], w=[("pb", si)])
                            add("pe", lambda e, j=j, si=si: e.matmul(ps[pvb][:, (h % 4) * 65:(h % 4) * 65 + 65], lhsT=pb[:, si, :], rhs=bV[:, j, h, 0:65], start=(j == 0), stop=(j == n)),
                                r=[("pb", si), ("bV", j), "bVones"], w=[PSK[pvb]])

                    def fox():
                        chunks = []
                        for h in range(8):
                            for j0 in range(0, n + 1, 4):
                                chunks.append((h, j0, min(n + 1, j0 + 4)))
                        prev = fox_S(chunks[0])
                        for i, ch in enumerate(chunks):
                            nxt = fox_S(chunks[i + 1]) if i + 1 < len(chunks) else None
                            fox_rest(ch, prev)
                            prev = nxt
                        normalise(8, False)
                        add("pool", lambda e: e.tensor_tensor(out=ysb[:, 512:1024], in0=th[:, 512:1024], in1=osb[:], op=ALU.mult), r=[("th", 1), ("osb", 0), ("osb", 1)], w=[("ysb", 1)])

                    def tail_a():
                        ytp = ps[2][:, :].bitcast(BF16)
                        for kc in range(KC):
                            add("pe", lambda e, kc=kc: e.transpose(ytp[:, kc * P:(kc + 1) * P], ysb[:, kc * P:(kc + 1) * P], identb[:]),
                                r=[("ysb", kc // 4), "identb"], w=[PSK[2]])
                        add("dve", lambda e: e.tensor_copy(out=yT[:, :, q0:q0 + P], in_=ytp.rearrange("p (k t) -> p k t", t=P)), r=[PSK[2]], w=[("yT", nl)] + Y1K)

                    def tail_b():
                        out_proj_residual(W1, "W1", yT, lambda kc: [("yT", nl)], nl)

                    return gate_mm, gate_act, swa, fox, tail_a, tail_b

                Y1K = [("y1", c) for c in range(KC)]
                blocks = [make_block(nl) for nl in range(GT)]
                for nl in range(GT):
                    gm, ga, sw, fx, ta, tb = blocks[nl]
                    if nl == 0:
                        gm()
                        ga()
                    sw()
                    fx()
                    if nl + 1 < GT:
                        blocks[nl + 1][0]()
                    ta()
                    if nl + 1 < GT:
                        blocks[nl + 1][1]()
                    tb()
                    if EARLY_L1_NORM and stop is None:
                        norm_tile_to_hT(1, b, nl, 3)
                        if nl == GT - 1:
                            load_bc_l1()

                if dbg:
                    for ti in range(GT):
                        add("sp", lambda e, ti=ti: e.dma_start(out=dbg_d[b, t0 + ti * P:t0 + (ti + 1) * P, :], in_=xs[:, ti, :]), r=[xk(ti)], dsem="dbgs%d" % ti)

                if stop == "l0":
                    last_store = store_x()
                    continue
                if not (EARLY_L1_NORM and stop is None):
                    norm_to_hT(1, b)
                    load_bc_l1()
                YTK = [("yT", nl) for nl in range(GT)]

                def l1_A(c, T):
                    bA = T["banks"][0]
                    for kc in range(KC):
                        add("pe", lambda e, kc=kc: e.matmul(ps[bA][:, 0:G], lhsT=W2[:, kc, c * P:(c + 1) * P], rhs=hT[:, kc, :], start=(kc == 0), stop=(kc == KC - 1)),
                            r=WK["W2"] + HTK, w=[PSK[bA]])
                    for kc in range(KC):
                        add("pe", lambda e, kc=kc: e.matmul(ps[bA][:, G:2 * G], lhsT=W2[:, kc, D + c * P:D + (c + 1) * P], rhs=hT[:, kc, :], start=(kc == 0), stop=(kc == KC - 1)),
                            r=WK["W2"] + HTK, w=[PSK[bA]])

                def l1_B1(c, T):
                    bA = T["banks"][0]
                    add("pool", lambda e: e.tensor_copy(out=T["xrb"][:, 0:3], in_=tails[:, c, 0:3]), r=["tails"], w=[T["xrtK"]])
                    add("act", lambda e: e.copy(out=T["xrb"][:, 3:3 + G], in_=ps[bA][:, 0:G]), r=[PSK[bA]], w=[T["xrbK"]])
                    add("pool", lambda e: e.tensor_copy(out=tails[:, c, 0:3], in_=T["xrb"][:, G:G + 3]), r=[T["xrbK"]], w=["tails"])
                    add("act", lambda e: e.activation(out=T["sg"], in_=ps[bA][:, G:2 * G], func=AF.Tanh, scale=0.5), r=[PSK[bA]], w=[T["sgK"]])

                def l1_B2(c, T):
                    bA = T["banks"][0]
                    add("dve", lambda e: e.scalar_tensor_tensor(out=T["sg"], in0=T["sg"], scalar=1.0, in1=ps[bA][:, G:2 * G], op0=ALU.add, op1=ALU.mult), r=[PSK[bA], T["sgK"]], w=[T["sgK"]])
                    cw = lambda j: colp[:, CP_CW + c * 4 + j:CP_CW + c * 4 + j + 1]
                    add("dve", lambda e: e.tensor_scalar(out=T["xc"], in0=T["xrb"][:, 0:G], scalar1=cw(0), scalar2=colp[:, CP_CB + c:CP_CB + c + 1], op0=ALU.mult, op1=ALU.add),
                        r=[T["xrbK"], T["xrtK"], "colp"], w=[T["xcK"]])
                    for j in range(1, 4):
                        add("dve", lambda e, j=j: e.scalar_tensor_tensor(out=T["xc"], in0=T["xrb"][:, j:j + G], scalar=cw(j), in1=T["xc"], op0=ALU.mult, op1=ALU.add),
                            r=[T["xrbK"], T["xrtK"], "colp", T["xcK"]], w=[T["xcK"]])
                    add("dve", lambda e: e.tensor_copy(out=T["xcb"], in_=T["xc"]), r=[T["xcK"]], w=[T["xcbK"]])

                def l1_C(c, T):
                    bB = T["banks"][1]
                    add("pe", lambda e: e.matmul(ps[bB][:, 0:G], lhsT=W4[:, 0, c, :], rhs=T["xcb"], start=True, stop=True), r=WK["W4"] + [T["xcbK"]], w=[PSK[bB]])
                    add("pe", lambda e: e.matmul(ps[bB][:, G:2 * G], lhsT=W4[:, 1, c, :], rhs=T["xcb"], start=True, stop=True), r=WK["W4"] + [T["xcbK"]], w=[PSK[bB]])

                def l1_D1(c, T):
                    bB = T["banks"][1]
                    add("act", lambda e: e.activation(out=T["tha"], in_=ps[bB][:, 0:G], func=AF.Tanh, scale=0.5, bias=lrp[:, 2, c:c + 1]), r=[PSK[bB]] + LRK, w=[T["thaK"]])
                    add("act", lambda e: e.activation(out=T["thx"], in_=ps[bB][:, G:2 * G], func=AF.Tanh, scale=0.5, bias=lrp[:, 3, c:c + 1]), r=[PSK[bB]] + LRK, w=[T["thxK"]])

                def l1_D2(c, T):
                    add("act", lambda e: e.activation(out=T["ab"], in_=T["tha"], func=AF.Exp, scale=lrp[:, 1, c:c + 1], bias=lrp[:, 1, c:c + 1]), r=[T["thaK"]] + LRK, w=[T["abK"]])
                    add("act", lambda e: e.activation(out=T["tha"], in_=T["tha"], func=AF.Exp, scale=lrp[:, 0, c:c + 1], bias=lrp[:, 0, c:c + 1]), r=[T["thaK"]] + LRK, w=[T["thaK"]])
                    add("dve", lambda e: e.scalar_tensor_tensor(out=T["thx"], in0=T["thx"], scalar=1.0, in1=T["xc"], op0=ALU.add, op1=ALU.mult), r=[T["thxK"], T["xcK"]], w=[T["thxK"]])

                def l1_D3(c, T):
                    add("act", lambda e: e.activation(out=T["tha"], in_=T["tha"], func=AF.Sqrt, scale=-1.0, bias=1.0), r=[T["thaK"]], w=[T["thaK"]])

                def l1_D4(c, T):
                    add("dve", lambda e: e.scalar_tensor_tensor(out=T["tha"], in0=T["tha"], scalar=0.5, in1=T["thx"], op0=ALU.mult, op1=ALU.mult), r=[T["thaK"], T["thxK"]], w=[T["thaK"]])
                    add("dve", lambda e: e.tensor_tensor_scan(out=T["xc"], data0=T["ab"], data1=T["tha"], initial=state[:, c:c + 1], op0=ALU.mult, op1=ALU.add),
                        r=[T["abK"], T["thaK"], "state", T["thxK"]], w=[T["xcK"]])
                    add("pool", lambda e: e.tensor_copy(out=state[:, c:c + 1], in_=T["xc"][:, G - 1:G]), r=[T["xcK"]], w=["state"])
                    add("dve", lambda e: e.scalar_tensor_tensor(out=yT[:, c, :], in0=T["xc"], scalar=0.5, in1=T["sg"], op0=ALU.mult, op1=ALU.mult),
                        r=[T["xcK"], T["sgK"]], w=YTK + [("y1", c)])

                ACCB = (0, 1, 4, 5)

                def l1_O(c, T):
                    for ti in range(GT):
                        for hf in range(2):
                            ab_ = ACCB[ti * 2 + hf]
                            add("pe", lambda e, ti=ti, hf=hf, ab_=ab_: e.matmul(ps[ab_][:, :], lhsT=yT[:, c, ti * P:(ti + 1) * P], rhs=W3[:, c, hf * 512:(hf + 1) * 512],
                                                                           start=(c == 0), stop=(c == KC - 1)),
                                r=WK["W3"] + [("y1", c)], w=[PSK[ab_]])

                pairs = [(2 * k, 2 * k + 1) for k in range(KC // 2)]

                def stage(fn, pr):
                    for i, c in enumerate(pr):
                        fn(c, L1S[i])

                stage(l1_A, pairs[0])
                if stop is None and gi + 1 < nseq * ngrp:
                    xs.cur = (gi + 1) % 2
                    for ti in range(GT):
                        rmsnorm_rstd(ti, 2 + ti)
                    xs.cur = gi % 2
                    pre_stats[0] = True
                for k, pr in enumerate(pairs):
                    stage(l1_B1, pr)
                    stage(l1_B2, pr)
                    stage(l1_C, pr)
                    stage(l1_D1, pr)
                    if k + 1 < len(pairs):
                        stage(l1_A, pairs[k + 1])
                    stage(l1_D2, pr)
                    stage(l1_D3, pr)
                    stage(l1_D4, pr)
                    stage(l1_O, pr)
                if stop == "l1":
                    for ti in range(GT):
                        for hf in range(2):
                            residual_from(ti, hf, ACCB[ti * 2 + hf])
                    last_store = store_x()
                    continue
                for ti in range(GT):
                    for hf in range(2):
                        residual_from(ti, hf, ACCB[ti * 2 + hf])
                load_bc(fg_d.ap()[0, :].partition_broadcast(P), [])
                fin = [(th[:], [("th", 0), ("th", 1)]), (l1b[:, 0:D], ["l1xc", "l1tha", "l1thx", "l1ab"])]
                for ti in range(GT):
                    rmsnorm_rstd(ti, 4 + ti)
                    fo, fk = fin[ti]
                    add("dve", lambda e, ti=ti, fo=fo: e.scalar_tensor_tensor(out=fo, in0=xs[:, ti, :], scalar=small[:, 4 + ti:5 + ti], in1=bc[:], op0=ALU.mult, op1=ALU.mult),
                        r=[xk(ti), ("sm", 4 + ti), "bc"], w=fk)
                    add("sp", lambda e, ti=ti, fo=fo: e.dma_start(out=out_d[b, t0 + ti * P:t0 + (ti + 1) * P, :], in_=fo), r=fk, dsem="xst%d" % ti)
                    last_store = [("xst0", SC.cnt["xst0"]), ("xst1", SC.cnt["xst1"])]
        finals = list(last_store)
        if dbg:
            finals.append(("dbgs0", SC.cnt["dbgs0"]))
            finals.append(("dbgs1", SC.cnt["dbgs1"]))
        SC.run(final_conds=finals)
    return nc


def _t5_bucket(rel):
    n = np.maximum(rel, 0)
    nf = np.maximum(n, 1).astype(np.float32)
    large = 16 + (np.log(nf / np.float32(16)) / np.float32(math.log(128 / 16)) * np.float32(32 - 16)).astype(np.int32)
    large = np.minimum(large, 31)
    return np.where(n < 16, n, large)


def _host_layout(inp, nseq_core=SEQ_PER_CORE, ncores=NCORES):
    f = lambda a: np.ascontiguousarray(np.asarray(a, dtype=np.float32))
    w_in = f(inp["attn_w_in"])[0]
    cols = np.concatenate([
        np.concatenate([np.concatenate([np.arange(c * 64, (c + 1) * 64), np.arange((4 + c) * 64, (5 + c) * 64)]) for c in range(4)]),
        np.arange(512, 640),
        np.arange(768, 1280),
        np.arange(1280, 1792),
        np.arange(2304, 2312),
        np.arange(640, 768),
        np.arange(1792, 2304),
        np.arange(2312, 3336),
    ])
    assert cols.size == C0
    kcl = lambda w: np.ascontiguousarray(w.reshape(KC, P, w.shape[1]).transpose(1, 0, 2))
    shared = {
        "w0": kcl(w_in[:, cols]),
        "w1": kcl(f(inp["attn_w_out"])[0]),
        "w2": kcl(f(inp["lru_w_in"])[0]),
        "w3": kcl(f(inp["lru_w_out"])[0]),
        "w4": np.ascontiguousarray(np.stack([f(inp["lru_w_a"])[0], f(inp["lru_w_x"])[0]], 0).transpose(2, 0, 1, 3)),
        "ada_w": f(inp["ada_w"]),
        "final_g": f(inp["final_g"]).reshape(1, D),
    }
    colv = lambda v: np.ascontiguousarray(v.reshape(KC, P).T)
    colp = np.zeros((P, CP_N), np.float32)
    ng = f(inp["norm_g"])
    ab = f(inp["ada_b"])
    for l in range(2):
        colp[:, CP_NG + l * 8:CP_NG + (l + 1) * 8] = colv(ng[l])
        for part in range(3):
            colp[:, CP_AB + (l * 3 + part) * 8:CP_AB + (l * 3 + part + 1) * 8] = colv(ab[l, part * D:(part + 1) * D])
    cw = f(inp["lru_conv_w"])[0]
    for j in range(4):
        colp[:, CP_CW + j:CP_CW + 32:4] = colv(cw[j])
    colp[:, CP_CB:CP_CB + 8] = colv(f(inp["lru_conv_b"])[0])
    colp[:, CP_BA:CP_BA + 8] = colv(f(inp["lru_b_a"])[0])
    colp[:, CP_BX:CP_BX + 8] = colv(f(inp["lru_b_x"])[0])
    colp[:, CP_LAM:CP_LAM + 8] = colv(f(inp["lru_lambda"])[0])
    colp[:, CP_SINK:CP_SINK + 8] = np.broadcast_to(f(inp["attn_sinks"])[0][None, :], (P, 8))
    eye = np.eye(P, dtype=np.float32)
    sidx = np.arange(P)
    cmask = (sidx[:, None] <= sidx[None, :]).astype(np.float32)
    shared["cmat"] = np.ascontiguousarray(np.concatenate([eye, eye[::-1], cmask], axis=1))
    rel = np.arange(383) - 127
    onehot = np.zeros((32, 383), np.float32)
    onehot[_t5_bucket(rel), np.arange(383)] = 1.0
    valid = ((rel >= 0) & (rel < 128)).astype(np.float32)
    shared["rb"] = np.ascontiguousarray(np.concatenate([f(inp["rel_bias"]), onehot], axis=1))
    v8 = np.zeros((8, 384), np.float32)
    v8[:, 0:383] = valid[None, :]
    v8[:, 383] = f(inp["attn_b_f"])[0]
    shared["v8"] = v8
    x = f(inp["x"])
    c = f(inp["c"])
    maps = []
    for core in range(ncores):
        b0 = core * nseq_core
        m = dict(shared)
        m["x"] = x[b0:b0 + nseq_core]
        cp_ = colp.copy()
        ct = c[b0:b0 + nseq_core].reshape(nseq_core, KC, P).transpose(2, 1, 0)
        ctf = np.zeros((P, KC, 4), np.float32)
        ctf[:, :, 0:nseq_core] = ct
        cp_[:, CP_CT:CP_CT + KC * 4] = ctf.reshape(P, KC * 4)
        m["colp"] = cp_
        maps.append(m)
    return maps


def kernel(**inputs):
    maps = _host_layout(inputs)
    nc = build()
    res = run_bass_kernel_spmd(nc, maps, core_ids=list(range(NCORES)))
    out = np.concatenate([np.asarray(r["out"], dtype=np.float32) for r in res.results], axis=0)
    return out
```

```python
import math
from contextlib import ExitStack

import numpy as np
import concourse.bass as bass
import concourse.mybir as mybir
from concourse.bass_utils import run_bass_kernel_spmd

F32 = mybir.dt.float32
BF16 = mybir.dt.bfloat16
AF = mybir.ActivationFunctionType
ALU = mybir.AluOpType

NCORES = 8
P = 128
D = 1024
KC = 8
S = 2048
G = 256
GT = G // P
NB = S // P
NGRP = S // G
SEQ_PER_CORE = 4
EPS = 1e-6
VW = 66

AQ, AK, BQ, BK, FL, AV, BV, GA, C0 = 0, 512, 640, 1152, 1664, 1672, 1800, 2312, 3336
CP_NG, CP_AB, CP_CW, CP_CB, CP_BA, CP_BX, CP_LAM, CP_SINK, CP_CT, CP_N = 0, 16, 64, 96, 104, 112, 120, 128, 136, 168

SAME_ENGINE_SYNC = True
RSTD_POW = True
EARLY_L1_NORM = False


class _Rec:
    def __init__(self):
        self.call = None

    def __getattr__(self, name):
        def f(*a, **k):
            self.call = (name, a, k)
            return self
        return f


class _XS:
    def __init__(self, bufs):
        self.bufs = bufs
        self.cur = 0

    def __getitem__(self, idx):
        return self.bufs[self.cur][idx]


class Sched:
    STREAMS = ("pe", "act", "dve", "pool", "sp")

    def __init__(self, nc, es):
        self.nc = nc
        self.es = es
        self.sems = {}
        self.cnt = {}
        self.ops = {s: [] for s in self.STREAMS}
        self.lastw = {}
        self.rds = {}
        self.excl = set()
        for s in ("pe", "act", "dve", "pool"):
            self.new_sem("E_" + s)

    def new_sem(self, name):
        self.sems[name] = self.es.enter_context(self.nc.semaphore(name))
        self.cnt[name] = 0
        return name

    def add(self, stream, fn, r=(), w=(), dsem=None, extra=()):
        deps = set(extra)
        own = dsem if dsem is not None else "E_" + stream
        for k in r:
            if k in self.lastw:
                deps.add(self.lastw[k])
            if k in self.excl:
                deps.update(d for d in self.rds.get(k, ()) if d[0] != own)
        for k in w:
            if k in self.lastw:
                deps.add(self.lastw[k])
            deps.update(self.rds.get(k, ()))
        if dsem is None:
            sname = "E_" + stream
            self.cnt[sname] += 1
            inc = 1
        else:
            sname = dsem
            self.cnt[sname] += 16
            inc = 16
        done = (sname, self.cnt[sname])
        for k in r:
            self.rds.setdefault(k, []).append(done)
        for k in w:
            self.lastw[k] = done
            self.rds[k] = []
        rec = _Rec()
        fn(rec)
        assert rec.call is not None
        self.ops[stream].append((deps, rec.call, sname, inc))
        return done

    def emit(self, stream, eng):
        known = {}
        own = "E_" + stream
        for deps, fn, sname, inc in self.ops[stream]:
            best = {}
            for (s, v) in deps:
                if v > best.get(s, 0):
                    best[s] = v
            for s, v in best.items():
                if s == own and (stream == "pe" or not SAME_ENGINE_SYNC):
                    continue
                if known.get(s, 0) >= v:
                    continue
                eng.wait_ge(self.sems[s], v)
                known[s] = v
            name, a, k = fn
            ins = getattr(eng, name)(*a, **k)
            ins.then_inc(self.sems[sname], inc)

    def run(self, final_conds=()):
        nc = self.nc
        with nc.Block() as block:
            @block.tensor
            def _(e):
                self.emit("pe", e)

            @block.scalar
            def _(e):
                self.emit("act", e)

            @block.vector
            def _(e):
                self.emit("dve", e)

            @block.gpsimd
            def _(e):
                self.emit("pool", e)

            @block.sync
            def _(e):
                self.emit("sp", e)
                best = {}
                for (s, v) in final_conds:
                    best[s] = max(best.get(s, 0), v)
                for s, v in best.items():
                    e.wait_ge(self.sems[s], v)


def build(nseq=SEQ_PER_CORE, ngrp=NGRP, dbg=False, stop=None):
    nc = bass.Bass("TRN2", target_bir_lowering=False, dynamic_dma_scratch_size=2048)
    x_d = nc.dram_tensor("x", [nseq, S, D], F32, kind="ExternalInput").ap()
    out_d = nc.dram_tensor("out", [nseq, S, D], F32, kind="ExternalOutput").ap()
    w0_d = nc.dram_tensor("w0", [P, KC, C0], F32, kind="ExternalInput").ap()
    w1_d = nc.dram_tensor("w1", [P, KC, D], F32, kind="ExternalInput").ap()
    w2_d = nc.dram_tensor("w2", [P, KC, 2 * D], F32, kind="ExternalInput").ap()
    w3_d = nc.dram_tensor("w3", [P, KC, D], F32, kind="ExternalInput").ap()
    w4_d = nc.dram_tensor("w4", [P, 2, KC, P], F32, kind="ExternalInput").ap()
    ada_d = nc.dram_tensor("ada_w", [2, D, 3 * D], F32, kind="ExternalInput").ap()
    colp_d = nc.dram_tensor("colp", [P, CP_N], F32, kind="ExternalInput").ap()
    cmat_d = nc.dram_tensor("cmat", [P, 3 * P], F32, kind="ExternalInput").ap()
    rb_d = nc.dram_tensor("rb", [32, 8 + 383], F32, kind="ExternalInput").ap()
    v8_d = nc.dram_tensor("v8", [8, 384], F32, kind="ExternalInput").ap()
    fg_d = nc.dram_tensor("final_g", [1, D], F32, kind="ExternalInput")
    gscr = nc.dram_tensor("gscr", [4, 2, D], F32, kind="Internal")
    ebscr = nc.dram_tensor("ebscr", [8, 384], F32, kind="Internal")
    if dbg:
        dbg_d = nc.dram_tensor("dbg", [nseq, S, D], F32, kind="ExternalOutput").ap()

    with ExitStack() as es:
        SC = Sched(nc, es)
        add = SC.add

        def sb(name, shape, dt):
            return es.enter_context(nc.sbuf_tensor("s_" + name, shape, dt))

        W0 = sb("W0", [P, KC, C0], BF16)
        W1 = sb("W1", [P, KC, D], BF16)
        W2 = sb("W2", [P, KC, 2 * D], BF16)
        W3 = sb("W3", [P, KC, D], BF16)
        W4 = sb("W4", [P, 2, KC, P], BF16)
        bKT = sb("bKT", [P, 4, S], BF16)
        bV = sb("bV", [P, NB, 8, VW], BF16)
        aKT = sb("aKT", [P, P + G], BF16)
        aV = sb("aV", [P, GT + 1, 2, VW], BF16)
        xs = _XS([sb("xs0", [P, GT, D], F32), sb("xs1", [P, GT, D], F32)])
        xn = sb("xn", [P, D], F32)
        bc = sb("bc", [P, D], F32)
        hT = sb("hT", [P, KC, G], BF16)
        yT = sb("yT", [P, KC, G], BF16)
        aQT = sb("aQT", [P, 4, G], BF16)
        bQT = sb("bQT", [P, 4, G], BF16)
        th = sb("th", [P, D], F32)
        osb = sb("osb", [P, 512], F32)
        ysb = sb("ysb", [P, D], BF16)
        NPB = 8
        pb = sb("pb", [P, NPB, P], BF16)
        swt = sb("swt", [P, 2, 2, P], F32)
        pa = sb("pa", [P, 2, 2, P], BF16)
        EB = sb("EB", [P, 8, 2, P], BF16)
        colp = sb("colp", [P, CP_N], F32)
        cmat = sb("cmat", [P, 3 * P], F32)
        identb = sb("identb", [P, P], BF16)
        cmaskb = sb("cmaskb", [P, P], BF16)
        rbs = ysb[:].bitcast(F32)[0:32, 0:8 + 383]
        RBS_K = [("ysb", 0), ("ysb", 1)]
        v8 = pb[:].rearrange("p a b -> p (a b)").bitcast(F32)[0:8, 0:384]
        V8_K = [("pb", i) for i in range(NPB)]
        gtab = swt[:].rearrange("p a b c -> p (a b c)")[0:8, 0:384]
        GT_K = [("swt", 0), ("swt", 1)]
        scT = sb("scT", [P, KC, 4], F32)
        modT = sb("modT", [P, 48, 4], F32)
        gsT = sb("gsT", [P, 2, 8, 4], F32)
        small = sb("small", [P, 64], F32)
        lrp = sb("lrp", [P, 5, 8], F32)
        spt = sb("spt", [P, 5, 8], F32)
        FnegT = sb("FnegT", [P, NB, 8], F32)
        FendBC = sb("FendBC", [P, GT, 8], F32)
        l1b = sb("l1b", [P, 5 * G], F32)
        xrb1 = sb("xrb1", [P, G + 4], F32)
        xcb1 = sb("xcb1", [P, G], BF16)
        fdiag = sb("fdiag", [8, GT, 8], F32)
        ones8 = sb("ones8", [8, G], F32)
        fcar = sb("fcar", [8, 2], F32)
        den = sb("den", [P, 16], F32)
        tails = sb("tails", [P, KC, 4], F32)
        state = sb("state", [P, KC], F32)
        xrb = sb("xrb", [P, G + 4], F32)
        xcb = sb("xcb", [P, G], BF16)
        L1S = [
            dict(xc=osb[:, 0:G], xcK=("osb", 0), tha=th[:, 0:G], thaK=("th", 0), thx=th[:, G:2 * G], thxK=("th", 0),
                 ab=th[:, 2 * G:3 * G], abK=("th", 1), sg=th[:, 3 * G:4 * G], sgK=("th", 1),
                 xrb=xrb[:], xrbK="xrb0", xrtK="xrbt0", xcb=xcb[:], xcbK="xcb0", banks=(2, 3)),
            dict(xc=l1b[:, 0:G], xcK="l1xc", tha=l1b[:, G:2 * G], thaK="l1tha", thx=l1b[:, 2 * G:3 * G], thxK="l1thx",
                 ab=l1b[:, 3 * G:4 * G], abK="l1ab", sg=l1b[:, 4 * G:5 * G], sgK="l1sg",
                 xrb=xrb1[:], xrbK="xrb1", xrtK="xrbt1", xcb=xcb1[:], xcbK="xcb1", banks=(6, 7)),
        ]
        frow = [th[0:8, 0:G], th[0:8, G:2 * G], th[0:8, 2 * G:3 * G], th[0:8, 3 * G:4 * G], osb[0:8, 0:G]]
        FRK = [("th", 0), ("th", 0), ("th", 1), ("th", 1), ("osb", 0)]

        ps = [es.enter_context(nc.psum_tensor("psum%d" % i, [P, 512], F32)) for i in range(8)]
        PSK = ["ps%d" % i for i in range(8)]
        SC.excl.update(PSK)

        ident = cmat[:, 0:P]
        BT = cmat[:, P:3 * P].rearrange("p (j n h) -> p j n h", n=GT, h=8)
        jrev = cmat[:, P:2 * P]
        cmaskf = cmat[:, 2 * P:3 * P]

        def cp(off, n):
            return colp[:, off:off + n]

        for nm, dst, src, wk in (("colp", colp[:], colp_d, ["colp"]), ("cmat", cmat[:], cmat_d, ["cmat", "cmat_j"]), ("rbs", rbs, rb_d, RBS_K), ("v8", v8, v8_d, V8_K)):
            sem = SC.new_sem("ld_" + nm)
            add("sp", lambda e, dst=dst, src=src: e.dma_start(out=dst, in_=src), w=wk, dsem=sem)

        NLANE = 6
        for i in range(NLANE):
            SC.new_sem("wl%d" % i)
        WK = {nm: [] for nm in ("W0", "W1", "W2", "W3", "W4")}
        lane_last = [None] * NLANE
        wq = [0]

        def wload(nm, dst, src):
            i = wq[0]
            wq[0] += 1
            key = (nm, len(WK[nm]))
            WK[nm].append(key)
            ln = i % NLANE
            ex = [lane_last[ln]] if lane_last[ln] is not None else []
            lane_last[ln] = add("pool", lambda e: e.dma_start(out=dst, in_=src), w=[key], dsem="wl%d" % ln, extra=ex)

        for kc in range(KC):
            for (c0, c1) in ((0, 1668), (1668, C0)):
                wload("W0", W0[:, kc, c0:c1], w0_d[:, kc, c0:c1])
        for kc in range(0, KC, 2):
            wload("W1", W1[:, kc:kc + 2, :], w1_d[:, kc:kc + 2, :])
        for kc in range(KC):
            wload("W2", W2[:, kc, :], w2_d[:, kc, :])
        for kc in range(0, KC, 2):
            wload("W3", W3[:, kc:kc + 2, :], w3_d[:, kc:kc + 2, :])
        for i in range(2):
            for k2 in range(0, KC, 2):
                wload("W4", W4[:, i, k2:k2 + 2, :], w4_d[:, i, k2:k2 + 2, :])

        add("dve", lambda e: e.tensor_copy(out=identb[:], in_=ident), r=["cmat"], w=["identb"])
        add("dve", lambda e: e.tensor_copy(out=cmaskb[:], in_=cmaskf), r=["cmat_j"], w=["cmaskb"])
        add("dve", lambda e: e.memset(ones8[:], 1.0), w=["ones8"])
        add("dve", lambda e: e.memset(small[:, 0:1], -0.5), w=["negh"])
        add("dve", lambda e: e.memset(bV[:, :, :, 64:VW], 1.0), w=["bVones"])
        add("dve", lambda e: e.memset(aV[:, :, :, 64:VW], 1.0), w=["aVones"])
        negh = small[:, 0:1]

        cT = cp(CP_CT, 32)
        scf = scT[:].rearrange("p k b -> p (k b)")
        add("act", lambda e: e.activation(out=scf, in_=cT, func=AF.Tanh, scale=0.5), r=["colp"], w=["scT"])
        add("dve", lambda e: e.scalar_tensor_tensor(out=scf, in0=scf, scalar=1.0, in1=cT, op0=ALU.add, op1=ALU.mult), r=["scT", "colp"], w=["scT"])
        add("dve", lambda e: e.tensor_scalar(out=scf, in0=scf, scalar1=0.5, scalar2=None, op0=ALU.mult), r=["scT"], w=["scT"])

        stag = [(xn[:].rearrange("p (k j) -> p k j", k=KC), ("xn", 0)), (bc[:].rearrange("p (k j) -> p k j", k=KC), "bc"),
                (th[:].rearrange("p (k j) -> p k j", k=KC), ("th", 0)), (l1b[:, 0:D].rearrange("p (k j) -> p k j", k=KC), "l1xc")]
        NST = len(stag)
        for i in range(NST):
            SC.new_sem("ad%d" % i)
        idx = 0
        for l in range(2):
            for part in range(3):
                for c in range(KC):
                    slot, skey = stag[idx % NST]
                    col0 = part * D + c * P
                    src = ada_d[l, :, col0:col0 + P].rearrange("(k p) j -> p k j", p=P)
                    add("sp", lambda e, slot=slot, src=src: e.dma_start(out=slot, in_=src), w=[skey], dsem="ad%d" % (idx % NST))
                    for kc in range(KC):
                        add("pe", lambda e, slot=slot, kc=kc, idx=idx: e.matmul(ps[2][:, idx * 4:idx * 4 + 4], lhsT=slot[:, kc, :], rhs=scT[:, kc, :], start=(kc == 0), stop=(kc == KC - 1)),
                            r=[skey, "scT"], w=[PSK[2]])
                    idx += 1
        add("dve", lambda e: e.tensor_tensor(out=modT[:], in0=ps[2][:, 0:192].rearrange("p (i b) -> p i b", b=4),
                                             in1=cp(CP_AB, 48).unsqueeze(2).to_broadcast([P, 48, 4]), op=ALU.add),
            r=[PSK[2], "colp"], w=["modT"])
        for l in range(2):
            sc_l = modT[:, l * 24 + 8:l * 24 + 16, :]
            add("dve", lambda e, l=l, sc_l=sc_l: e.scalar_tensor_tensor(out=gsT[:, l, :, :], in0=sc_l, scalar=1.0,
                                                                    in1=cp(CP_NG + l * 8, 8).unsqueeze(2).to_broadcast([P, 8, 4]),
                                                                    op0=ALU.add, op1=ALU.mult), r=["modT", "colp"], w=["gsT"])
        SC.new_sem("gs_w")
        for l in range(2):
            for bb in range(4):
                dst = bass.AP(tensor=gscr, offset=bb * 2 * D + l * D, ap=[[1, P], [P, KC]])
                add("sp", lambda e, l=l, bb=bb, dst=dst: e.dma_start(out=dst, in_=modT[:, l * 24 + 16:l * 24 + 24, bb], allow_slow_non_contiguous=True),
                    r=["modT"], w=["gscr"], dsem="gs_w")

        def shiftc(l, c, b):
            return modT[:, l * 24 + c, b:b + 1]

        def gsc(l, c, b):
            return gsT[:, l, c, b:b + 1]

        def softplus_from_e(e_ap, eK, out_ap, oK, t_z, zK, t_z2, z2K, t_ln, lnK):
            t_p = out_ap
            add("act", lambda e: e.activation(out=t_ln, in_=e_ap, func=AF.Ln, bias=1.0), r=[eK], w=[lnK])
            add("dve", lambda e: e.tensor_scalar(out=t_z, in0=e_ap, scalar1=2.0, scalar2=None, op0=ALU.add), r=[eK], w=[zK])
            add("dve", lambda e: e.reciprocal(out=t_z, in_=t_z), r=[zK], w=[zK])
            add("dve", lambda e: e.tensor_tensor(out=t_z, in0=t_z, in1=e_ap, op=ALU.mult), r=[zK, eK], w=[zK])
            add("dve", lambda e: e.tensor_tensor(out=t_z2, in0=t_z, in1=t_z, op=ALU.mult), r=[zK], w=[z2K])
            add("dve", lambda e: e.tensor_scalar(out=t_p, in0=t_z2, scalar1=1.0 / 9, scalar2=1.0 / 7, op0=ALU.mult, op1=ALU.add), r=[z2K], w=[oK])
            for cst in (1.0 / 5, 1.0 / 3, 1.0):
                add("dve", lambda e: e.tensor_tensor(out=t_p, in0=t_p, in1=t_z2, op=ALU.mult), r=[oK, z2K], w=[oK])
                add("dve", lambda e, cst=cst: e.tensor_scalar(out=t_p, in0=t_p, scalar1=cst, scalar2=None, op0=ALU.add), r=[oK], w=[oK])
            add("dve", lambda e: e.scalar_tensor_tensor(out=t_p, in0=t_z, scalar=2.0, in1=t_p, op0=ALU.mult, op1=ALU.mult), r=[oK, zK], w=[oK])
            add("dve", lambda e: e.tensor_scalar(out=t_z2, in0=e_ap, scalar1=0.5, scalar2=None, op0=ALU.is_lt), r=[eK, oK], w=[z2K])
            add("dve", lambda e: e.tensor_tensor(out=t_p, in0=t_p, in1=t_ln, op=ALU.subtract), r=[oK, lnK], w=[oK])
            add("dve", lambda e: e.tensor_tensor(out=t_p, in0=t_p, in1=t_z2, op=ALU.mult), r=[oK, z2K], w=[oK])
            add("dve", lambda e: e.tensor_tensor(out=t_p, in0=t_p, in1=t_ln, op=ALU.add), r=[oK, lnK], w=[oK])

        add("act", lambda e: e.activation(out=spt[:, 3, :], in_=cp(CP_LAM, 8), func=AF.Exp, scale=-1.0), r=["colp"], w=["spt_e"])
        softplus_from_e(spt[:, 3, :], "spt_e", spt[:, 2, :], "sptout", spt[:, 0, :], "sptz", spt[:, 1, :], "sptz2", spt[:, 4, :], "sptln")
        add("dve", lambda e: e.tensor_scalar(out=lrp[:, 0, :], in0=spt[:, 2, :], scalar1=-8.0, scalar2=None, op0=ALU.mult), r=["sptout"], w=["lrp0"])
        add("dve", lambda e: e.tensor_scalar(out=lrp[:, 1, :], in0=spt[:, 2, :], scalar1=-4.0, scalar2=None, op0=ALU.mult), r=["sptout"], w=["lrp1"])
        add("dve", lambda e: e.tensor_scalar(out=lrp[:, 2, :], in0=cp(CP_BA, 8), scalar1=0.5, scalar2=None, op0=ALU.mult), r=["colp"], w=["lrp2"])
        add("dve", lambda e: e.tensor_scalar(out=lrp[:, 3, :], in0=cp(CP_BX, 8), scalar1=0.5, scalar2=None, op0=ALU.mult), r=["colp"], w=["lrp3"])
        add("act", lambda e: e.activation(out=lrp[:, 4, :], in_=cp(CP_SINK, 8), func=AF.Exp), r=["colp"], w=["lrp4"])
        LRK = ["lrp0", "lrp1", "lrp2", "lrp3"]
        add("dve", lambda e: e.tensor_scalar(out=fcar[:, 1:2], in0=v8[:, 383:384], scalar1=-1.0, scalar2=None, op0=ALU.mult), r=V8_K, w=["nbf"])
        nbf = fcar[:, 1:2]

        add("pe", lambda e: e.matmul(ps[3][0:8, 0:383], lhsT=rbs[:, 0:8], rhs=rbs[:, 8:391], start=True, stop=True), r=RBS_K, w=[PSK[3]])
        add("act", lambda e: e.activation(out=gtab[:, 0:383], in_=ps[3][0:8, 0:383], func=AF.Exp), r=[PSK[3]], w=GT_K)
        add("dve", lambda e: e.memset(gtab[:, 383:384], 0.0), w=GT_K)
        add("dve", lambda e: e.tensor_tensor(out=gtab[:, 0:383], in0=gtab[:, 0:383], in1=v8[:, 0:383], op=ALU.mult), r=GT_K + V8_K, w=GT_K)
        SC.new_sem("eb_w")
        SC.new_sem("eb_r")
        add("sp", lambda e: e.dma_start(out=ebscr.ap(), in_=gtab), r=GT_K, w=["ebscr"], dsem="eb_w")
        hbuf = th[:].rearrange("p (h t) -> p h t", h=8)
        for kb in range(2):
            src = bass.AP(tensor=ebscr, offset=(1 - kb) * P, ap=[[1, P], [384, 8], [1, P]])
            add("sp", lambda e, src=src: e.dma_start(out=hbuf, in_=src), r=["ebscr"], w=[("th", 0), ("th", 1)], dsem="eb_r")
            for q in range(2):
                add("pe", lambda e, q=q: e.matmul(ps[4 + q][:, :], lhsT=jrev, rhs=th[:, q * 512:(q + 1) * 512], start=True, stop=True),
                    r=[("th", 0), ("th", 1), "cmat_j"], w=[PSK[4 + q]])
                add("dve", lambda e, q=q, kb=kb: e.tensor_copy(out=EB[:, q * 4:(q + 1) * 4, kb, :], in_=ps[4 + q][:, :].rearrange("p (h t) -> p h t", h=4)),
                    r=[PSK[4 + q]], w=["EB"])

        for pq in ("00", "01", "10", "11"):
            SC.new_sem("xl" + pq)
        SC.new_sem("xst")
        SC.new_sem("xst0")
        SC.new_sem("xst1")
        SC.new_sem("bcs")
        if dbg:
            SC.new_sem("dbgs0")
            SC.new_sem("dbgs1")
        evq = [0]

        def xk(ti):
            return ("x", xs.cur, ti)

        def rmsnorm_rstd(ti, rcol):
            add("act", lambda e: e.activation(out=ysb[:], in_=xs[:, ti, :], func=AF.Square, accum_out=small[:, rcol:rcol + 1]),
                r=[xk(ti)], w=[("ysb", 0), ("ysb", 1), ("sm", rcol)])
            if RSTD_POW:
                add("dve", lambda e: e.tensor_scalar(out=small[:, rcol:rcol + 1], in0=small[:, rcol:rcol + 1], scalar1=1.0 / D, scalar2=EPS, op0=ALU.mult, op1=ALU.add),
                    r=[("sm", rcol)], w=[("sm", rcol)])
                add("pool", lambda e: e.tensor_tensor(out=small[:, rcol:rcol + 1], in0=small[:, rcol:rcol + 1], in1=negh, op=ALU.pow),
                    r=[("sm", rcol), "negh"], w=[("sm", rcol)])
            else:
                add("act", lambda e: e.activation(out=small[:, rcol:rcol + 1], in_=small[:, rcol:rcol + 1], func=AF.Ln, scale=1.0 / D, bias=EPS),
                    r=[("sm", rcol)], w=[("sm", rcol)])
                add("act", lambda e: e.activation(out=small[:, rcol:rcol + 1], in_=small[:, rcol:rcol + 1], func=AF.Exp, scale=-0.5),
                    r=[("sm", rcol)], w=[("sm", rcol)])

        def norm_to_hT(l, b, skip_stats=False):
            for ti in range(GT):
                if not skip_stats:
                    rmsnorm_rstd(ti, 2 + ti)
                if stop == "na":
                    continue
                scr, scrK = (xn, [("xn", 0), ("xn", 1)]) if ti == 0 else (bc, ["bc"])
                add("dve", lambda e, ti=ti, scr=scr: e.tensor_scalar(out=scr[:], in0=xs[:, ti, :], scalar1=small[:, 2 + ti:3 + ti], scalar2=None, op0=ALU.mult),
                    r=[xk(ti), ("sm", 2 + ti)], w=scrK)
                if stop == "nb":
                    continue
                for c in range(KC):
                    add("pe", lambda e, ti=ti, c=c, scr=scr: e.transpose(ps[c // 2][:, (c % 2) * G + ti * P:(c % 2) * G + (ti + 1) * P], scr[:, c * P:(c + 1) * P], ident),
                        r=scrK + ["cmat"], w=[PSK[c // 2]])
            if stop in ("na", "nb", "nc"):
                return
            for c in range(KC):
                src = ps[c // 2][:, (c % 2) * G:(c % 2 + 1) * G]
                if (c // 2) % 2 == 0:
                    add("act", lambda e, c=c, src=src: e.activation(out=hT[:, c, :], in_=src, func=AF.Identity, scale=gsc(l, c, b), bias=shiftc(l, c, b)),
                        r=[PSK[c // 2], "gsT", "modT"], w=[("hT", c)])
                else:
                    add("dve", lambda e, c=c, src=src: e.tensor_scalar(out=hT[:, c, :], in0=src, scalar1=gsc(l, c, b), scalar2=shiftc(l, c, b), op0=ALU.mult, op1=ALU.add),
                        r=[PSK[c // 2], "gsT", "modT"], w=[("hT", c)])

        def norm_tile_to_hT(l, b, ti, bank):
            rmsnorm_rstd(ti, 2 + ti)
            add("dve", lambda e: e.tensor_scalar(out=xn[:], in0=xs[:, ti, :], scalar1=small[:, 2 + ti:3 + ti], scalar2=None, op0=ALU.mult),
                r=[xk(ti), ("sm", 2 + ti)], w=[("xn", 0), ("xn", 1)])
            for half in range(2):
                for cc in range(4):
                    c = half * 4 + cc
                    add("pe", lambda e, c=c, cc=cc: e.transpose(ps[bank][:, cc * P:(cc + 1) * P], xn[:, c * P:(c + 1) * P], ident),
                        r=[("xn", 0), ("xn", 1), "cmat"], w=[PSK[bank]])
                for cc in range(4):
                    c = half * 4 + cc
                    src = ps[bank][:, cc * P:(cc + 1) * P]
                    if half == 0:
                        add("act", lambda e, c=c, src=src: e.activation(out=hT[:, c, ti * P:(ti + 1) * P], in_=src, func=AF.Identity, scale=gsc(l, c, b), bias=shiftc(l, c, b)),
                            r=[PSK[bank], "gsT", "modT"], w=[("hT", c)])
                    else:
                        add("dve", lambda e, c=c, src=src: e.tensor_scalar(out=hT[:, c, ti * P:(ti + 1) * P], in0=src, scalar1=gsc(l, c, b), scalar2=shiftc(l, c, b), op0=ALU.mult, op1=ALU.add),
                            r=[PSK[bank], "gsT", "modT"], w=[("hT", c)])

        HTK = [("hT", c) for c in range(KC)]

        def proj_fm(Wt, wkey, col0, M, evac):
            bi = 2 + (evq[0] % 2)
            evq[0] += 1
            for kc in range(KC):
                add("pe", lambda e, kc=kc, bi=bi: e.matmul(ps[bi][0:M, 0:G], lhsT=Wt[:, kc, col0:col0 + M], rhs=hT[:, kc, :], start=(kc == 0), stop=(kc == KC - 1)),
                    r=WK[wkey] + HTK, w=[PSK[bi]])
            evac(ps[bi][0:M, 0:G], PSK[bi])

        def load_bc(src_ap, rkeys):
            add("sp", lambda e: e.dma_start(out=bc[:], in_=src_ap), r=rkeys, w=["bc"], dsem="bcs")

        def residual_from(ti, hf, acc_bank):
            add("dve", lambda e: e.tensor_tensor(out=xn[:, hf * 512:(hf + 1) * 512], in0=ps[acc_bank][:, :], in1=bc[:, hf * 512:(hf + 1) * 512], op=ALU.mult),
                r=[PSK[acc_bank], "bc"], w=[("xn", hf)])
            add("dve", lambda e: e.tensor_tensor(out=xs[:, ti, hf * 512:(hf + 1) * 512], in0=xs[:, ti, hf * 512:(hf + 1) * 512], in1=xn[:, hf * 512:(hf + 1) * 512], op=ALU.add),
                r=[("xn", hf), xk(ti)], w=[xk(ti)])

        def out_proj_residual(Wt, wkey, srcT, skey_fn, ti):
            for hf in range(2):
                for kc in range(KC):
                    add("pe", lambda e, kc=kc, hf=hf: e.matmul(ps[hf][:, :], lhsT=srcT[:, kc, ti * P:(ti + 1) * P], rhs=Wt[:, kc, hf * 512:(hf + 1) * 512],
                                                               start=(kc == 0), stop=(kc == KC - 1)),
                        r=WK[wkey] + skey_fn(kc), w=[PSK[hf]])
                residual_from(ti, hf, hf)

        sbank = [0]
        pbq = [0]
        swq = [0]

        last_store = []
        pre_stats = [False]
        for b in range(nseq):
            add("pool", lambda e: e.memset(tails[:], 0.0), w=["tails"])
            add("pool", lambda e: e.memset(state[:], 0.0), w=["state"])
            add("pool", lambda e: e.memset(fcar[:, 0:1], 0.0), w=["fcar"])
            for g in range(ngrp):
                t0 = g * G
                gi = b * ngrp + g
                xs.cur = gi % 2

                def load_x(gj):
                    bj, gg = divmod(gj, ngrp)
                    keep = xs.cur
                    xs.cur = gj % 2
                    for ti in range(GT):
                        add("sp", lambda e, ti=ti: e.dma_start(out=xs[:, ti, :], in_=x_d[bj, gg * G + ti * P:gg * G + (ti + 1) * P, :]), w=[xk(ti)], dsem="xl%d%d" % (gj % 2, ti))
                    xs.cur = keep
                if gi == 0:
                    load_x(0)
                if gi + 1 < nseq * ngrp:
                    load_x(gi + 1)

                def store_x():
                    for ti in range(GT):
                        d = add("sp", lambda e, ti=ti: e.dma_start(out=out_d[b, t0 + ti * P:t0 + (ti + 1) * P, :], in_=xs[:, ti, :]), r=[xk(ti)], dsem="xst")
                    return [d]
                if stop == "pro":
                    last_store = store_x()
                    continue
                norm_to_hT(0, b, skip_stats=pre_stats[0])
                pre_stats[0] = False
                if stop in ("n0", "na", "nb", "nc"):
                    last_store = store_x()
                    continue
                load_bc(gscr.ap()[b, 0, :].partition_broadcast(P), ["gscr"])
                if stop == "n1":
                    last_store = store_x()
                    continue
                if g > 0:
                    add("pool", lambda e: e.tensor_copy(out=aKT[:, 0:P], in_=aKT[:, G:G + P]), r=["aK"], w=["aK"])
                    add("pool", lambda e: e.tensor_copy(out=aV[:, 0, :, 0:64], in_=aV[:, GT, :, 0:64]), r=[("aV", GT)], w=[("aV", 0)])
                proj_fm(W0, "W0", FL, 8, lambda src, k: add("act", lambda e: e.activation(out=frow[3], in_=src, func=AF.Exp, scale=-1.0, bias=nbf), r=[k, "nbf"], w=[FRK[3]]))
                softplus_from_e(frow[3], FRK[3], frow[2], FRK[2], frow[0], FRK[0], frow[1], FRK[1], frow[4], FRK[4])
                add("dve", lambda e: e.tensor_tensor_scan(out=frow[4], data0=ones8[:], data1=frow[2], initial=fcar[:, 0:1], op0=ALU.mult, op1=ALU.add),
                    r=[FRK[2], "ones8", "fcar"], w=[FRK[4]])
                add("dve", lambda e: e.tensor_copy(out=fcar[:, 0:1], in_=frow[4][:, G - 1:G]), r=[FRK[4]], w=["fcar"])
                for c in range(4):
                    proj_fm(W0, "W0", AQ + c * P, P, lambda src, k, c=c: add("act", lambda e: e.copy(out=aQT[:, c, :], in_=src), r=[k], w=[("aQ", c)]))
                proj_fm(W0, "W0", AK, P, lambda src, k: add("act", lambda e: e.copy(out=aKT[:, P:P + G], in_=src), r=[k], w=["aK"]))
                for c in range(4):
                    proj_fm(W0, "W0", BQ + c * P, P, lambda src, k, c=c: add("act", lambda e: e.copy(out=bQT[:, c, :], in_=src), r=[k], w=[("bQ", c)]))
                for c in range(4):
                    proj_fm(W0, "W0", BK + c * P, P, lambda src, k, c=c: add("act", lambda e: e.copy(out=bKT[:, c, t0:t0 + G], in_=src), r=[k], w=[("bK", c, g)]))
                for nl in range(GT):
                    add("pe", lambda e, nl=nl: e.transpose(ps[7][:, 300 + nl * 8:300 + nl * 8 + 8], frow[4][:, nl * P:(nl + 1) * P], ident[0:8, 0:8]),
                        r=[FRK[4], "cmat"], w=[PSK[7]])
                    add("dve", lambda e, nl=nl: e.tensor_scalar(out=fdiag[:, nl, :], in0=ident[0:8, 0:8], scalar1=frow[4][:, (nl + 1) * P - 1:(nl + 1) * P], scalar2=None, op0=ALU.mult),
                        r=[FRK[4], "cmat"], w=["fdiag"])
                add("dve", lambda e: e.tensor_copy(out=FnegT[:, g * GT:(g + 1) * GT, :], in_=ps[7][:, 300:300 + GT * 8].rearrange("p (n h) -> p n h", h=8)),
                    r=[PSK[7]], w=["FnegT"])
                add("pe", lambda e: e.matmul(ps[7][:, 400:400 + GT * 8], lhsT=ones8[:, 0:P], rhs=fdiag[:].rearrange("k n h -> k (n h)"), start=True, stop=True),
                    r=["ones8", "fdiag"], w=[PSK[7]])
                add("dve", lambda e: e.tensor_copy(out=FendBC[:], in_=ps[7][:, 400:400 + GT * 8].rearrange("p (n h) -> p n h", h=8)), r=[PSK[7]], w=["FendBC"])
                nj = (g + 1) * GT
                add("dve", lambda e, nj=nj: e.tensor_tensor(out=BT[:, 0:nj, :, :], in0=FnegT[:, 0:nj, :].unsqueeze(2).to_broadcast([P, nj, GT, 8]),
                                                            in1=FendBC[:].unsqueeze(1).to_broadcast([P, nj, GT, 8]), op=ALU.subtract),
                    r=["FnegT", "FendBC"], w=["BT", "cmat_j"])
                for ti in range(GT):
                    blk = g * GT + ti
                    bi = 2 + (evq[0] % 2)
                    evq[0] += 1
                    for kc in range(KC):
                        add("pe", lambda e, kc=kc, bi=bi, ti=ti: e.matmul(ps[bi][:, 0:P], lhsT=hT[:, kc, ti * P:(ti + 1) * P], rhs=W0[:, kc, AV:AV + P], start=(kc == 0), stop=(kc == KC - 1)),
                            r=WK["W0"] + HTK, w=[PSK[bi]])
                    add("act", lambda e, bi=bi, ti=ti: e.copy(out=aV[:, 1 + ti, :, 0:64], in_=ps[bi][:, 0:P].rearrange("p (g d) -> p g d", g=2)), r=[PSK[bi]], w=[("aV", 1 + ti)])
                    bi = 2 + (evq[0] % 2)
                    evq[0] += 1
                    for kc in range(KC):
                        add("pe", lambda e, kc=kc, bi=bi, ti=ti: e.matmul(ps[bi][:, :], lhsT=hT[:, kc, ti * P:(ti + 1) * P], rhs=W0[:, kc, BV:BV + 512], start=(kc == 0), stop=(kc == KC - 1)),
                            r=WK["W0"] + HTK, w=[PSK[bi]])
                    add("act", lambda e, bi=bi, blk=blk: e.copy(out=bV[:, blk, :, 0:64], in_=ps[bi][:, :].rearrange("p (h d) -> p h d", h=8)), r=[PSK[bi]], w=[("bV", blk)])

                def load_bc_l1():
                    load_bc(gscr.ap()[b, 1, :].partition_broadcast(P), ["gscr"])

                def make_block(nl):
                    n = g * GT + nl
                    q0 = nl * P
                    kbs = [1] if n == 0 else [0, 1]

                    def gate_mm():
                        for hf in range(2):
                            for kc in range(KC):
                                add("pe", lambda e, kc=kc, hf=hf: e.matmul(ps[hf][:, :], lhsT=hT[:, kc, q0:q0 + P], rhs=W0[:, kc, GA + hf * 512:GA + (hf + 1) * 512], start=(kc == 0), stop=(kc == KC - 1)),
                                    r=WK["W0"] + HTK, w=[PSK[hf]])

                    def gate_act():
                        for hf in range(2):
                            add("act", lambda e, hf=hf: e.activation(out=th[:, hf * 512:(hf + 1) * 512], in_=ps[hf][:, :], func=AF.Tanh, scale=0.5), r=[PSK[hf]], w=[("th", hf)])
                            add("dve", lambda e, hf=hf: e.scalar_tensor_tensor(out=th[:, hf * 512:(hf + 1) * 512], in0=th[:, hf * 512:(hf + 1) * 512], scalar=1.0, in1=ps[hf][:, :], op0=ALU.add, op1=ALU.mult),
                                r=[PSK[hf], ("th", hf)], w=[("th", hf)])

                    def swa_S(h):
                        bi = (3, 4, 5)[sbank[0] % 3]
                        sbank[0] += 1
                        c, base = h % 4, (h // 4) * 64
                        for kb in kbs:
                            add("pe", lambda e, kb=kb, bi=bi: e.matmul(ps[bi][:, kb * P:(kb + 1) * P], lhsT=aKT[base:base + 64, (nl + kb) * P:(nl + kb + 1) * P],
                                                                      rhs=aQT[base:base + 64, c, q0:q0 + P], start=True, stop=True),
                                r=["aK", ("aQ", c)], w=[PSK[bi]])
                        return bi

                    def swa_rest(h, bi):
                        kvh = h // 4
                        si = swq[0] % 2
                        swq[0] += 1
                        k0, k1 = kbs[0], kbs[-1] + 1
                        add("act", lambda e: e.activation(out=swt[:, si, k0:k1, :], in_=ps[bi][:, k0 * P:k1 * P].rearrange("p (k t) -> p k t", t=P), func=AF.Exp, scale=0.125),
                            r=[PSK[bi]], w=[("swt", si)])
                        add("pool" if h % 2 == 0 else "dve", lambda e: e.tensor_tensor(out=pa[:, si, k0:k1, :], in0=swt[:, si, k0:k1, :], in1=EB[:, h, k0:k1, :], op=ALU.mult),
                            r=[("swt", si), "EB"], w=[("pa", si)])
                        pvb = 6 + h // 4
                        for kb in kbs:
                            add("pe", lambda e, kb=kb: e.matmul(ps[pvb][:, (h % 4) * 65:(h % 4) * 65 + 65], lhsT=pa[:, si, kb, :], rhs=aV[:, nl + kb, kvh, 0:65],
                                                                start=(kb == kbs[0]), stop=(kb == kbs[-1])),
                                r=[("pa", si), ("aV", nl + kb), "aVones"], w=[PSK[pvb]])

                    def normalise(dcol, sink):
                        for hb in range(2):
                            pv = ps[6 + hb][:, 0:260].rearrange("p (h d) -> p h d", d=65)
                            dn = den[:, dcol + hb * 4:dcol + hb * 4 + 4]
                            dk = ("den", dcol // 4 + hb)
                            if sink:
                                add("dve", lambda e: e.tensor_tensor(out=dn, in0=pv[:, :, 64], in1=lrp[:, 4, hb * 4:hb * 4 + 4], op=ALU.add), r=[PSK[6 + hb], "lrp4"], w=[dk])
                                add("dve", lambda e: e.reciprocal(out=dn, in_=dn), r=[dk], w=[dk])
                            else:
                                add("dve", lambda e: e.reciprocal(out=dn, in_=pv[:, :, 64]), r=[PSK[6 + hb]], w=[dk])
                            add("dve", lambda e: e.tensor_scalar(out=dn, in0=dn, scalar1=0.5, scalar2=None, op0=ALU.mult), r=[dk], w=[dk])
                            add("dve", lambda e: e.tensor_tensor(out=osb[:, hb * 256:(hb + 1) * 256].rearrange("p (h d) -> p h d", d=64), in0=pv[:, :, 0:64],
                                                                in1=dn.unsqueeze(2).to_broadcast([P, 4, 64]), op=ALU.mult),
                                r=[PSK[6 + hb], dk], w=[("osb", hb)])

                    def swa():
                        sq = [swa_S(0), swa_S(1)]
                        for h in range(8):
                            if h + 2 < 8:
                                sq.append(swa_S(h + 2))
                            swa_rest(h, sq[h])
                        normalise(0, True)
                        add("pool", lambda e: e.tensor_tensor(out=ysb[:, 0:512], in0=th[:, 0:512], in1=osb[:], op=ALU.mult), r=[("th", 0), ("osb", 0), ("osb", 1)], w=[("ysb", 0)])

                    def fox_S(ch):
                        h, j0, j1 = ch
                        bi = (3, 4, 5)[sbank[0] % 3]
                        sbank[0] += 1
                        c, base = h // 2, (h % 2) * 64
                        for j in range(j0, j1):
                            add("pe", lambda e, j=j, bi=bi: e.matmul(ps[bi][:, (j - j0) * P:(j - j0 + 1) * P], lhsT=bKT[base:base + 64, c, j * P:(j + 1) * P],
                                                                    rhs=bQT[base:base + 64, c, q0:q0 + P], start=True, stop=True),
                                r=[("bK", c, j // GT), ("bQ", c)], w=[PSK[bi]])
                        return bi

                    def fox_rest(ch, bi):
                        h, j0, j1 = ch
                        pvb = 6 + h // 4
                        for j in range(j0, j1):
                            si = pbq[0] % NPB
                            pbq[0] += 1
                            add("act", lambda e, j=j, si=si: e.activation(out=pb[:, si, :], in_=ps[bi][:, (j - j0) * P:(j - j0 + 1) * P], func=AF.Exp, scale=0.125, bias=BT[:, j, nl, h:h + 1]),
                                r=[PSK[bi], "BT"], w=[("pb", si)])
                            if j == n:
                                add("pool", lambda e, si=si: e.tensor_tensor(out=pb[:, si, :], in0=pb[:, si, :], in1=cmaskb[:], op=ALU.mult), r=[("pb", si), "cmaskb"], w=[("pb", si)])
                            add("pe", lambda e, j=j, si=si: e.matmul(ps[pvb][:, (h % 4) * 65:(h % 4) * 65 + 65], lhsT=pb[:, si, :], rhs=bV[:, j, h, 0:65], start=(j == 0), stop=(j == n)),
                                r=[("pb", si), ("bV", j), "bVones"], w=[PSK[pvb]])

                    def fox():
                        chunks = []
                        for h in range(8):
                            for j0 in range(0, n + 1, 4):
                                chunks.append((h, j0, min(n + 1, j0 + 4)))
                        fq = [fox_S(ch) for ch in chunks[0:2]]
                        for i, ch in enumerate(chunks):
                            if i + 2 < len(chunks):
                                fq.append(fox_S(chunks[i + 2]))
                            fox_rest(ch, fq[i])
                        normalise(8, False)
                        add("pool", lambda e: e.tensor_tensor(out=ysb[:, 512:1024], in0=th[:, 512:1024], in1=osb[:], op=ALU.mult), r=[("th", 1), ("osb", 0), ("osb", 1)], w=[("ysb", 1)])

                    def tail_a():
                        ytp = ps[2][:, :].bitcast(BF16)
                        for kc in range(KC):
                            add("pe", lambda e, kc=kc: e.transpose(ytp[:, kc * P:(kc + 1) * P], ysb[:, kc * P:(kc + 1) * P], identb[:]),
                                r=[("ysb", kc // 4), "identb"], w=[PSK[2]])
                        add("dve", lambda e: e.tensor_copy(out=yT[:, :, q0:q0 + P], in_=ytp.rearrange("p (k t) -> p k t", t=P)), r=[PSK[2]], w=[("yT", nl)] + Y1K)

                    def tail_b():
                        out_proj_residual(W1, "W1", yT, lambda kc: [("yT", nl)], nl)

                    return gate_mm, gate_act, swa, fox, tail_a, tail_b

                Y1K = [("y1", c) for c in range(KC)]
                blocks = [make_block(nl) for nl in range(GT)]
                for nl in range(GT):
                    gm, ga, sw, fx, ta, tb = blocks[nl]
                    if nl == 0:
                        gm()
                        ga()
                    sw()
                    fx()
                    if nl + 1 < GT:
                        blocks[nl + 1][0]()
                    ta()
                    if nl + 1 < GT:
                        blocks[nl + 1][1]()
                    tb()
                    if EARLY_L1_NORM and stop is None:
                        norm_tile_to_hT(1, b, nl, 3)
                        if nl == GT - 1:
                            load_bc_l1()

                if dbg:
                    for ti in range(GT):
                        add("sp", lambda e, ti=ti: e.dma_start(out=dbg_d[b, t0 + ti * P:t0 + (ti + 1) * P, :], in_=xs[:, ti, :]), r=[xk(ti)], dsem="dbgs%d" % ti)

                if stop == "l0":
                    last_store = store_x()
                    continue
                if not (EARLY_L1_NORM and stop is None):
                    norm_to_hT(1, b)
                    load_bc_l1()
                YTK = [("yT", nl) for nl in range(GT)]

                def l1_A(c, T):
                    bA = T["banks"][0]
                    for kc in range(KC):
                        add("pe", lambda e, kc=kc: e.matmul(ps[bA][:, 0:G], lhsT=W2[:, kc, c * P:(c + 1) * P], rhs=hT[:, kc, :], start=(kc == 0), stop=(kc == KC - 1)),
                            r=WK["W2"] + HTK, w=[PSK[bA]])
                    for kc in range(KC):
                        add("pe", lambda e, kc=kc: e.matmul(ps[bA][:, G:2 * G], lhsT=W2[:, kc, D + c * P:D + (c + 1) * P], rhs=hT[:, kc, :], start=(kc == 0), stop=(kc == KC - 1)),
                            r=WK["W2"] + HTK, w=[PSK[bA]])

                def l1_B1(c, T):
                    bA = T["banks"][0]
                    add("pool", lambda e: e.tensor_copy(out=T["xrb"][:, 0:3], in_=tails[:, c, 0:3]), r=["tails"], w=[T["xrtK"]])
                    add("act", lambda e: e.copy(out=T["xrb"][:, 3:3 + G], in_=ps[bA][:, 0:G]), r=[PSK[bA]], w=[T["xrbK"]])
                    add("pool", lambda e: e.tensor_copy(out=tails[:, c, 0:3], in_=T["xrb"][:, G:G + 3]), r=[T["xrbK"]], w=["tails"])
                    add("act", lambda e: e.activation(out=T["sg"], in_=ps[bA][:, G:2 * G], func=AF.Tanh, scale=0.5), r=[PSK[bA]], w=[T["sgK"]])

                def l1_B2(c, T):
                    bA = T["banks"][0]
                    add("dve", lambda e: e.scalar_tensor_tensor(out=T["sg"], in0=T["sg"], scalar=1.0, in1=ps[bA][:, G:2 * G], op0=ALU.add, op1=ALU.mult), r=[PSK[bA], T["sgK"]], w=[T["sgK"]])
                    cw = lambda j: colp[:, CP_CW + c * 4 + j:CP_CW + c * 4 + j + 1]
                    add("dve", lambda e: e.tensor_scalar(out=T["xc"], in0=T["xrb"][:, 0:G], scalar1=cw(0), scalar2=colp[:, CP_CB + c:CP_CB + c + 1], op0=ALU.mult, op1=ALU.add),
                        r=[T["xrbK"], T["xrtK"], "colp"], w=[T["xcK"]])
                    for j in range(1, 4):
                        add("dve", lambda e, j=j: e.scalar_tensor_tensor(out=T["xc"], in0=T["xrb"][:, j:j + G], scalar=cw(j), in1=T["xc"], op0=ALU.mult, op1=ALU.add),
                            r=[T["xrbK"], T["xrtK"], "colp", T["xcK"]], w=[T["xcK"]])
                    add("dve", lambda e: e.tensor_copy(out=T["xcb"], in_=T["xc"]), r=[T["xcK"]], w=[T["xcbK"]])

                def l1_C(c, T):
                    bB = T["banks"][1]
                    add("pe", lambda e: e.matmul(ps[bB][:, 0:G], lhsT=W4[:, 0, c, :], rhs=T["xcb"], start=True, stop=True), r=WK["W4"] + [T["xcbK"]], w=[PSK[bB]])
                    add("pe", lambda e: e.matmul(ps[bB][:, G:2 * G], lhsT=W4[:, 1, c, :], rhs=T["xcb"], start=True, stop=True), r=WK["W4"] + [T["xcbK"]], w=[PSK[bB]])

                def l1_D1(c, T):
                    bB = T["banks"][1]
                    add("act", lambda e: e.activation(out=T["tha"], in_=ps[bB][:, 0:G], func=AF.Tanh, scale=0.5, bias=lrp[:, 2, c:c + 1]), r=[PSK[bB]] + LRK, w=[T["thaK"]])
                    add("act", lambda e: e.activation(out=T["thx"], in_=ps[bB][:, G:2 * G], func=AF.Tanh, scale=0.5, bias=lrp[:, 3, c:c + 1]), r=[PSK[bB]] + LRK, w=[T["thxK"]])

                def l1_D2(c, T):
                    add("act", lambda e: e.activation(out=T["ab"], in_=T["tha"], func=AF.Exp, scale=lrp[:, 1, c:c + 1], bias=lrp[:, 1, c:c + 1]), r=[T["thaK"]] + LRK, w=[T["abK"]])
                    add("act", lambda e: e.activation(out=T["tha"], in_=T["tha"], func=AF.Exp, scale=lrp[:, 0, c:c + 1], bias=lrp[:, 0, c:c + 1]), r=[T["thaK"]] + LRK, w=[T["thaK"]])
                    add("dve", lambda e: e.scalar_tensor_tensor(out=T["thx"], in0=T["thx"], scalar=1.0, in1=T["xc"], op0=ALU.add, op1=ALU.mult), r=[T["thxK"], T["xcK"]], w=[T["thxK"]])

                def l1_D3(c, T):
                    add("act", lambda e: e.activation(out=T["tha"], in_=T["tha"], func=AF.Sqrt, scale=-1.0, bias=1.0), r=[T["thaK"]], w=[T["thaK"]])

                def l1_D4(c, T):
                    add("dve", lambda e: e.scalar_tensor_tensor(out=T["tha"], in0=T["tha"], scalar=0.5, in1=T["thx"], op0=ALU.mult, op1=ALU.mult), r=[T["thaK"], T["thxK"]], w=[T["thaK"]])
                    add("dve", lambda e: e.tensor_tensor_scan(out=T["xc"], data0=T["ab"], data1=T["tha"], initial=state[:, c:c + 1], op0=ALU.mult, op1=ALU.add),
                        r=[T["abK"], T["thaK"], "state", T["thxK"]], w=[T["xcK"]])
                    add("pool", lambda e: e.tensor_copy(out=state[:, c:c + 1], in_=T["xc"][:, G - 1:G]), r=[T["xcK"]], w=["state"])
                    add("dve", lambda e: e.scalar_tensor_tensor(out=yT[:, c, :], in0=T["xc"], scalar=0.5, in1=T["sg"], op0=ALU.mult, op1=ALU.mult),
                        r=[T["xcK"], T["sgK"]], w=YTK + [("y1", c)])

                ACCB = (0, 1, 4, 5)

                def l1_O(c, T):
                    for ti in range(GT):
                        for hf in range(2):
                            ab_ = ACCB[ti * 2 + hf]
                            add("pe", lambda e, ti=ti, hf=hf, ab_=ab_: e.matmul(ps[ab_][:, :], lhsT=yT[:, c, ti * P:(ti + 1) * P], rhs=W3[:, c, hf * 512:(hf + 1) * 512],
                                                                           start=(c == 0), stop=(c == KC - 1)),
                                r=WK["W3"] + [("y1", c)], w=[PSK[ab_]])

                pairs = [(2 * k, 2 * k + 1) for k in range(KC // 2)]

                def stage(fn, pr):
                    for i, c in enumerate(pr):
                        fn(c, L1S[i])

                stage(l1_A, pairs[0])
                if stop is None and gi + 1 < nseq * ngrp:
                    xs.cur = (gi + 1) % 2
                    for ti in range(GT):
                        rmsnorm_rstd(ti, 2 + ti)
                    xs.cur = gi % 2
                    pre_stats[0] = True
                for k, pr in enumerate(pairs):
                    stage(l1_B1, pr)
                    stage(l1_B2, pr)
                    stage(l1_C, pr)
                    stage(l1_D1, pr)
                    if k + 1 < len(pairs):
                        stage(l1_A, pairs[k + 1])
                    stage(l1_D2, pr)
                    stage(l1_D3, pr)
                    stage(l1_D4, pr)
                    stage(l1_O, pr)
                if stop == "l1":
                    for ti in range(GT):
                        for hf in range(2):
                            residual_from(ti, hf, ACCB[ti * 2 + hf])
                    last_store = store_x()
                    continue
                for ti in range(GT):
                    for hf in range(2):
                        residual_from(ti, hf, ACCB[ti * 2 + hf])
                load_bc(fg_d.ap()[0, :].partition_broadcast(P), [])
                fin = [(th[:], [("th", 0), ("th", 1)]), (l1b[:, 0:D], ["l1xc", "l1tha", "l1thx", "l1ab"])]
                for ti in range(GT):
                    rmsnorm_rstd(ti, 4 + ti)
                    fo, fk = fin[ti]
                    add("dve", lambda e, ti=ti, fo=fo: e.scalar_tensor_tensor(out=fo, in0=xs[:, ti, :], scalar=small[:, 4 + ti:5 + ti], in1=bc[:], op0=ALU.mult, op1=ALU.mult),
                        r=[xk(ti), ("sm", 4 + ti), "bc"], w=fk)
                    add("sp", lambda e, ti=ti, fo=fo: e.dma_start(out=out_d[b, t0 + ti * P:t0 + (ti + 1) * P, :], in_=fo), r=fk, dsem="xst%d" % ti)
                    last_store = [("xst0", SC.cnt["xst0"]), ("xst1", SC.cnt["xst1"])]
        finals = list(last_store)
        if dbg:
            finals.append(("dbgs0", SC.cnt["dbgs0"]))
            finals.append(("dbgs1", SC.cnt["dbgs1"]))
        SC.run(final_conds=finals)
    return nc


def _t5_bucket(rel):
    n = np.maximum(rel, 0)
    nf = np.maximum(n, 1).astype(np.float32)
    large = 16 + (np.log(nf / np.float32(16)) / np.float32(math.log(128 / 16)) * np.float32(32 - 16)).astype(np.int32)
    large = np.minimum(large, 31)
    return np.where(n < 16, n, large)


def _host_layout(inp, nseq_core=SEQ_PER_CORE, ncores=NCORES):
    f = lambda a: np.ascontiguousarray(np.asarray(a, dtype=np.float32))
    w_in = f(inp["attn_w_in"])[0]
    cols = np.concatenate([
        np.concatenate([np.concatenate([np.arange(c * 64, (c + 1) * 64), np.arange((4 + c) * 64, (5 + c) * 64)]) for c in range(4)]),
        np.arange(512, 640),
        np.arange(768, 1280),
        np.arange(1280, 1792),
        np.arange(2304, 2312),
        np.arange(640, 768),
        np.arange(1792, 2304),
        np.arange(2312, 3336),
    ])
    assert cols.size == C0
    kcl = lambda w: np.ascontiguousarray(w.reshape(KC, P, w.shape[1]).transpose(1, 0, 2))
    shared = {
        "w0": kcl(w_in[:, cols]),
        "w1": kcl(f(inp["attn_w_out"])[0]),
        "w2": kcl(f(inp["lru_w_in"])[0]),
        "w3": kcl(f(inp["lru_w_out"])[0]),
        "w4": np.ascontiguousarray(np.stack([f(inp["lru_w_a"])[0], f(inp["lru_w_x"])[0]], 0).transpose(2, 0, 1, 3)),
        "ada_w": f(inp["ada_w"]),
        "final_g": f(inp["final_g"]).reshape(1, D),
    }
    colv = lambda v: np.ascontiguousarray(v.reshape(KC, P).T)
    colp = np.zeros((P, CP_N), np.float32)
    ng = f(inp["norm_g"])
    ab = f(inp["ada_b"])
    for l in range(2):
        colp[:, CP_NG + l * 8:CP_NG + (l + 1) * 8] = colv(ng[l])
        for part in range(3):
            colp[:, CP_AB + (l * 3 + part) * 8:CP_AB + (l * 3 + part + 1) * 8] = colv(ab[l, part * D:(part + 1) * D])
    cw = f(inp["lru_conv_w"])[0]
    for j in range(4):
        colp[:, CP_CW + j:CP_CW + 32:4] = colv(cw[j])
    colp[:, CP_CB:CP_CB + 8] = colv(f(inp["lru_conv_b"])[0])
    colp[:, CP_BA:CP_BA + 8] = colv(f(inp["lru_b_a"])[0])
    colp[:, CP_BX:CP_BX + 8] = colv(f(inp["lru_b_x"])[0])
    colp[:, CP_LAM:CP_LAM + 8] = colv(f(inp["lru_lambda"])[0])
    colp[:, CP_SINK:CP_SINK + 8] = np.broadcast_to(f(inp["attn_sinks"])[0][None, :], (P, 8))
    eye = np.eye(P, dtype=np.float32)
    sidx = np.arange(P)
    cmask = (sidx[:, None] <= sidx[None, :]).astype(np.float32)
    shared["cmat"] = np.ascontiguousarray(np.concatenate([eye, eye[::-1], cmask], axis=1))
    rel = np.arange(383) - 127
    onehot = np.zeros((32, 383), np.float32)
    onehot[_t5_bucket(rel), np.arange(383)] = 1.0
    valid = ((rel >= 0) & (rel < 128)).astype(np.float32)
    shared["rb"] = np.ascontiguousarray(np.concatenate([f(inp["rel_bias"]), onehot], axis=1))
    v8 = np.zeros((8, 384), np.float32)
    v8[:, 0:383] = valid[None, :]
    v8[:, 383] = f(inp["attn_b_f"])[0]
    shared["v8"] = v8
    x = f(inp["x"])
    c = f(inp["c"])
    maps = []
    for core in range(ncores):
        b0 = core * nseq_core
        m = dict(shared)
        m["x"] = x[b0:b0 + nseq_core]
        cp_ = colp.copy()
        ct = c[b0:b0 + nseq_core].reshape(nseq_core, KC, P).transpose(2, 1, 0)
        ctf = np.zeros((P, KC, 4), np.float32)
        ctf[:, :, 0:nseq_core] = ct
        cp_[:, CP_CT:CP_CT + KC * 4] = ctf.reshape(P, KC * 4)
        m["colp"] = cp_
        maps.append(m)
    return maps


def kernel(**inputs):
    maps = _host_layout(inputs)
    nc = build()
    res = run_bass_kernel_spmd(nc, maps, core_ids=list(range(NCORES)))
    out = np.concatenate([np.asarray(r["out"], dtype=np.float32) for r in res.results], axis=0)
    return out
```

```python
import math
from contextlib import ExitStack

import numpy as np
import concourse.bass as bass
import concourse.mybir as mybir
from concourse.bass_utils import run_bass_kernel_spmd

F32 = mybir.dt.float32
BF16 = mybir.dt.bfloat16
AF = mybir.ActivationFunctionType
ALU = mybir.AluOpType

NCORES = 8
P = 128
D = 1024
KC = 8
S = 2048
G = 256
GT = G // P
NB = S // P
NGRP = S // G
SEQ_PER_CORE = 4
EPS = 1e-6
VW = 66

AQ, AK, BQ, BK, FL, AV, BV, GA, C0 = 0, 512, 640, 1152, 1664, 1672, 1800, 2312, 3336
CP_NG, CP_AB, CP_CW, CP_CB, CP_BA, CP_BX, CP_LAM, CP_SINK, CP_CT, CP_N = 0, 16, 64, 96, 104, 112, 120, 128, 136, 168

SAME_ENGINE_SYNC = True
RSTD_POW = True
EARLY_L1_NORM = False


class _Rec:
    def __init__(self):
        self.call = None

    def __getattr__(self, name):
        def f(*a, **k):
            self.call = (name, a, k)
            return self
        return f


class _XS:
    def __init__(self, bufs):
        self.bufs = bufs
        self.cur = 0

    def __getitem__(self, idx):
        return self.bufs[self.cur][idx]


class Sched:
    STREAMS = ("pe", "act", "dve", "pool", "sp")

    def __init__(self, nc, es):
        self.nc = nc
        self.es = es
        self.sems = {}
        self.cnt = {}
        self.ops = {s: [] for s in self.STREAMS}
        self.lastw = {}
        self.rds = {}
        self.excl = set()
        for s in ("pe", "act", "dve", "pool"):
            self.new_sem("E_" + s)

    def new_sem(self, name):
        self.sems[name] = self.es.enter_context(self.nc.semaphore(name))
        self.cnt[name] = 0
        return name

    def add(self, stream, fn, r=(), w=(), dsem=None, extra=()):
        deps = set(extra)
        own = dsem if dsem is not None else "E_" + stream
        for k in r:
            if k in self.lastw:
                deps.add(self.lastw[k])
            if k in self.excl:
                deps.update(d for d in self.rds.get(k, ()) if d[0] != own)
        for k in w:
            if k in self.lastw:
                deps.add(self.lastw[k])
            deps.update(self.rds.get(k, ()))
        if dsem is None:
            sname = "E_" + stream
            self.cnt[sname] += 1
            inc = 1
        else:
            sname = dsem
            self.cnt[sname] += 16
            inc = 16
        done = (sname, self.cnt[sname])
        for k in r:
            self.rds.setdefault(k, []).append(done)
        for k in w:
            self.lastw[k] = done
            self.rds[k] = []
        rec = _Rec()
        fn(rec)
        assert rec.call is not None
        self.ops[stream].append((deps, rec.call, sname, inc))
        return done

    def emit(self, stream, eng):
        known = {}
        own = "E_" + stream
        for deps, fn, sname, inc in self.ops[stream]:
            best = {}
            for (s, v) in deps:
                if v > best.get(s, 0):
                    best[s] = v
            for s, v in best.items():
                if s == own and (stream == "pe" or not SAME_ENGINE_SYNC):
                    continue
                if known.get(s, 0) >= v:
                    continue
                eng.wait_ge(self.sems[s], v)
                known[s] = v
            name, a, k = fn
            ins = getattr(eng, name)(*a, **k)
            ins.then_inc(self.sems[sname], inc)

    def run(self, final_conds=()):
        nc = self.nc
        with nc.Block() as block:
            @block.tensor
            def _(e):
                self.emit("pe", e)

            @block.scalar
            def _(e):
                self.emit("act", e)

            @block.vector
            def _(e):
                self.emit("dve", e)

            @block.gpsimd
            def _(e):
                self.emit("pool", e)

            @block.sync
            def _(e):
                self.emit("sp", e)
                best = {}
                for (s, v) in final_conds:
                    best[s] = max(best.get(s, 0), v)
                for s, v in best.items():
                    e.wait_ge(self.sems[s], v)


def build(nseq=SEQ_PER_CORE, ngrp=NGRP, dbg=False, stop=None):
    nc = bass.Bass("TRN2", target_bir_lowering=False, dynamic_dma_scratch_size=2048)
    x_d = nc.dram_tensor("x", [nseq, S, D], F32, kind="ExternalInput").ap()
    out_d = nc.dram_tensor("out", [nseq, S, D], F32, kind="ExternalOutput").ap()
    w0_d = nc.dram_tensor("w0", [P, KC, C0], F32, kind="ExternalInput").ap()
    w1_d = nc.dram_tensor("w1", [P, KC, D], F32, kind="ExternalInput").ap()
    w2_d = nc.dram_tensor("w2", [P, KC, 2 * D], F32, kind="ExternalInput").ap()
    w3_d = nc.dram_tensor("w3", [P, KC, D], F32, kind="ExternalInput").ap()
    w4_d = nc.dram_tensor("w4", [P, 2, KC, P], F32, kind="ExternalInput").ap()
    ada_d = nc.dram_tensor("ada_w", [2, D, 3 * D], F32, kind="ExternalInput").ap()
    colp_d = nc.dram_tensor("colp", [P, CP_N], F32, kind="ExternalInput").ap()
    cmat_d = nc.dram_tensor("cmat", [P, 3 * P], F32, kind="ExternalInput").ap()
    rb_d = nc.dram_tensor("rb", [32, 8 + 383], F32, kind="ExternalInput").ap()
    v8_d = nc.dram_tensor("v8", [8, 384], F32, kind="ExternalInput").ap()
    fg_d = nc.dram_tensor("final_g", [1, D], F32, kind="ExternalInput")
    gscr = nc.dram_tensor("gscr", [4, 2, D], F32, kind="Internal")
    ebscr = nc.dram_tensor("ebscr", [8, 384], F32, kind="Internal")
    if dbg:
        dbg_d = nc.dram_tensor("dbg", [nseq, S, D], F32, kind="ExternalOutput").ap()

    with ExitStack() as es:
        SC = Sched(nc, es)
        add = SC.add

        def sb(name, shape, dt):
            return es.enter_context(nc.sbuf_tensor("s_" + name, shape, dt))

        W0 = sb("W0", [P, KC, C0], BF16)
        W1 = sb("W1", [P, KC, D], BF16)
        W2 = sb("W2", [P, KC, 2 * D], BF16)
        W3 = sb("W3", [P, KC, D], BF16)
        W4 = sb("W4", [P, 2, KC, P], BF16)
        bKT = sb("bKT", [P, 4, S], BF16)
        bV = sb("bV", [P, NB, 8, VW], BF16)
        aKT = sb("aKT", [P, P + G], BF16)
        aV = sb("aV", [P, GT + 1, 2, VW], BF16)
        xs = _XS([sb("xs0", [P, GT, D], F32), sb("xs1", [P, GT, D], F32)])
        xn = sb("xn", [P, D], F32)
        bc = sb("bc", [P, D], F32)
        hT = sb("hT", [P, KC, G], BF16)
        yT = sb("yT", [P, KC, G], BF16)
        aQT = sb("aQT", [P, 4, G], BF16)
        bQT = sb("bQT", [P, 4, G], BF16)
        th = sb("th", [P, D], F32)
        osb = sb("osb", [P, 512], F32)
        ysb = sb("ysb", [P, D], BF16)
        NPB = 8
        pb = sb("pb", [P, NPB, P], BF16)
        swt = sb("swt", [P, 2, 2, P], F32)
        pa = sb("pa", [P, 2, 2, P], BF16)
        EB = sb("EB", [P, 8, 2, P], BF16)
        colp = sb("colp", [P, CP_N], F32)
        cmat = sb("cmat", [P, 3 * P], F32)
        identb = sb("identb", [P, P], BF16)
        cmaskb = sb("cmaskb", [P, P], BF16)
        rbs = ysb[:].bitcast(F32)[0:32, 0:8 + 383]
        RBS_K = [("ysb", 0), ("ysb", 1)]
        v8 = pb[:].rearrange("p a b -> p (a b)").bitcast(F32)[0:8, 0:384]
        V8_K = [("pb", i) for i in range(NPB)]
        gtab = swt[:].rearrange("p a b c -> p (a b c)")[0:8, 0:384]
        GT_K = [("swt", 0), ("swt", 1)]
        scT = sb("scT", [P, KC, 4], F32)
        modT = sb("modT", [P, 48, 4], F32)
        gsT = sb("gsT", [P, 2, 8, 4], F32)
        small = sb("small", [P, 64], F32)
        lrp = sb("lrp", [P, 5, 8], F32)
        spt = sb("spt", [P, 5, 8], F32)
        FnegT = sb("FnegT", [P, NB, 8], F32)
        FendBC = sb("FendBC", [P, GT, 8], F32)
        l1b = sb("l1b", [P, 5 * G], F32)
        xrb1 = sb("xrb1", [P, G + 4], F32)
        xcb1 = sb("xcb1", [P, G], BF16)
        fdiag = sb("fdiag", [8, GT, 8], F32)
        ones8 = sb("ones8", [8, G], F32)
        fcar = sb("fcar", [8, 2], F32)
        den = sb("den", [P, 16], F32)
        tails = sb("tails", [P, KC, 4], F32)
        state = sb("state", [P, KC], F32)
        xrb = sb("xrb", [P, G + 4], F32)
        xcb = sb("xcb", [P, G], BF16)
        L1S = [
            dict(xc=osb[:, 0:G], xcK=("osb", 0), tha=th[:, 0:G], thaK=("th", 0), thx=th[:, G:2 * G], thxK=("th", 0),
                 ab=th[:, 2 * G:3 * G], abK=("th", 1), sg=th[:, 3 * G:4 * G], sgK=("th", 1),
                 xrb=xrb[:], xrbK="xrb0", xrtK="xrbt0", xcb=xcb[:], xcbK="xcb0", banks=(2, 3)),
            dict(xc=l1b[:, 0:G], xcK="l1xc", tha=l1b[:, G:2 * G], thaK="l1tha", thx=l1b[:, 2 * G:3 * G], thxK="l1thx",
                 ab=l1b[:, 3 * G:4 * G], abK="l1ab", sg=l1b[:, 4 * G:5 * G], sgK="l1sg",
                 xrb=xrb1[:], xrbK="xrb1", xrtK="xrbt1", xcb=xcb1[:], xcbK="xcb1", banks=(6, 7)),
        ]
        frow = [th[0:8, 0:G], th[0:8, G:2 * G], th[0:8, 2 * G:3 * G], th[0:8, 3 * G:4 * G], osb[0:8, 0:G]]
        FRK = [("th", 0), ("th", 0), ("th", 1), ("th", 1), ("osb", 0)]

        ps = [es.enter_context(nc.psum_tensor("psum%d" % i, [P, 512], F32)) for i in range(8)]
        PSK = ["ps%d" % i for i in range(8)]
        SC.excl.update(PSK)

        ident = cmat[:, 0:P]
        BT = cmat[:, P:3 * P].rearrange("p (j n h) -> p j n h", n=GT, h=8)
        jrev = cmat[:, P:2 * P]
        cmaskf = cmat[:, 2 * P:3 * P]

        def cp(off, n):
            return colp[:, off:off + n]

        for nm, dst, src, wk in (("colp", colp[:], colp_d, ["colp"]), ("cmat", cmat[:], cmat_d, ["cmat", "cmat_j"]), ("rbs", rbs, rb_d, RBS_K), ("v8", v8, v8_d, V8_K)):
            sem = SC.new_sem("ld_" + nm)
            add("sp", lambda e, dst=dst, src=src: e.dma_start(out=dst, in_=src), w=wk, dsem=sem)

        NLANE = 6
        for i in range(NLANE):
            SC.new_sem("wl%d" % i)
        WK = {nm: [] for nm in ("W0", "W1", "W2", "W3", "W4")}
        lane_last = [None] * NLANE
        wq = [0]

        def wload(nm, dst, src):
            i = wq[0]
            wq[0] += 1
            key = (nm, len(WK[nm]))
            WK[nm].append(key)
            ln = i % NLANE
            ex = [lane_last[ln]] if lane_last[ln] is not None else []
            lane_last[ln] = add("pool", lambda e: e.dma_start(out=dst, in_=src), w=[key], dsem="wl%d" % ln, extra=ex)

        for kc in range(KC):
            for (c0, c1) in ((0, 1668), (1668, C0)):
                wload("W0", W0[:, kc, c0:c1], w0_d[:, kc, c0:c1])
        for kc in range(0, KC, 2):
            wload("W1", W1[:, kc:kc + 2, :], w1_d[:, kc:kc + 2, :])
        for kc in range(KC):
            wload("W2", W2[:, kc, :], w2_d[:, kc, :])
        for kc in range(0, KC, 2):
            wload("W3", W3[:, kc:kc + 2, :], w3_d[:, kc:kc + 2, :])
        for i in range(2):
            for k2 in range(0, KC, 2):
                wload("W4", W4[:, i, k2:k2 + 2, :], w4_d[:, i, k2:k2 + 2, :])

        add("dve", lambda e: e.tensor_copy(out=identb[:], in_=ident), r=["cmat"], w=["identb"])
        add("dve", lambda e: e.tensor_copy(out=cmaskb[:], in_=cmaskf), r=["cmat_j"], w=["cmaskb"])
        add("dve", lambda e: e.memset(ones8[:], 1.0), w=["ones8"])
        add("dve", lambda e: e.memset(small[:, 0:1], -0.5), w=["negh"])
        add("dve", lambda e: e.memset(bV[:, :, :, 64:VW], 1.0), w=["bVones"])
        add("dve", lambda e: e.memset(aV[:, :, :, 64:VW], 1.0), w=["aVones"])
        negh = small[:, 0:1]

        cT = cp(CP_CT, 32)
        scf = scT[:].rearrange("p k b -> p (k b)")
        add("act", lambda e: e.activation(out=scf, in_=cT, func=AF.Tanh, scale=0.5), r=["colp"], w=["scT"])
        add("dve", lambda e: e.scalar_tensor_tensor(out=scf, in0=scf, scalar=1.0, in1=cT, op0=ALU.add, op1=ALU.mult), r=["scT", "colp"], w=["scT"])
        add("dve", lambda e: e.tensor_scalar(out=scf, in0=scf, scalar1=0.5, scalar2=None, op0=ALU.mult), r=["scT"], w=["scT"])

        stag = [(xn[:].rearrange("p (k j) -> p k j", k=KC), ("xn", 0)), (bc[:].rearrange("p (k j) -> p k j", k=KC), "bc"),
                (th[:].rearrange("p (k j) -> p k j", k=KC), ("th", 0)), (l1b[:, 0:D].rearrange("p (k j) -> p k j", k=KC), "l1xc")]
        NST = len(stag)
        for i in range(NST):
            SC.new_sem("ad%d" % i)
        idx = 0
        for l in range(2):
            for part in range(3):
                for c in range(KC):
                    slot, skey = stag[idx % NST]
                    col0 = part * D + c * P
                    src = ada_d[l, :, col0:col0 + P].rearrange("(k p) j -> p k j", p=P)
                    add("sp", lambda e, slot=slot, src=src: e.dma_start(out=slot, in_=src), w=[skey], dsem="ad%d" % (idx % NST))
                    for kc in range(KC):
                        add("pe", lambda e, slot=slot, kc=kc, idx=idx: e.matmul(ps[2][:, idx * 4:idx * 4 + 4], lhsT=slot[:, kc, :], rhs=scT[:, kc, :], start=(kc == 0), stop=(kc == KC - 1)),
                            r=[skey, "scT"], w=[PSK[2]])
                    idx += 1
        add("dve", lambda e: e.tensor_tensor(out=modT[:], in0=ps[2][:, 0:192].rearrange("p (i b) -> p i b", b=4),
                                             in1=cp(CP_AB, 48).unsqueeze(2).to_broadcast([P, 48, 4]), op=ALU.add),
            r=[PSK[2], "colp"], w=["modT"])
        for l in range(2):
            sc_l = modT[:, l * 24 + 8:l * 24 + 16, :]
            add("dve", lambda e, l=l, sc_l=sc_l: e.scalar_tensor_tensor(out=gsT[:, l, :, :], in0=sc_l, scalar=1.0,
                                                                    in1=cp(CP_NG + l * 8, 8).unsqueeze(2).to_broadcast([P, 8, 4]),
                                                                    op0=ALU.add, op1=ALU.mult), r=["modT", "colp"], w=["gsT"])
        SC.new_sem("gs_w")
        for l in range(2):
            for bb in range(4):
                dst = bass.AP(tensor=gscr, offset=bb * 2 * D + l * D, ap=[[1, P], [P, KC]])
                add("sp", lambda e, l=l, bb=bb, dst=dst: e.dma_start(out=dst, in_=modT[:, l * 24 + 16:l * 24 + 24, bb], allow_slow_non_contiguous=True),
                    r=["modT"], w=["gscr"], dsem="gs_w")

        def shiftc(l, c, b):
            return modT[:, l * 24 + c, b:b + 1]

        def gsc(l, c, b):
            return gsT[:, l, c, b:b + 1]

        def softplus_from_e(e_ap, eK, out_ap, oK, t_z, zK, t_z2, z2K, t_ln, lnK):
            t_p = out_ap
            add("act", lambda e: e.activation(out=t_ln, in_=e_ap, func=AF.Ln, bias=1.0), r=[eK], w=[lnK])
            add("dve", lambda e: e.tensor_scalar(out=t_z, in0=e_ap, scalar1=2.0, scalar2=None, op0=ALU.add), r=[eK], w=[zK])
            add("dve", lambda e: e.reciprocal(out=t_z, in_=t_z), r=[zK], w=[zK])
            add("dve", lambda e: e.tensor_tensor(out=t_z, in0=t_z, in1=e_ap, op=ALU.mult), r=[zK, eK], w=[zK])
            add("dve", lambda e: e.tensor_tensor(out=t_z2, in0=t_z, in1=t_z, op=ALU.mult), r=[zK], w=[z2K])
            add("dve", lambda e: e.tensor_scalar(out=t_p, in0=t_z2, scalar1=1.0 / 9, scalar2=1.0 / 7, op0=ALU.mult, op1=ALU.add), r=[z2K], w=[oK])
            for cst in (1.0 / 5, 1.0 / 3, 1.0):
                add("dve", lambda e: e.tensor_tensor(out=t_p, in0=t_p, in1=t_z2, op=ALU.mult), r=[oK, z2K], w=[oK])
                add("dve", lambda e, cst=cst: e.tensor_scalar(out=t_p, in0=t_p, scalar1=cst, scalar2=None, op0=ALU.add), r=[oK], w=[oK])
            add("dve", lambda e: e.scalar_tensor_tensor(out=t_p, in0=t_z, scalar=2.0, in1=t_p, op0=ALU.mult, op1=ALU.mult), r=[oK, zK], w=[oK])
            add("dve", lambda e: e.tensor_scalar(out=t_z2, in0=e_ap, scalar1=0.5, scalar2=None, op0=ALU.is_lt), r=[eK, oK], w=[z2K])
            add("dve", lambda e: e.tensor_tensor(out=t_p, in0=t_p, in1=t_ln, op=ALU.subtract), r=[oK, lnK], w=[oK])
            add("dve", lambda e: e.tensor_tensor(out=t_p, in0=t_p, in1=t_z2, op=ALU.mult), r=[oK, z2K], w=[oK])
            add("dve", lambda e: e.tensor_tensor(out=t_p, in0=t_p, in1=t_ln, op=ALU.add), r=[oK, lnK], w=[oK])

        add("act", lambda e: e.activation(out=spt[:, 3, :], in_=cp(CP_LAM, 8), func=AF.Exp, scale=-1.0), r=["colp"], w=["spt_e"])
        softplus_from_e(spt[:, 3, :], "spt_e", spt[:, 2, :], "sptout", spt[:, 0, :], "sptz", spt[:, 1, :], "sptz2", spt[:, 4, :], "sptln")
        add("dve", lambda e: e.tensor_scalar(out=lrp[:, 0, :], in0=spt[:, 2, :], scalar1=-8.0, scalar2=None, op0=ALU.mult), r=["sptout"], w=["lrp0"])
        add("dve", lambda e: e.tensor_scalar(out=lrp[:, 1, :], in0=spt[:, 2, :], scalar1=-4.0, scalar2=None, op0=ALU.mult), r=["sptout"], w=["lrp1"])
        add("dve", lambda e: e.tensor_scalar(out=lrp[:, 2, :], in0=cp(CP_BA, 8), scalar1=0.5, scalar2=None, op0=ALU.mult), r=["colp"], w=["lrp2"])
        add("dve", lambda e: e.tensor_scalar(out=lrp[:, 3, :], in0=cp(CP_BX, 8), scalar1=0.5, scalar2=None, op0=ALU.mult), r=["colp"], w=["lrp3"])
        add("act", lambda e: e.activation(out=lrp[:, 4, :], in_=cp(CP_SINK, 8), func=AF.Exp), r=["colp"], w=["lrp4"])
        LRK = ["lrp0", "lrp1", "lrp2", "lrp3"]
        add("dve", lambda e: e.tensor_scalar(out=fcar[:, 1:2], in0=v8[:, 383:384], scalar1=-1.0, scalar2=None, op0=ALU.mult), r=V8_K, w=["nbf"])
        nbf = fcar[:, 1:2]

        add("pe", lambda e: e.matmul(ps[3][0:8, 0:383], lhsT=rbs[:, 0:8], rhs=rbs[:, 8:391], start=True, stop=True), r=RBS_K, w=[PSK[3]])
        add("act", lambda e: e.activation(out=gtab[:, 0:383], in_=ps[3][0:8, 0:383], func=AF.Exp), r=[PSK[3]], w=GT_K)
        add("dve", lambda e: e.memset(gtab[:, 383:384], 0.0), w=GT_K)
        add("dve", lambda e: e.tensor_tensor(out=gtab[:, 0:383], in0=gtab[:, 0:383], in1=v8[:, 0:383], op=ALU.mult), r=GT_K + V8_K, w=GT_K)
        SC.new_sem("eb_w")
        SC.new_sem("eb_r")
        add("sp", lambda e: e.dma_start(out=ebscr.ap(), in_=gtab), r=GT_K, w=["ebscr"], dsem="eb_w")
        hbuf = th[:].rearrange("p (h t) -> p h t", h=8)
        for kb in range(2):
            src = bass.AP(tensor=ebscr, offset=(1 - kb) * P, ap=[[1, P], [384, 8], [1, P]])
            add("sp", lambda e, src=src: e.dma_start(out=hbuf, in_=src), r=["ebscr"], w=[("th", 0), ("th", 1)], dsem="eb_r")
            for q in range(2):
                add("pe", lambda e, q=q: e.matmul(ps[4 + q][:, :], lhsT=jrev, rhs=th[:, q * 512:(q + 1) * 512], start=True, stop=True),
                    r=[("th", 0), ("th", 1), "cmat_j"], w=[PSK[4 + q]])
                add("dve", lambda e, q=q, kb=kb: e.tensor_copy(out=EB[:, q * 4:(q + 1) * 4, kb, :], in_=ps[4 + q][:, :].rearrange("p (h t) -> p h t", h=4)),
                    r=[PSK[4 + q]], w=["EB"])

        for pq in ("00", "01", "10", "11"):
            SC.new_sem("xl" + pq)
        SC.new_sem("xst")
        SC.new_sem("xst0")
        SC.new_sem("xst1")
        SC.new_sem("bcs")
        if dbg:
            SC.new_sem("dbgs0")
            SC.new_sem("dbgs1")
        evq = [0]

        def xk(ti):
            return ("x", xs.cur, ti)

        def rmsnorm_rstd(ti, rcol):
            add("act", lambda e: e.activation(out=ysb[:], in_=xs[:, ti, :], func=AF.Square, accum_out=small[:, rcol:rcol + 1]),
                r=[xk(ti)], w=[("ysb", 0), ("ysb", 1), ("sm", rcol)])
            if RSTD_POW:
                add("dve", lambda e: e.tensor_scalar(out=small[:, rcol:rcol + 1], in0=small[:, rcol:rcol + 1], scalar1=1.0 / D, scalar2=EPS, op0=ALU.mult, op1=ALU.add),
                    r=[("sm", rcol)], w=[("sm", rcol)])
                add("pool", lambda e: e.tensor_tensor(out=small[:, rcol:rcol + 1], in0=small[:, rcol:rcol + 1], in1=negh, op=ALU.pow),
                    r=[("sm", rcol), "negh"], w=[("sm", rcol)])
            else:
                add("act", lambda e: e.activation(out=small[:, rcol:rcol + 1], in_=small[:, rcol:rcol + 1], func=AF.Ln, scale=1.0 / D, bias=EPS),
                    r=[("sm", rcol)], w=[("sm", rcol)])
                add("act", lambda e: e.activation(out=small[:, rcol:rcol + 1], in_=small[:, rcol:rcol + 1], func=AF.Exp, scale=-0.5),
                    r=[("sm", rcol)], w=[("sm", rcol)])

        def norm_to_hT(l, b, skip_stats=False):
            for ti in range(GT):
                if not skip_stats:
                    rmsnorm_rstd(ti, 2 + ti)
                if stop == "na":
                    continue
                scr, scrK = (xn, [("xn", 0), ("xn", 1)]) if ti == 0 else (bc, ["bc"])
                add("dve", lambda e, ti=ti, scr=scr: e.tensor_scalar(out=scr[:], in0=xs[:, ti, :], scalar1=small[:, 2 + ti:3 + ti], scalar2=None, op0=ALU.mult),
                    r=[xk(ti), ("sm", 2 + ti)], w=scrK)
                if stop == "nb":
                    continue
                for c in range(KC):
                    add("pe", lambda e, ti=ti, c=c, scr=scr: e.transpose(ps[c // 2][:, (c % 2) * G + ti * P:(c % 2) * G + (ti + 1) * P], scr[:, c * P:(c + 1) * P], ident),
                        r=scrK + ["cmat"], w=[PSK[c // 2]])
            if stop in ("na", "nb", "nc"):
                return
            for c in range(KC):
                src = ps[c // 2][:, (c % 2) * G:(c % 2 + 1) * G]
                if (c // 2) % 2 == 0:
                    add("act", lambda e, c=c, src=src: e.activation(out=hT[:, c, :], in_=src, func=AF.Identity, scale=gsc(l, c, b), bias=shiftc(l, c, b)),
                        r=[PSK[c // 2], "gsT", "modT"], w=[("hT", c)])
                else:
                    add("dve", lambda e, c=c, src=src: e.tensor_scalar(out=hT[:, c, :], in0=src, scalar1=gsc(l, c, b), scalar2=shiftc(l, c, b), op0=ALU.mult, op1=ALU.add),
                        r=[PSK[c // 2], "gsT", "modT"], w=[("hT", c)])

        def norm_tile_to_hT(l, b, ti, bank):
            rmsnorm_rstd(ti, 2 + ti)
            add("dve", lambda e: e.tensor_scalar(out=xn[:], in0=xs[:, ti, :], scalar1=small[:, 2 + ti:3 + ti], scalar2=None, op0=ALU.mult),
                r=[xk(ti), ("sm", 2 + ti)], w=[("xn", 0), ("xn", 1)])
            for half in range(2):
                for cc in range(4):
                    c = half * 4 + cc
                    add("pe", lambda e, c=c, cc=cc: e.transpose(ps[bank][:, cc * P:(cc + 1) * P], xn[:, c * P:(c + 1) * P], ident),
                        r=[("xn", 0), ("xn", 1), "cmat"], w=[PSK[bank]])
                for cc in range(4):
                    c = half * 4 + cc
                    src = ps[bank][:, cc * P:(cc + 1) * P]
                    if half == 0:
                        add("act", lambda e, c=c, src=src: e.activation(out=hT[:, c, ti * P:(ti + 1) * P], in_=src, func=AF.Identity, scale=gsc(l, c, b), bias=shiftc(l, c, b)),
                            r=[PSK[bank], "gsT", "modT"], w=[("hT", c)])
                    else:
                        add("dve", lambda e, c=c, src=src: e.tensor_scalar(out=hT[:, c, ti * P:(ti + 1) * P], in0=src, scalar1=gsc(l, c, b), scalar2=shiftc(l, c, b), op0=ALU.mult, op1=ALU.add),
                            r=[PSK[bank], "gsT", "modT"], w=[("hT", c)])

        HTK = [("hT", c) for c in range(KC)]

        def proj_fm(Wt, wkey, col0, M, evac):
            bi = 2 + (evq[0] % 2)
            evq[0] += 1
            for kc in range(KC):
                add("pe", lambda e, kc=kc, bi=bi: e.matmul(ps[bi][0:M, 0:G], lhsT=Wt[:, kc, col0:col0 + M], rhs=hT[:, kc, :], start=(kc == 0), stop=(kc == KC - 1)),
                    r=WK[wkey] + HTK, w=[PSK[bi]])
            evac(ps[bi][0:M, 0:G], PSK[bi])

        def load_bc(src_ap, rkeys):
            add("sp", lambda e: e.dma_start(out=bc[:], in_=src_ap), r=rkeys, w=["bc"], dsem="bcs")

        def residual_from(ti, hf, acc_bank):
            add("dve", lambda e: e.tensor_tensor(out=xn[:, hf * 512:(hf + 1) * 512], in0=ps[acc_bank][:, :], in1=bc[:, hf * 512:(hf + 1) * 512], op=ALU.mult),
                r=[PSK[acc_bank], "bc"], w=[("xn", hf)])
            add("dve", lambda e: e.tensor_tensor(out=xs[:, ti, hf * 512:(hf + 1) * 512], in0=xs[:, ti, hf * 512:(hf + 1) * 512], in1=xn[:, hf * 512:(hf + 1) * 512], op=ALU.add),
                r=[("xn", hf), xk(ti)], w=[xk(ti)])

        def out_proj_residual(Wt, wkey, srcT, skey_fn, ti):
            for hf in range(2):
                for kc in range(KC):
                    add("pe", lambda e, kc=kc, hf=hf: e.matmul(ps[hf][:, :], lhsT=srcT[:, kc, ti * P:(ti + 1) * P], rhs=Wt[:, kc, hf * 512:(hf + 1) * 512],
                                                               start=(kc == 0), stop=(kc == KC - 1)),
                        r=WK[wkey] + skey_fn(kc), w=[PSK[hf]])
                residual_from(ti, hf, hf)

        sbank = [0]
        pbq = [0]
        swq = [0]

        last_store = []
        pre_stats = [False]
        for b in range(nseq):
            add("pool", lambda e: e.memset(tails[:], 0.0), w=["tails"])
            add("pool", lambda e: e.memset(state[:], 0.0), w=["state"])
            add("pool", lambda e: e.memset(fcar[:, 0:1], 0.0), w=["fcar"])
            for g in range(ngrp):
                t0 = g * G
                gi = b * ngrp + g
                xs.cur = gi % 2

                def load_x(gj):
                    bj, gg = divmod(gj, ngrp)
                    keep = xs.cur
                    xs.cur = gj % 2
                    for ti in range(GT):
                        add("sp", lambda e, ti=ti: e.dma_start(out=xs[:, ti, :], in_=x_d[bj, gg * G + ti * P:gg * G + (ti + 1) * P, :]), w=[xk(ti)], dsem="xl%d%d" % (gj % 2, ti))
                    xs.cur = keep
                if gi == 0:
                    load_x(0)
                if gi + 1 < nseq * ngrp:
                    load_x(gi + 1)

                def store_x():
                    for ti in range(GT):
                        d = add("sp", lambda e, ti=ti: e.dma_start(out=out_d[b, t0 + ti * P:t0 + (ti + 1) * P, :], in_=xs[:, ti, :]), r=[xk(ti)], dsem="xst")
                    return [d]
                if stop == "pro":
                    last_store = store_x()
                    continue
                norm_to_hT(0, b, skip_stats=pre_stats[0])
                pre_stats[0] = False
                if stop in ("n0", "na", "nb", "nc"):
                    last_store = store_x()
                    continue
                load_bc(gscr.ap()[b, 0, :].partition_broadcast(P), ["gscr"])
                if stop == "n1":
                    last_store = store_x()
                    continue
                if g > 0:
                    add("pool", lambda e: e.tensor_copy(out=aKT[:, 0:P], in_=aKT[:, G:G + P]), r=["aK"], w=["aK"])
                    add("pool", lambda e: e.tensor_copy(out=aV[:, 0, :, 0:64], in_=aV[:, GT, :, 0:64]), r=[("aV", GT)], w=[("aV", 0)])
                proj_fm(W0, "W0", FL, 8, lambda src, k: add("act", lambda e: e.activation(out=frow[3], in_=src, func=AF.Exp, scale=-1.0, bias=nbf), r=[k, "nbf"], w=[FRK[3]]))
                softplus_from_e(frow[3], FRK[3], frow[2], FRK[2], frow[0], FRK[0], frow[1], FRK[1], frow[4], FRK[4])
                add("dve", lambda e: e.tensor_tensor_scan(out=frow[4], data0=ones8[:], data1=frow[2], initial=fcar[:, 0:1], op0=ALU.mult, op1=ALU.add),
                    r=[FRK[2], "ones8", "fcar"], w=[FRK[4]])
                add("dve", lambda e: e.tensor_copy(out=fcar[:, 0:1], in_=frow[4][:, G - 1:G]), r=[FRK[4]], w=["fcar"])
                for c in range(4):
                    proj_fm(W0, "W0", AQ + c * P, P, lambda src, k, c=c: add("act", lambda e: e.copy(out=aQT[:, c, :], in_=src), r=[k], w=[("aQ", c)]))
                proj_fm(W0, "W0", AK, P, lambda src, k: add("act", lambda e: e.copy(out=aKT[:, P:P + G], in_=src), r=[k], w=["aK"]))
                for c in range(4):
                    proj_fm(W0, "W0", BQ + c * P, P, lambda src, k, c=c: add("act", lambda e: e.copy(out=bQT[:, c, :], in_=src), r=[k], w=[("bQ", c)]))
                for c in range(4):
                    proj_fm(W0, "W0", BK + c * P, P, lambda src, k, c=c: add("act", lambda e: e.copy(out=bKT[:, c, t0:t0 + G], in_=src), r=[k], w=[("bK", c, g)]))
                for nl in range(GT):
                    add("pe", lambda e, nl=nl: e.transpose(ps[7][:, 300 + nl * 8:300 + nl * 8 + 8], frow[4][:, nl * P:(nl + 1) * P], ident[0:8, 0:8]),
                        r=[FRK[4], "cmat"], w=[PSK[7]])
                    add("dve", lambda e, nl=nl: e.tensor_scalar(out=fdiag[:, nl, :], in0=ident[0:8, 0:8], scalar1=frow[4][:, (nl + 1) * P - 1:(nl + 1) * P], scalar2=None, op0=ALU.mult),
                        r=[FRK[4], "cmat"], w=["fdiag"])
                add("dve", lambda e: e.tensor_copy(out=FnegT[:, g * GT:(g + 1) * GT, :], in_=ps[7][:, 300:300 + GT * 8].rearrange("p (n h) -> p n h", h=8)),
                    r=[PSK[7]], w=["FnegT"])
                add("pe", lambda e: e.matmul(ps[7][:, 400:400 + GT * 8], lhsT=ones8[:, 0:P], rhs=fdiag[:].rearrange("k n h -> k (n h)"), start=True, stop=True),
                    r=["ones8", "fdiag"], w=[PSK[7]])
                add("dve", lambda e: e.tensor_copy(out=FendBC[:], in_=ps[7][:, 400:400 + GT * 8].rearrange("p (n h) -> p n h", h=8)), r=[PSK[7]], w=["FendBC"])
                nj = (g + 1) * GT
                add("dve", lambda e, nj=nj: e.tensor_tensor(out=BT[:, 0:nj, :, :], in0=FnegT[:, 0:nj, :].unsqueeze(2).to_broadcast([P, nj, GT, 8]),
                                                            in1=FendBC[:].unsqueeze(1).to_broadcast([P, nj, GT, 8]), op=ALU.subtract),
                    r=["FnegT", "FendBC"], w=["BT", "cmat_j"])
                for ti in range(GT):
                    blk = g * GT + ti
                    bi = 2 + (evq[0] % 2)
                    evq[0] += 1
                    for kc in range(KC):
                        add("pe", lambda e, kc=kc, bi=bi, ti=ti: e.matmul(ps[bi][:, 0:P], lhsT=hT[:, kc, ti * P:(ti + 1) * P], rhs=W0[:, kc, AV:AV + P], start=(kc == 0), stop=(kc == KC - 1)),
                            r=WK["W0"] + HTK, w=[PSK[bi]])
                    add("act", lambda e, bi=bi, ti=ti: e.copy(out=aV[:, 1 + ti, :, 0:64], in_=ps[bi][:, 0:P].rearrange("p (g d) -> p g d", g=2)), r=[PSK[bi]], w=[("aV", 1 + ti)])
                    bi = 2 + (evq[0] % 2)
                    evq[0] += 1
                    for kc in range(KC):
                        add("pe", lambda e, kc=kc, bi=bi, ti=ti: e.matmul(ps[bi][:, :], lhsT=hT[:, kc, ti * P:(ti + 1) * P], rhs=W0[:, kc, BV:BV + 512], start=(kc == 0), stop=(kc == KC - 1)),
                            r=WK["W0"] + HTK, w=[PSK[bi]])
                    add("act", lambda e, bi=bi, blk=blk: e.copy(out=bV[:, blk, :, 0:64], in_=ps[bi][:, :].rearrange("p (h d) -> p h d", h=8)), r=[PSK[bi]], w=[("bV", blk)])

                def load_bc_l1():
                    load_bc(gscr.ap()[b, 1, :].partition_broadcast(P), ["gscr"])

                def make_block(nl):
                    n = g * GT + nl
                    q0 = nl * P
                    kbs = [1] if n == 0 else [0, 1]

                    def gate_mm():
                        for hf in range(2):
                            for kc in range(KC):
                                add("pe", lambda e, kc=kc, hf=hf: e.matmul(ps[hf][:, :], lhsT=hT[:, kc, q0:q0 + P], rhs=W0[:, kc, GA + hf * 512:GA + (hf + 1) * 512], start=(kc == 0), stop=(kc == KC - 1)),
                                    r=WK["W0"] + HTK, w=[PSK[hf]])

                    def gate_act():
                        for hf in range(2):
                            add("act", lambda e, hf=hf: e.activation(out=th[:, hf * 512:(hf + 1) * 512], in_=ps[hf][:, :], func=AF.Tanh, scale=0.5), r=[PSK[hf]], w=[("th", hf)])
                            add("dve", lambda e, hf=hf: e.scalar_tensor_tensor(out=th[:, hf * 512:(hf + 1) * 512], in0=th[:, hf * 512:(hf + 1) * 512], scalar=1.0, in1=ps[hf][:, :], op0=ALU.add, op1=ALU.mult),
                                r=[PSK[hf], ("th", hf)], w=[("th", hf)])

                    def swa_S(h):
                        bi = (3, 4, 5, 2)[sbank[0] % 4]
                        sbank[0] += 1
                        c, base = h % 4, (h // 4) * 64
                        for kb in kbs:
                            add("pe", lambda e, kb=kb, bi=bi: e.matmul(ps[bi][:, kb * P:(kb + 1) * P], lhsT=aKT[base:base + 64, (nl + kb) * P:(nl + kb + 1) * P],
                                                                      rhs=aQT[base:base + 64, c, q0:q0 + P], start=True, stop=True),
                                r=["aK", ("aQ", c)], w=[PSK[bi]])
                        return bi

                    def swa_rest(h, bi):
                        kvh = h // 4
                        si = swq[0] % 2
                        swq[0] += 1
                        k0, k1 = kbs[0], kbs[-1] + 1
                        add("act", lambda e: e.activation(out=swt[:, si, k0:k1, :], in_=ps[bi][:, k0 * P:k1 * P].rearrange("p (k t) -> p k t", t=P), func=AF.Exp, scale=0.125),
                            r=[PSK[bi]], w=[("swt", si)])
                        add("pool" if h % 2 == 0 else "dve", lambda e: e.tensor_tensor(out=pa[:, si, k0:k1, :], in0=swt[:, si, k0:k1, :], in1=EB[:, h, k0:k1, :], op=ALU.mult),
                            r=[("swt", si), "EB"], w=[("pa", si)])
                        pvb = 6 + h // 4
                        for kb in kbs:
                            add("pe", lambda e, kb=kb: e.matmul(ps[pvb][:, (h % 4) * 65:(h % 4) * 65 + 65], lhsT=pa[:, si, kb, :], rhs=aV[:, nl + kb, kvh, 0:65],
                                                                start=(kb == kbs[0]), stop=(kb == kbs[-1])),
                                r=[("pa", si), ("aV", nl + kb), "aVones"], w=[PSK[pvb]])

                    def normalise(dcol, sink):
                        for hb in range(2):
                            pv = ps[6 + hb][:, 0:260].rearrange("p (h d) -> p h d", d=65)
                            dn = den[:, dcol + hb * 4:dcol + hb * 4 + 4]
                            dk = ("den", dcol // 4 + hb)
                            if sink:
                                add("dve", lambda e: e.tensor_tensor(out=dn, in0=pv[:, :, 64], in1=lrp[:, 4, hb * 4:hb * 4 + 4], op=ALU.add), r=[PSK[6 + hb], "lrp4"], w=[dk])
                                add("dve", lambda e: e.reciprocal(out=dn, in_=dn), r=[dk], w=[dk])
                            else:
                                add("dve", lambda e: e.reciprocal(out=dn, in_=pv[:, :, 64]), r=[PSK[6 + hb]], w=[dk])
                            add("dve", lambda e: e.tensor_scalar(out=dn, in0=dn, scalar1=0.5, scalar2=None, op0=ALU.mult), r=[dk], w=[dk])
                            add("dve", lambda e: e.tensor_tensor(out=osb[:, hb * 256:(hb + 1) * 256].rearrange("p (h d) -> p h d", d=64), in0=pv[:, :, 0:64],
                                                                in1=dn.unsqueeze(2).to_broadcast([P, 4, 64]), op=ALU.mult),
                                r=[PSK[6 + hb], dk], w=[("osb", hb)])

                    def swa():
                        sq = [swa_S(0), swa_S(1), swa_S(2)]
                        for h in range(8):
                            if h + 3 < 8:
                                sq.append(swa_S(h + 3))
                            swa_rest(h, sq[h])
                        normalise(0, True)
                        add("pool", lambda e: e.tensor_tensor(out=ysb[:, 0:512], in0=th[:, 0:512], in1=osb[:], op=ALU.mult), r=[("th", 0), ("osb", 0), ("osb", 1)], w=[("ysb", 0)])

                    def fox_S(ch):
                        h, j0, j1 = ch
                        bi = (3, 4, 5, 2)[sbank[0] % 4]
                        sbank[0] += 1
                        c, base = h // 2, (h % 2) * 64
                        for j in range(j0, j1):
                            add("pe", lambda e, j=j, bi=bi: e.matmul(ps[bi][:, (j - j0) * P:(j - j0 + 1) * P], lhsT=bKT[base:base + 64, c, j * P:(j + 1) * P],
                                                                    rhs=bQT[base:base + 64, c, q0:q0 + P], start=True, stop=True),
                                r=[("bK", c, j // GT), ("bQ", c)], w=[PSK[bi]])
                        return bi

                    def fox_rest(ch, bi):
                        h, j0, j1 = ch
                        pvb = 6 + h // 4
                        for j in range(j0, j1):
                            si = pbq[0] % NPB
                            pbq[0] += 1
                            add("act", lambda e, j=j, si=si: e.activation(out=pb[:, si, :], in_=ps[bi][:, (j - j0) * P:(j - j0 + 1) * P], func=AF.Exp, scale=0.125, bias=BT[:, j, nl, h:h + 1]),
                                r=[PSK[bi], "BT"], w=[("pb", si)])
                            if j == n:
                                add("pool", lambda e, si=si: e.tensor_tensor(out=pb[:, si, :], in0=pb[:, si, :], in1=cmaskb[:], op=ALU.mult), r=[("pb", si), "cmaskb"], w=[("pb", si)])
                            add("pe", lambda e, j=j, si=si: e.matmul(ps[pvb][:, (h % 4) * 65:(h % 4) * 65 + 65], lhsT=pb[:, si, :], rhs=bV[:, j, h, 0:65], start=(j == 0), stop=(j == n)),
                                r=[("pb", si), ("bV", j), "bVones"], w=[PSK[pvb]])

                    def fox():
                        chunks = []
                        for h in range(8):
                            for j0 in range(0, n + 1, 4):
                                chunks.append((h, j0, min(n + 1, j0 + 4)))
                        fq = [fox_S(ch) for ch in chunks[0:3]]
                        for i, ch in enumerate(chunks):
                            if i + 3 < len(chunks):
                                fq.append(fox_S(chunks[i + 3]))
                            fox_rest(ch, fq[i])
                        normalise(8, False)
                        add("pool", lambda e: e.tensor_tensor(out=ysb[:, 512:1024], in0=th[:, 512:1024], in1=osb[:], op=ALU.mult), r=[("th", 1), ("osb", 0), ("osb", 1)], w=[("ysb", 1)])

                    def tail_a():
                        ytp = ps[2][:, :].bitcast(BF16)
                        for kc in range(KC):
                            add("pe", lambda e, kc=kc: e.transpose(ytp[:, kc * P:(kc + 1) * P], ysb[:, kc * P:(kc + 1) * P], identb[:]),
                                r=[("ysb", kc // 4), "identb"], w=[PSK[2]])
                        add("dve", lambda e: e.tensor_copy(out=yT[:, :, q0:q0 + P], in_=ytp.rearrange("p (k t) -> p k t", t=P)), r=[PSK[2]], w=[("yT", nl)] + Y1K)

                    def tail_b():
                        out_proj_residual(W1, "W1", yT, lambda kc: [("yT", nl)], nl)

                    return gate_mm, gate_act, swa, fox, tail_a, tail_b

                Y1K = [("y1", c) for c in range(KC)]
                blocks = [make_block(nl) for nl in range(GT)]
                for nl in range(GT):
                    gm, ga, sw, fx, ta, tb = blocks[nl]
                    if nl == 0:
                        gm()
                        ga()
                    sw()
                    fx()
                    if nl + 1 < GT:
                        blocks[nl + 1][0]()
                    ta()
                    if nl + 1 < GT:
                        blocks[nl + 1][1]()
                    tb()
                    if EARLY_L1_NORM and stop is None:
                        norm_tile_to_hT(1, b, nl, 3)
                        if nl == GT - 1:
                            load_bc_l1()

                if dbg:
                    for ti in range(GT):
                        add("sp", lambda e, ti=ti: e.dma_start(out=dbg_d[b, t0 + ti * P:t0 + (ti + 1) * P, :], in_=xs[:, ti, :]), r=[xk(ti)], dsem="dbgs%d" % ti)

                if stop == "l0":
                    last_store = store_x()
                    continue
                if not (EARLY_L1_NORM and stop is None):
                    norm_to_hT(1, b)
                    load_bc_l1()
                YTK = [("yT", nl) for nl in range(GT)]

                def l1_A(c, T):
                    bA = T["banks"][0]
                    for kc in range(KC):
                        add("pe", lambda e, kc=kc: e.matmul(ps[bA][:, 0:G], lhsT=W2[:, kc, c * P:(c + 1) * P], rhs=hT[:, kc, :], start=(kc == 0), stop=(kc == KC - 1)),
                            r=WK["W2"] + HTK, w=[PSK[bA]])
                    for kc in range(KC):
                        add("pe", lambda e, kc=kc: e.matmul(ps[bA][:, G:2 * G], lhsT=W2[:, kc, D + c * P:D + (c + 1) * P], rhs=hT[:, kc, :], start=(kc == 0), stop=(kc == KC - 1)),
                            r=WK["W2"] + HTK, w=[PSK[bA]])

                def l1_B1(c, T):
                    bA = T["banks"][0]
                    add("pool", lambda e: e.tensor_copy(out=T["xrb"][:, 0:3], in_=tails[:, c, 0:3]), r=["tails"], w=[T["xrtK"]])
                    add("act", lambda e: e.copy(out=T["xrb"][:, 3:3 + G], in_=ps[bA][:, 0:G]), r=[PSK[bA]], w=[T["xrbK"]])
                    add("pool", lambda e: e.tensor_copy(out=tails[:, c, 0:3], in_=T["xrb"][:, G:G + 3]), r=[T["xrbK"]], w=["tails"])
                    add("act", lambda e: e.activation(out=T["sg"], in_=ps[bA][:, G:2 * G], func=AF.Tanh, scale=0.5), r=[PSK[bA]], w=[T["sgK"]])

                def l1_B2(c, T):
                    bA = T["banks"][0]
                    add("dve", lambda e: e.scalar_tensor_tensor(out=T["sg"], in0=T["sg"], scalar=1.0, in1=ps[bA][:, G:2 * G], op0=ALU.add, op1=ALU.mult), r=[PSK[bA], T["sgK"]], w=[T["sgK"]])
                    cw = lambda j: colp[:, CP_CW + c * 4 + j:CP_CW + c * 4 + j + 1]
                    add("dve", lambda e: e.tensor_scalar(out=T["xc"], in0=T["xrb"][:, 0:G], scalar1=cw(0), scalar2=colp[:, CP_CB + c:CP_CB + c + 1], op0=ALU.mult, op1=ALU.add),
                        r=[T["xrbK"], T["xrtK"], "colp"], w=[T["xcK"]])
                    for j in range(1, 4):
                        add("dve", lambda e, j=j: e.scalar_tensor_tensor(out=T["xc"], in0=T["xrb"][:, j:j + G], scalar=cw(j), in1=T["xc"], op0=ALU.mult, op1=ALU.add),
                            r=[T["xrbK"], T["xrtK"], "colp", T["xcK"]], w=[T["xcK"]])
                    add("dve", lambda e: e.tensor_copy(out=T["xcb"], in_=T["xc"]), r=[T["xcK"]], w=[T["xcbK"]])

                def l1_C(c, T):
                    bB = T["banks"][1]
                    add("pe", lambda e: e.matmul(ps[bB][:, 0:G], lhsT=W4[:, 0, c, :], rhs=T["xcb"], start=True, stop=True), r=WK["W4"] + [T["xcbK"]], w=[PSK[bB]])
                    add("pe", lambda e: e.matmul(ps[bB][:, G:2 * G], lhsT=W4[:, 1, c, :], rhs=T["xcb"], start=True, stop=True), r=WK["W4"] + [T["xcbK"]], w=[PSK[bB]])

                def l1_D1(c, T):
                    bB = T["banks"][1]
                    add("act", lambda e: e.activation(out=T["tha"], in_=ps[bB][:, 0:G], func=AF.Tanh, scale=0.5, bias=lrp[:, 2, c:c + 1]), r=[PSK[bB]] + LRK, w=[T["thaK"]])
                    add("act", lambda e: e.activation(out=T["thx"], in_=ps[bB][:, G:2 * G], func=AF.Tanh, scale=0.5, bias=lrp[:, 3, c:c + 1]), r=[PSK[bB]] + LRK, w=[T["thxK"]])

                def l1_D2(c, T):
                    add("act", lambda e: e.activation(out=T["ab"], in_=T["tha"], func=AF.Exp, scale=lrp[:, 1, c:c + 1], bias=lrp[:, 1, c:c + 1]), r=[T["thaK"]] + LRK, w=[T["abK"]])
                    add("act", lambda e: e.activation(out=T["tha"], in_=T["tha"], func=AF.Exp, scale=lrp[:, 0, c:c + 1], bias=lrp[:, 0, c:c + 1]), r=[T["thaK"]] + LRK, w=[T["thaK"]])
                    add("dve", lambda e: e.scalar_tensor_tensor(out=T["thx"], in0=T["thx"], scalar=1.0, in1=T["xc"], op0=ALU.add, op1=ALU.mult), r=[T["thxK"], T["xcK"]], w=[T["thxK"]])

                def l1_D3(c, T):
                    add("act", lambda e: e.activation(out=T["tha"], in_=T["tha"], func=AF.Sqrt, scale=-1.0, bias=1.0), r=[T["thaK"]], w=[T["thaK"]])

                def l1_D4(c, T):
                    add("dve", lambda e: e.scalar_tensor_tensor(out=T["tha"], in0=T["tha"], scalar=0.5, in1=T["thx"], op0=ALU.mult, op1=ALU.mult), r=[T["thaK"], T["thxK"]], w=[T["thaK"]])
                    add("dve", lambda e: e.tensor_tensor_scan(out=T["xc"], data0=T["ab"], data1=T["tha"], initial=state[:, c:c + 1], op0=ALU.mult, op1=ALU.add),
                        r=[T["abK"], T["thaK"], "state", T["thxK"]], w=[T["xcK"]])
                    add("pool", lambda e: e.tensor_copy(out=state[:, c:c + 1], in_=T["xc"][:, G - 1:G]), r=[T["xcK"]], w=["state"])
                    add("dve", lambda e: e.scalar_tensor_tensor(out=yT[:, c, :], in0=T["xc"], scalar=0.5, in1=T["sg"], op0=ALU.mult, op1=ALU.mult),
                        r=[T["xcK"], T["sgK"]], w=YTK + [("y1", c)])

                ACCB = (0, 1, 4, 5)

                def l1_O(c, T):
                    for ti in range(GT):
                        for hf in range(2):
                            ab_ = ACCB[ti * 2 + hf]
                            add("pe", lambda e, ti=ti, hf=hf, ab_=ab_: e.matmul(ps[ab_][:, :], lhsT=yT[:, c, ti * P:(ti + 1) * P], rhs=W3[:, c, hf * 512:(hf + 1) * 512],
                                                                           start=(c == 0), stop=(c == KC - 1)),
                                r=WK["W3"] + [("y1", c)], w=[PSK[ab_]])

                pairs = [(2 * k, 2 * k + 1) for k in range(KC // 2)]

                def stage(fn, pr):
                    for i, c in enumerate(pr):
                        fn(c, L1S[i])

                stage(l1_A, pairs[0])
                if stop is None and gi + 1 < nseq * ngrp:
                    xs.cur = (gi + 1) % 2
                    for ti in range(GT):
                        rmsnorm_rstd(ti, 2 + ti)
                    xs.cur = gi % 2
                    pre_stats[0] = True
                for k, pr in enumerate(pairs):
                    stage(l1_B1, pr)
                    stage(l1_B2, pr)
                    stage(l1_C, pr)
                    stage(l1_D1, pr)
                    if k + 1 < len(pairs):
                        stage(l1_A, pairs[k + 1])
                    stage(l1_D2, pr)
                    stage(l1_D3, pr)
                    stage(l1_D4, pr)
                    stage(l1_O, pr)
                if stop == "l1":
                    for ti in range(GT):
                        for hf in range(2):
                            residual_from(ti, hf, ACCB[ti * 2 + hf])
                    last_store = store_x()
                    continue
                for ti in range(GT):
                    for hf in range(2):
                        residual_from(ti, hf, ACCB[ti * 2 + hf])
                load_bc(fg_d.ap()[0, :].partition_broadcast(P), [])
                fin = [(th[:], [("th", 0), ("th", 1)]), (l1b[:, 0:D], ["l1xc", "l1tha", "l1thx", "l1ab"])]
                for ti in range(GT):
                    rmsnorm_rstd(ti, 4 + ti)
                    fo, fk = fin[ti]
                    add("dve", lambda e, ti=ti, fo=fo: e.scalar_tensor_tensor(out=fo, in0=xs[:, ti, :], scalar=small[:, 4 + ti:5 + ti], in1=bc[:], op0=ALU.mult, op1=ALU.mult),
                        r=[xk(ti), ("sm", 4 + ti), "bc"], w=fk)
                    add("sp", lambda e, ti=ti, fo=fo: e.dma_start(out=out_d[b, t0 + ti * P:t0 + (ti + 1) * P, :], in_=fo), r=fk, dsem="xst%d" % ti)
                    last_store = [("xst0", SC.cnt["xst0"]), ("xst1", SC.cnt["xst1"])]
        finals = list(last_store)
        if dbg:
            finals.append(("dbgs0", SC.cnt["dbgs0"]))
            finals.append(("dbgs1", SC.cnt["dbgs1"]))
        SC.run(final_conds=finals)
    return nc


def _t5_bucket(rel):
    n = np.maximum(rel, 0)
    nf = np.maximum(n, 1).astype(np.float32)
    large = 16 + (np.log(nf / np.float32(16)) / np.float32(math.log(128 / 16)) * np.float32(32 - 16)).astype(np.int32)
    large = np.minimum(large, 31)
    return np.where(n < 16, n, large)


def _host_layout(inp, nseq_core=SEQ_PER_CORE, ncores=NCORES):
    f = lambda a: np.ascontiguousarray(np.asarray(a, dtype=np.float32))
    w_in = f(inp["attn_w_in"])[0]
    cols = np.concatenate([
        np.concatenate([np.concatenate([np.arange(c * 64, (c + 1) * 64), np.arange((4 + c) * 64, (5 + c) * 64)]) for c in range(4)]),
        np.arange(512, 640),
        np.arange(768, 1280),
        np.arange(1280, 1792),
        np.arange(2304, 2312),
        np.arange(640, 768),
        np.arange(1792, 2304),
        np.arange(2312, 3336),
    ])
    assert cols.size == C0
    kcl = lambda w: np.ascontiguousarray(w.reshape(KC, P, w.shape[1]).transpose(1, 0, 2))
    shared = {
        "w0": kcl(w_in[:, cols]),
        "w1": kcl(f(inp["attn_w_out"])[0]),
        "w2": kcl(f(inp["lru_w_in"])[0]),
        "w3": kcl(f(inp["lru_w_out"])[0]),
        "w4": np.ascontiguousarray(np.stack([f(inp["lru_w_a"])[0], f(inp["lru_w_x"])[0]], 0).transpose(2, 0, 1, 3)),
        "ada_w": f(inp["ada_w"]),
        "final_g": f(inp["final_g"]).reshape(1, D),
    }
    colv = lambda v: np.ascontiguousarray(v.reshape(KC, P).T)
    colp = np.zeros((P, CP_N), np.float32)
    ng = f(inp["norm_g"])
    ab = f(inp["ada_b"])
    for l in range(2):
        colp[:, CP_NG + l * 8:CP_NG + (l + 1) * 8] = colv(ng[l])
        for part in range(3):
            colp[:, CP_AB + (l * 3 + part) * 8:CP_AB + (l * 3 + part + 1) * 8] = colv(ab[l, part * D:(part + 1) * D])
    cw = f(inp["lru_conv_w"])[0]
    for j in range(4):
        colp[:, CP_CW + j:CP_CW + 32:4] = colv(cw[j])
    colp[:, CP_CB:CP_CB + 8] = colv(f(inp["lru_conv_b"])[0])
    colp[:, CP_BA:CP_BA + 8] = colv(f(inp["lru_b_a"])[0])
    colp[:, CP_BX:CP_BX + 8] = colv(f(inp["lru_b_x"])[0])
    colp[:, CP_LAM:CP_LAM + 8] = colv(f(inp["lru_lambda"])[0])
    colp[:, CP_SINK:CP_SINK + 8] = np.broadcast_to(f(inp["attn_sinks"])[0][None, :], (P, 8))
    eye = np.eye(P, dtype=np.float32)
    sidx = np.arange(P)
    cmask = (sidx[:, None] <= sidx[None, :]).astype(np.float32)
    shared["cmat"] = np.ascontiguousarray(np.concatenate([eye, eye[::-1], cmask], axis=1))
    rel = np.arange(383) - 127
    onehot = np.zeros((32, 383), np.float32)
    onehot[_t5_bucket(rel), np.arange(383)] = 1.0
    valid = ((rel >= 0) & (rel < 128)).astype(np.float32)
    shared["rb"] = np.ascontiguousarray(np.concatenate([f(inp["rel_bias"]), onehot], axis=1))
    v8 = np.zeros((8, 384), np.float32)
    v8[:, 0:383] = valid[None, :]
    v8[:, 383] = f(inp["attn_b_f"])[0]
    shared["v8"] = v8
    x = f(inp["x"])
    c = f(inp["c"])
    maps = []
    for core in range(ncores):
        b0 = core * nseq_core
        m = dict(shared)
        m["x"] = x[b0:b0 + nseq_core]
        cp_ = colp.copy()
        ct = c[b0:b0 + nseq_core].reshape(nseq_core, KC, P).transpose(2, 1, 0)
        ctf = np.zeros((P, KC, 4), np.float32)
        ctf[:, :, 0:nseq_core] = ct
        cp_[:, CP_CT:CP_CT + KC * 4] = ctf.reshape(P, KC * 4)
        m["colp"] = cp_
        maps.append(m)
    return maps


def kernel(**inputs):
    maps = _host_layout(inputs)
    nc = build()
    res = run_bass_kernel_spmd(nc, maps, core_ids=list(range(NCORES)))
    out = np.concatenate([np.asarray(r["out"], dtype=np.float32) for r in res.results], axis=0)
    return out
```
